# Optimizing a Trainium2 kernel written in Bass

```python
import jax, jax.numpy as jnp
from jax import lax
import numpy as np

D_MODEL = 1024
BATCH = 2
SEQ = 16384
DEPTH = 2

N_EVEN = (DEPTH + 1) // 2
N_ODD = DEPTH // 2

POOL_DIM = D_MODEL // 2
POOL_WINDOWS = (2, 4, 8, 16)
N_POOL_GROUPS = len(POOL_WINDOWS)
POOL_GROUP_DIM = POOL_DIM // N_POOL_GROUPS

MLA_HEADS = 8
QK_NOPE_DIM = 64
QK_ROPE_DIM = 32
QK_DIM = QK_NOPE_DIM + QK_ROPE_DIM
V_HEAD_DIM = 64
Q_LORA_RANK = 256
KV_LORA_RANK = 128
ROPE_BASE = 10000.0
Q_BLOCK = 128

EVEN_IN_DIM = POOL_DIM + Q_LORA_RANK + KV_LORA_RANK + QK_ROPE_DIM
EVEN_MIX_DIM = POOL_DIM + MLA_HEADS * V_HEAD_DIM

LRU_WIDTH = D_MODEL
LRU_HEADS = 4
LRU_HEAD_DIM = LRU_WIDTH // LRU_HEADS
CONV_WIDTH = 4
LRU_C = 8.0

MEM_TOKENS = 256
MEM_HEADS = 4
MEM_HEAD_DIM = D_MODEL // MEM_HEADS

D_FF = -(-8 * D_MODEL // (3 * 256)) * 256

RMS_EPS = 1e-6
NEG_INF = -1e30

kernel_name = "hybrid_pool_mla_rglru_memxattn"


def rms_norm(x, g):
    xf = x.astype(jnp.float32)
    y = xf * lax.rsqrt(jnp.mean(xf * xf, axis=-1, keepdims=True) + RMS_EPS)
    return (y * g).astype(x.dtype)


def rope_tables(positions):
    inv_freq = ROPE_BASE ** (-jnp.arange(0, QK_ROPE_DIM, 2, dtype=jnp.float32) / QK_ROPE_DIM)
    ang = positions.astype(jnp.float32)[..., None] * inv_freq
    return jnp.cos(ang), jnp.sin(ang)


def apply_rope(t, cos, sin):
    t1, t2 = jnp.split(t.astype(jnp.float32), 2, axis=-1)
    out = jnp.concatenate([t1 * cos - t2 * sin, t2 * cos + t1 * sin], axis=-1)
    return out.astype(t.dtype)


def pool_mixer(u, pool_w, pool_scale):
    B, S, _ = u.shape
    ug = u.reshape(B, S, N_POOL_GROUPS, POOL_GROUP_DIM)
    uf = ug.astype(jnp.float32)
    csum = jnp.concatenate([jnp.zeros((B, 1, N_POOL_GROUPS, POOL_GROUP_DIM), jnp.float32),
                            jnp.cumsum(uf, axis=1)], axis=1)
    t = jnp.arange(S)
    means = []
    for g, w in enumerate(POOL_WINDOWS):
        lo = jnp.maximum(t + 1 - w, 0)
        win_sum = csum[:, 1:, g] - csum[:, lo, g]
        cnt = jnp.minimum(t + 1, w).astype(jnp.float32)
        means.append(win_sum / cnt[None, :, None])
    pooled = (jnp.stack(means, axis=2) - uf).astype(u.dtype)
    y = jnp.einsum('bsgc,gcd->bsgd', pooled, pool_w).reshape(B, S, POOL_DIM)
    return y * pool_scale.astype(y.dtype)


def mla_causal_attention(q_nope, q_rope, k_nope, k_rope, v):
    B, S, H, _ = q_nope.shape
    nb = S // Q_BLOCK
    qn = q_nope.reshape(B, nb, Q_BLOCK, H, QK_NOPE_DIM).transpose(1, 0, 2, 3, 4)
    qr = q_rope.reshape(B, nb, Q_BLOCK, H, QK_ROPE_DIM).transpose(1, 0, 2, 3, 4)
    starts = jnp.arange(nb, dtype=jnp.int32) * Q_BLOCK
    kpos = jnp.arange(S, dtype=jnp.int32)
    scale = QK_DIM ** -0.5

    def one_block(args):
        qn_b, qr_b, start = args
        s = (jnp.einsum('bqhd,bkhd->bhqk', qn_b, k_nope).astype(jnp.float32)
             + jnp.einsum('bqhr,bkr->bhqk', qr_b, k_rope).astype(jnp.float32)) * scale
        qpos = start + jnp.arange(Q_BLOCK, dtype=jnp.int32)
        mask = kpos[None, :] <= qpos[:, None]
        s = jnp.where(mask[None, None], s, NEG_INF)
        p = jax.nn.softmax(s, axis=-1).astype(v.dtype)
        return jnp.einsum('bhqk,bkhd->bqhd', p, v)

    out = lax.map(one_block, (qn, qr, starts))
    return out.transpose(1, 0, 2, 3, 4).reshape(B, S, H * V_HEAD_DIM)


def even_mixer(h, cos, sin, w_in, pool_w, pool_scale, q_norm, w_q_up, kv_norm, w_kv_up, w_out):
    B, S, _ = h.shape
    z = h @ w_in
    u, cq, ckv, kr = jnp.split(z, [POOL_DIM, POOL_DIM + Q_LORA_RANK,
                                   POOL_DIM + Q_LORA_RANK + KV_LORA_RANK], axis=-1)
    y_pool = pool_mixer(u, pool_w, pool_scale)
    q = (rms_norm(cq, q_norm) @ w_q_up).reshape(B, S, MLA_HEADS, QK_DIM)
    q_nope, q_rope = jnp.split(q, [QK_NOPE_DIM], axis=-1)
    kv = (rms_norm(ckv, kv_norm) @ w_kv_up).reshape(B, S, MLA_HEADS, QK_NOPE_DIM + V_HEAD_DIM)
    k_nope, v = jnp.split(kv, [QK_NOPE_DIM], axis=-1)
    q_rope = apply_rope(q_rope, cos[:, :, None, :], sin[:, :, None, :])
    k_rope = apply_rope(kr, cos, sin)
    y_att = mla_causal_attention(q_nope, q_rope, k_nope, k_rope, v)
    return jnp.concatenate([y_pool, y_att], axis=-1) @ w_out


def causal_depthwise_conv(xb, conv_w, conv_b):
    y = lax.conv_general_dilated(xb, conv_w[:, None, :].astype(xb.dtype), window_strides=(1,),
                                 padding=((CONV_WIDTH - 1, 0),),
                                 dimension_numbers=('NWC', 'WIO', 'NWC'),
                                 feature_group_count=xb.shape[-1])
    return y + conv_b.astype(y.dtype)


def linear_scan_combine(c1, c2):
    a1, b1 = c1
    a2, b2 = c2
    return a1 * a2, a2 * b1 + b2


def odd_mixer(h, reset, w_in, conv_w, conv_b, w_rgate, b_rgate, w_igate, b_igate, lam, w_out):
    B, S, _ = h.shape
    z = h @ w_in
    gate_branch, xb = jnp.split(z, [LRU_WIDTH], axis=-1)
    xb = causal_depthwise_conv(xb, conv_w, conv_b)
    xg = xb.reshape(B, S, LRU_HEADS, LRU_HEAD_DIM)
    r = jax.nn.sigmoid(jnp.einsum('bshc,hcd->bshd', xg, w_rgate).reshape(B, S, LRU_WIDTH) + b_rgate)
    i = jax.nn.sigmoid(jnp.einsum('bshc,hcd->bshd', xg, w_igate).reshape(B, S, LRU_WIDTH) + b_igate)
    log_a = -LRU_C * r.astype(jnp.float32) * jax.nn.softplus(-lam.astype(jnp.float32))
    a = jnp.exp(log_a)
    mult = jnp.sqrt(jnp.maximum(-jnp.expm1(2.0 * log_a), 0.0))
    a = jnp.where(reset, 0.0, a)
    mult = jnp.where(reset, 1.0, mult)
    b = mult * (i * xb).astype(jnp.float32)
    _, hseq = lax.associative_scan(linear_scan_combine, (a, b), axis=1)
    y = jax.nn.gelu(gate_branch) * hseq.astype(h.dtype)
    return y @ w_out


def mem_cross_attention(h, mem, norm_mem, w_q, w_kv, w_o):
    B, S, _ = h.shape
    m = rms_norm(mem, norm_mem)
    q = (h @ w_q).reshape(B, S, MEM_HEADS, MEM_HEAD_DIM)
    k, v = jnp.split(m @ w_kv, 2, axis=-1)
    k = k.reshape(B, -1, MEM_HEADS, MEM_HEAD_DIM)
    v = v.reshape(B, -1, MEM_HEADS, MEM_HEAD_DIM)
    s = jnp.einsum('bqhd,bkhd->bhqk', q, k).astype(jnp.float32) * (MEM_HEAD_DIM ** -0.5)
    p = jax.nn.softmax(s, axis=-1).astype(v.dtype)
    o = jnp.einsum('bhqk,bkhd->bqhd', p, v).reshape(B, S, D_MODEL)
    return o @ w_o


def swiglu(h, w_gate_up, w_down):
    g, u = jnp.split(h @ w_gate_up, 2, axis=-1)
    return (jax.nn.silu(g) * u) @ w_down


def setup_inputs(seed: int = 0) -> dict:
    key = jax.random.key(seed)
    ks = iter(jax.random.split(key, 48))
    f32 = jnp.float32

    def w(shape, fan_in):
        return jax.random.normal(next(ks), shape, f32) * fan_in ** -0.5

    def gain(shape):
        return 1.0 + 0.02 * jax.random.normal(next(ks), shape, f32)

    def bias(shape):
        return 0.02 * jax.random.normal(next(ks), shape, f32)

    E, O, L = N_EVEN, N_ODD, DEPTH
    x = jax.random.normal(next(ks), (BATCH, SEQ, D_MODEL), f32)
    mem = jax.random.normal(next(ks), (BATCH, MEM_TOKENS, D_MODEL), f32)
    positions = jnp.broadcast_to(jnp.arange(SEQ, dtype=jnp.int32), (BATCH, SEQ))
    a_c = jax.random.uniform(next(ks), (O, LRU_WIDTH), f32, 0.9, 0.999)
    s_l = a_c ** (1.0 / LRU_C)
    lam = jnp.log(s_l) - jnp.log1p(-s_l)
    return {
        "x": x,
        "mem": mem,
        "positions": positions,
        "ev_norm": gain((E, D_MODEL)),
        "ev_w_in": w((E, D_MODEL, EVEN_IN_DIM), D_MODEL),
        "ev_pool_w": w((E, N_POOL_GROUPS, POOL_GROUP_DIM, POOL_GROUP_DIM), POOL_GROUP_DIM),
        "ev_pool_scale": gain((E, POOL_DIM)),
        "ev_q_norm": gain((E, Q_LORA_RANK)),
        "ev_w_q_up": w((E, Q_LORA_RANK, MLA_HEADS * QK_DIM), Q_LORA_RANK),
        "ev_kv_norm": gain((E, KV_LORA_RANK)),
        "ev_w_kv_up": w((E, KV_LORA_RANK, MLA_HEADS * (QK_NOPE_DIM + V_HEAD_DIM)), KV_LORA_RANK),
        "ev_w_out": w((E, EVEN_MIX_DIM, D_MODEL), EVEN_MIX_DIM),
        "od_norm": gain((O, D_MODEL)),
        "od_w_in": w((O, D_MODEL, 2 * LRU_WIDTH), D_MODEL),
        "od_conv_w": w((O, CONV_WIDTH, LRU_WIDTH), CONV_WIDTH),
        "od_conv_b": bias((O, LRU_WIDTH)),
        "od_w_rgate": w((O, LRU_HEADS, LRU_HEAD_DIM, LRU_HEAD_DIM), LRU_HEAD_DIM),
        "od_b_rgate": bias((O, LRU_WIDTH)),
        "od_w_igate": w((O, LRU_HEADS, LRU_HEAD_DIM, LRU_HEAD_DIM), LRU_HEAD_DIM),
        "od_b_igate": bias((O, LRU_WIDTH)),
        "od_lambda": lam,
        "od_w_out": w((O, LRU_WIDTH, D_MODEL), LRU_WIDTH),
        "xa_norm_x": gain((L, D_MODEL)),
        "xa_norm_mem": gain((L, D_MODEL)),
        "xa_w_q": w((L, D_MODEL, D_MODEL), D_MODEL),
        "xa_w_kv": w((L, D_MODEL, 2 * D_MODEL), D_MODEL),
        "xa_w_o": w((L, D_MODEL, D_MODEL), D_MODEL),
        "ffn_norm": gain((L, D_MODEL)),
        "ffn_w_gate_up": w((L, D_MODEL, 2 * D_FF), D_MODEL),
        "ffn_w_down": w((L, D_FF, D_MODEL), D_FF),
        "final_norm": gain((D_MODEL,)),
    }


def reference(x, mem, positions,
              ev_norm, ev_w_in, ev_pool_w, ev_pool_scale, ev_q_norm, ev_w_q_up,
              ev_kv_norm, ev_w_kv_up, ev_w_out,
              od_norm, od_w_in, od_conv_w, od_conv_b, od_w_rgate, od_b_rgate,
              od_w_igate, od_b_igate, od_lambda, od_w_out,
              xa_norm_x, xa_norm_mem, xa_w_q, xa_w_kv, xa_w_o,
              ffn_norm, ffn_w_gate_up, ffn_w_down, final_norm):
    cos, sin = rope_tables(positions)
    reset = (positions == 0)[..., None]
    for layer in range(DEPTH):
        j = layer // 2
        if layer % 2 == 0:
            h = rms_norm(x, ev_norm[j])
            x = x + even_mixer(h, cos, sin, ev_w_in[j], ev_pool_w[j], ev_pool_scale[j],
                               ev_q_norm[j], ev_w_q_up[j], ev_kv_norm[j], ev_w_kv_up[j],
                               ev_w_out[j])
        else:
            h = rms_norm(x, od_norm[j])
            x = x + odd_mixer(h, reset, od_w_in[j], od_conv_w[j], od_conv_b[j],
                              od_w_rgate[j], od_b_rgate[j], od_w_igate[j], od_b_igate[j],
                              od_lambda[j], od_w_out[j])
        x = x + mem_cross_attention(rms_norm(x, xa_norm_x[layer]), mem, xa_norm_mem[layer],
                                    xa_w_q[layer], xa_w_kv[layer], xa_w_o[layer])
        x = x + swiglu(rms_norm(x, ffn_norm[layer]), ffn_w_gate_up[layer], ffn_w_down[layer])
    return rms_norm(x, final_norm)
```

```python
import os
import numpy as np
from contextlib import ExitStack
DBG = int(os.environ.get('KDBG', '9'))
STRICT = os.environ.get('KSTRICT', '0') == '1'
import ml_dtypes
import concourse.bass as bass
import concourse.mybir as mybir
from concourse.bass_utils import run_bass_kernel_spmd

F32 = mybir.dt.float32
BF16 = mybir.dt.bfloat16
I32 = mybir.dt.int32
AF = mybir.ActivationFunctionType
ALU = mybir.AluOpType
NPBF = ml_dtypes.bfloat16

D = 1024
S = 16384
B = 2
NCORE = 8
TOK = 4096
T = 512
EPS = 1e-6
MAGIC = 12582912.0
TWO_PI = 6.283185307179586
C1 = 6.28125
C2 = TWO_PI - C1


class Prog:
    COMPUTE = ("pe", "act", "dve", "pool")
    ENGS = ("pe", "act", "dve", "pool", "sp")
    NDMA = 24

    def __init__(self, nc, es):
        self.nc = nc
        self.es = es
        self.ops = {e: [] for e in self.ENGS}
        self.sems = {e: es.enter_context(nc.semaphore("s_" + e)) for e in self.COMPUTE}
        self.cnt = {e: 0 for e in self.COMPUTE}
        self.seen = {e: {} for e in self.ENGS}
        self.res = {}
        self.dsem = {q: [es.enter_context(nc.semaphore("d%s%d" % (q, i))) for i in range(self.NDMA)] for q in ("sp", "pool")}
        self.duse = {q: [0] * self.NDMA for q in ("sp", "pool")}
        self.dnext = {"sp": 0, "pool": 0}
        self.extra_sems = {}

    def _deps(self, eng, reads, writes, strict=False):
        deps = []
        for r in reads:
            st = self.res.get(r)
            if st is not None and st[0] is not None:
                tk = st[0]
                if tk[3] == eng and eng in ("pe", "sp"):
                    continue
                deps.append(tk)
        for w in writes:
            st = self.res.get(w)
            if st is not None:
                for tk in ([st[0]] if st[0] is not None else []) + st[1]:
                    if tk[3] == eng and not ((STRICT or strict) and eng != "pe"):
                        continue
                    deps.append(tk)
        return deps

    def _filter(self, eng, deps):
        waits = []
        best = {}
        for (key, sem, val, _e) in deps:
            if key not in best or best[key][1] < val:
                best[key] = (sem, val)
        for key, (sem, val) in best.items():
            if self.waited[eng].get(key, 0) < val:
                self.waited[eng][key] = val
                waits.append((sem, val))
        return waits

    waited = None

    def _update(self, tok, reads, writes):
        for r in reads:
            st = self.res.get(r)
            if st is None:
                self.res[r] = (None, [tok])
            else:
                st[1].append(tok)
        for w in writes:
            self.res[w] = (tok, [])

    def op(self, eng, fn, reads=(), writes=(), strict=False):
        if self.waited is None:
            self.waited = {e: {} for e in self.ENGS}
        deps = self._deps(eng, reads, writes, strict)
        waits = self._filter(eng, deps)
        self.cnt[eng] += 1
        tok = ("c_" + eng, self.sems[eng], self.cnt[eng], eng)
        self.ops[eng].append((fn, waits, (self.sems[eng], 1)))
        self._update(tok, reads, writes)
        return tok

    def dma(self, q, out, in_, reads=(), writes=(), **kw):
        if self.waited is None:
            self.waited = {e: {} for e in self.ENGS}
        k = self.dnext[q]
        self.dnext[q] = (k + 1) % self.NDMA
        prev = self.duse[q][k]
        self.duse[q][k] += 1
        dsem = self.dsem[q][k]
        dkey = "d%s%d" % (q, k)
        deps = self._deps(None, reads, writes)
        if prev > 0:
            deps.append((dkey, dsem, 16 * prev, "dma"))
        waits = self._filter(q, deps)
        tok = (dkey, dsem, 16 * self.duse[q][k], "dma")

        def fn(e, out=out, in_=in_, kw=kw):
            return e.dma_start(out=out, in_=in_, **kw)
        self.ops[q].append((fn, waits, (dsem, 16)))
        self._update(tok, reads, writes)
        return tok

    def custom(self, q, fn, sem_name, reads=(), writes=()):
        if sem_name not in self.extra_sems:
            self.extra_sems[sem_name] = [self.es.enter_context(self.nc.semaphore(sem_name)), 0]
        ent = self.extra_sems[sem_name]
        deps = self._deps(None, reads, writes)
        waits = self._filter(q, deps)
        ent[1] += 1
        tok = (sem_name, ent[0], ent[1], "x")
        self.ops[q].append((fn, waits, (ent[0], 1)))
        self._update(tok, reads, writes)
        return tok

    def fence(self):
        if self.waited is None:
            self.waited = {e: {} for e in self.ENGS}
        deps = [("c_" + e, self.sems[e], self.cnt[e], e) for e in self.COMPUTE if self.cnt[e] > 0]
        for q in ("sp", "pool"):
            for k in range(self.NDMA):
                if self.duse[q][k] > 0:
                    deps.append(("d%s%d" % (q, k), self.dsem[q][k], 16 * self.duse[q][k], "dma"))
        for name, ent in self.extra_sems.items():
            if ent[1] > 0:
                deps.append((name, ent[0], ent[1], "x"))
        for q in self.ENGS:
            own = [d for d in deps if d[3] != q]
            self.ops[q].append((None, self._filter(q, own), None))

    def wait_all(self, q, toks):
        waits = self._filter(q, list(toks))
        self.ops[q].append((None, waits, None))

    def emit(self):
        nc = self.nc
        with nc.Block() as block:
            def run(name, e):
                for (fn, waits, inc) in self.ops[name]:
                    for (sem, val) in waits:
                        e.wait_ge(sem, val)
                    if fn is None:
                        continue
                    ins = fn(e)
                    if inc is not None:
                        ins.then_inc(inc[0], inc[1])

            @block.tensor
            def _(e):
                run("pe", e)

            @block.scalar
            def _(e):
                run("act", e)

            @block.vector
            def _(e):
                run("dve", e)

            @block.gpsimd
            def _(e):
                run("pool", e)

            @block.sync
            def _(e):
                run("sp", e)
        self.ops = {e: [] for e in self.ENGS}


class Rot:
    def __init__(self, items):
        self.items = list(items)
        self.i = 0

    def next(self):
        it = self.items[self.i % len(self.items)]
        self.i += 1
        return it


def build_p1(nc, P, es, io, NT, tick=None, after=None, tick1=None):
    NTICK = 6
    sb = lambda name, shape, dt: es.enter_context(nc.sbuf_tensor("sb_" + name, shape, dt))
    ps = lambda name, shape, dt: es.enter_context(nc.psum_tensor(name, shape, dt))
    SCALE = 96.0 ** -0.5
    NKB = 4 * NT

    ident = sb("ident", [128, 128], BF16)
    masks = sb("masks", [128, 4, 512], BF16)
    onesb = sb("onesb", [128, 128], BF16)
    ones32 = sb("ones32", [128, 64], F32)
    W1 = sb("W1", [128, 8, 448], BF16)
    Wq = sb("Wq", [128, 2, 2, 128], BF16)
    Wkv = sb("Wkv", [128, 256], BF16)
    gT = sb("gT", [128, 8], F32)
    gq = sb("gq", [128, 2], F32)
    gkv = sb("gkv", [128, 1], F32)
    invf = sb("invf", [128, 1], F32)
    xtm = [sb("xtm%d" % i, [128, 1024], F32) for i in range(8)]
    hn = [sb("hn%d" % i, [128, 1024], BF16) for i in range(2)]
    junk = sb("junk", [128, 1024], BF16)
    ssq = [sb("ssq%d" % i, [128, 1], F32) for i in range(4)]
    rstd = [sb("rstd%d" % i, [128, 1], F32) for i in range(4)]
    hT = [sb("hT%d" % i, [128, 8, 512], BF16) for i in range(1)]
    cq_sb = sb("cq_sb", [128, 2, 512], F32)
    cq_sq = sb("cq_sq", [128, 2, 512], BF16)
    ckv_sb = sb("ckv_sb", [128, 512], F32)
    ckv_sq = sb("ckv_sq", [128, 512], BF16)
    rstd_q = sb("rstd_q", [128, 512], F32)
    rstd_kv = sb("rstd_kv", [128, 512], F32)
    cqn = sb("cqn", [128, 2, 512], BF16)
    ckvn = sb("ckvn", [128, 512], BF16)
    cs = sb("cs", [128, 512], F32)
    sn = sb("sn", [128, 512], F32)
    pos_i = [sb("pos_i%d" % i, [128, 512], I32) for i in range(2)]
    ang = sb("ang", [128, 512], F32)
    ang2 = sb("ang2", [128, 512], F32)
    rr_k = sb("rr_k", [128, 512], F32)
    rr_r = sb("rr_r", [128, 512], F32)
    rt1 = sb("rt1", [128, 512], F32)
    rt2 = sb("rt2", [128, 512], F32)
    QT = [sb("QT%d" % i, [128, 2, 512], BF16) for i in range(2)]
    KT = sb("KT", [128, 2, NKB * 128], BF16)
    Vaug = sb("Vaug", [128, NKB, 2, 65], BF16)
    pT = [sb("pT%d" % i, [128, 512], BF16) for i in range(4)]
    o_sb = [sb("o_sb%d" % i, [128, 512], F32) for i in range(2)]
    yT = [sb("yT%d" % i, [128, 512], BF16) for i in range(2)]

    ps_m = [ps("ps_m%d" % i, [128, 512], F32) for i in range(3)]
    ps_s = [ps("ps_s%d" % i, [128, 512], F32) for i in range(3)]
    ps_acc = [ps("ps_acc%d" % i, [128, 512], F32) for i in range(2)]
    pm = Rot([("ps_m", i) for i in range(3)])
    sr = Rot([0, 1, 2])
    pr = Rot([0, 1, 2, 3])

    P.dma("sp", ident[:], io["ident"], writes=["ident"])
    P.dma("sp", masks[:], io["masks"], writes=["masks"])
    for j in range(8):
        P.dma("sp", xtm[j][:, 0:416], io["w_in"][j * 128:(j + 1) * 128, 512:928], writes=[("xtm", j)])
    for c in range(2):
        P.dma("sp", xtm[c][:, 512:704], io["wq"][c * 128:(c + 1) * 128, :], writes=[("xtm", c)])
    P.dma("sp", xtm[2][:, 512:768], io["wkv"], writes=[("xtm", 2)])
    P.dma("sp", gT[:], io["ev_normT"], writes=["gT"])
    P.dma("sp", gq[:], io["q_normT"], writes=["gq"])
    P.dma("sp", gkv[:], io["kv_normT"], writes=["gkv"])
    P.dma("sp", invf[:], io["invf"], writes=["invf"])
    P.op("pool", lambda e: e.memset(onesb[:], 1.0), writes=["onesb"])
    P.op("pool", lambda e: e.memset(ones32[:], 1.0), writes=["ones32"])
    P.op("pool", lambda e: e.memset(Vaug[:, :, :, 64:65], 1.0), writes=["vones"])
    for j in range(8):
        P.op("dve", lambda e, j=j: e.tensor_scalar(out=W1[:, j, 0:416], in0=xtm[j][:, 0:416], scalar1=gT[:, j:j + 1],
                                                    scalar2=None, op0=ALU.mult),
             reads=[("xtm", j), "gT"], writes=[("W1", j)])
        P.op("dve", lambda e, j=j: e.tensor_scalar(out=W1[:, j, 416:432], in0=xtm[j][:, 400:416], scalar1=gT[:, j:j + 1],
                                                    scalar2=-1.0, op0=ALU.mult, op1=ALU.mult),
             reads=[("xtm", j), "gT"], writes=[("W1", j)])
        P.op("dve", lambda e, j=j: e.tensor_scalar(out=W1[:, j, 432:448], in0=xtm[j][:, 384:400], scalar1=gT[:, j:j + 1],
                                                    scalar2=None, op0=ALU.mult),
             reads=[("xtm", j), "gT"], writes=[("W1", j)])
    for c in range(2):
        for h in range(2):
            b0 = h * 96
            P.op("dve", lambda e, c=c, h=h, b0=b0: e.tensor_scalar(out=Wq[:, c, h, 0:96], in0=xtm[c][:, 512 + b0:512 + b0 + 96],
                                                                   scalar1=gq[:, c:c + 1], scalar2=None, op0=ALU.mult),
                 reads=[("xtm", c), "gq"], writes=[("Wq", c, h)])
            P.op("dve", lambda e, c=c, h=h, b0=b0: e.tensor_scalar(out=Wq[:, c, h, 96:112], in0=xtm[c][:, 512 + b0 + 80:512 + b0 + 96],
                                                                   scalar1=gq[:, c:c + 1], scalar2=-1.0, op0=ALU.mult, op1=ALU.mult),
                 reads=[("xtm", c), "gq"], writes=[("Wq", c, h)])
            P.op("dve", lambda e, c=c, h=h, b0=b0: e.tensor_scalar(out=Wq[:, c, h, 112:128], in0=xtm[c][:, 512 + b0 + 64:512 + b0 + 80],
                                                                   scalar1=gq[:, c:c + 1], scalar2=None, op0=ALU.mult),
                 reads=[("xtm", c), "gq"], writes=[("Wq", c, h)])
    P.op("dve", lambda e: e.tensor_scalar(out=Wkv[:], in0=xtm[2][:, 512:768], scalar1=gkv[:, 0:1], scalar2=None, op0=ALU.mult),
         reads=[("xtm", 2), "gkv"], writes=["Wkv"])
    W1r = [("W1", j) for j in range(8)]
    Wqr = [("Wq", c, h) for c in range(2) for h in range(2)]

    xs = Rot(range(8))
    x_slots = {}

    def load_x(t):
        for s in range(4):
            sl = xs.next()
            x_slots[(t, s)] = sl
            r0 = t * 512 + s * 128
            P.dma("sp", xtm[sl][:], io["x"][r0:r0 + 128, :], writes=[("xtm", sl)])
        P.dma("sp", pos_i[t % 2][64:96, :], io["pos"][:, t * 512:(t + 1) * 512], writes=[("pos_i", t % 2)])

    def prep_stages(t):
        hb = t % 2
        hTt = hT[0]
        QTt = QT[hb]
        col0 = t * 512
        st = []

        def stage_norm_front(s):
            def f():
                sl = x_slots[(t, s)]
                a = s % 4
                hb2 = s % 2
                P.op("act", lambda e: e.activation(out=junk[:], in_=xtm[sl][:], func=AF.Square, accum_out=ssq[a][:]),
                     reads=[("xtm", sl)], writes=["junk", ("ssq", a)])
                P.op("act", lambda e: e.activation(out=rstd[a][:], in_=ssq[a][:], func=AF.Ln, scale=1.0 / D, bias=EPS),
                     reads=[("ssq", a)], writes=[("rstd", a)])
                P.op("act", lambda e: e.activation(out=rstd[a][:], in_=rstd[a][:], func=AF.Exp, scale=-0.5),
                     reads=[("rstd", a)], writes=[("rstd", a)])
                P.op("dve", lambda e: e.tensor_scalar(out=hn[hb2][:], in0=xtm[sl][:], scalar1=rstd[a][:, 0:1], scalar2=None,
                                                       op0=ALU.mult),
                     reads=[("xtm", sl), ("rstd", a)], writes=[("hn", hb2)])
            return f

        def stage_norm_back(s):
            def f():
                hb2 = s % 2
                btr = pm.next()
                ptr_ = ps_m[btr[1]][:].bitcast(BF16).rearrange("p (j n) -> p j n", j=8)

                def tr(e):
                    ins = None
                    for j in range(8):
                        ins = e.transpose(out=ptr_[:, j, :], in_=hn[hb2][:, j * 128:(j + 1) * 128], identity=ident[:])
                    return ins
                P.op("pe", tr, reads=[("hn", hb2), "ident"], writes=[btr])
                P.op("dve", lambda e: e.tensor_copy(out=hTt[:, :, s * 128:(s + 1) * 128], in_=ptr_),
                     reads=[btr], writes=[("hT", 0, s)])
            return f
        st.append(stage_norm_front(0))
        st.append(stage_norm_front(1))
        st.append(stage_norm_back(0))
        st.append(stage_norm_front(2))
        st.append(stage_norm_back(1))
        st.append(stage_norm_front(3))
        st.append(stage_norm_back(2))
        hTr = [("hT", 0, s) for s in range(4)]

        def stage_rope_tables():
            r = slice(64, 96)
            P.op("dve", lambda e: e.tensor_copy(out=ang[r, :], in_=pos_i[hb][r, :]), reads=[("pos_i", hb)], writes=["ang"])
            P.op("dve", lambda e: e.tensor_scalar(out=ang[r, :], in0=ang[r, :], scalar1=invf[r, 0:1], scalar2=None, op0=ALU.mult),
                 reads=["ang", "invf"], writes=["ang"])
            P.op("dve", lambda e: e.tensor_scalar(out=ang2[r, :], in0=ang[r, :], scalar1=float(np.float32(np.pi / 2)), scalar2=None,
                                                   op0=ALU.add), reads=["ang"], writes=["ang2"])
            for (src, srcn, dst, dstn) in ((ang, "ang", sn, "sn"), (ang2, "ang2", cs, "cs")):
                P.op("dve", lambda e, src=src: e.tensor_scalar(out=rr_k[r, :], in0=src[r, :], scalar1=float(1.0 / TWO_PI), scalar2=MAGIC,
                                                                 op0=ALU.mult, op1=ALU.add), reads=[srcn], writes=["rr_k"])
                P.op("dve", lambda e: e.tensor_scalar(out=rr_k[r, :], in0=rr_k[r, :], scalar1=-MAGIC, scalar2=None, op0=ALU.add),
                     reads=["rr_k"], writes=["rr_k"])
                P.op("dve", lambda e, src=src: e.scalar_tensor_tensor(out=rr_r[r, :], in0=rr_k[r, :], scalar=-C1, in1=src[r, :],
                                                                        op0=ALU.mult, op1=ALU.add), reads=["rr_k", srcn], writes=["rr_r"])
                P.op("dve", lambda e: e.scalar_tensor_tensor(out=rr_r[r, :], in0=rr_k[r, :], scalar=-C2, in1=rr_r[r, :],
                                                              op0=ALU.mult, op1=ALU.add), reads=["rr_k", "rr_r"], writes=["rr_r"])
                P.op("dve", lambda e: e.tensor_scalar(out=rr_r[r, :], in0=rr_r[r, :], scalar1=3.1415925, scalar2=-3.1415925, op0=ALU.min, op1=ALU.max),
                     reads=["rr_r"], writes=["rr_r"])
                P.op("act", lambda e, dst=dst: e.activation(out=dst[r, :], in_=rr_r[r, :], func=AF.Sin),
                     reads=["rr_r"], writes=[dstn])
        st.append(stage_rope_tables)
        st.append(stage_norm_back(3))

        def stage_proj():
            banks = {}
            for name, c0, m, prow in (("cq0", 0, 128, 0), ("cq1", 128, 128, 0), ("ckv", 256, 128, 0),
                                      ("kr", 384, 32, 64), ("krr", 416, 32, 64)):
                bk = pm.next()
                banks[name] = bk
                tile = ps_m[bk[1]]

                def mm(e, c0=c0, m=m, prow=prow, tile=tile):
                    ins = None
                    for j in range(8):
                        ins = e.matmul(out=tile[prow:prow + m, :], lhsT=W1[:, j, c0:c0 + m], rhs=hTt[:, j, :],
                                       start=(j == 0), stop=(j == 7))
                    return ins
                P.op("pe", mm, reads=W1r + hTr, writes=[bk])
                if name in ("cq0", "cq1"):
                    c = 0 if name == "cq0" else 1
                    P.op("act", lambda e, c=c, tile=tile: e.activation(out=cq_sb[:, c, :], in_=tile[:], func=AF.Copy),
                         reads=[bk], writes=[("cq_sb", c)])
                    P.op("pool", lambda e, c=c: e.tensor_tensor(out=cq_sq[:, c, :], in0=cq_sb[:, c, :], in1=cq_sb[:, c, :], op=ALU.mult),
                         reads=[("cq_sb", c)], writes=[("cq_sq", c)])
                elif name == "ckv":
                    P.op("act", lambda e, tile=tile: e.activation(out=ckv_sb[:], in_=tile[:], func=AF.Copy),
                         reads=[bk], writes=["ckv_sb"])
                    P.op("pool", lambda e: e.tensor_tensor(out=ckv_sq[:], in0=ckv_sb[:], in1=ckv_sb[:], op=ALU.mult),
                         reads=["ckv_sb"], writes=["ckv_sq"])
            r = slice(64, 96)
            tk, tkr = ps_m[banks["kr"][1]], ps_m[banks["krr"][1]]
            P.op("dve", lambda e: e.tensor_tensor(out=rt1[r, :], in0=tk[r, :], in1=cs[r, :], op=ALU.mult),
                 reads=[banks["kr"], "cs"], writes=["rt1"])
            P.op("dve", lambda e: e.tensor_tensor(out=rt2[r, :], in0=tkr[r, :], in1=sn[r, :], op=ALU.mult),
                 reads=[banks["krr"], "sn"], writes=["rt2"])
            P.op("dve", lambda e: e.tensor_tensor(out=KT[r, 0, col0:col0 + 512], in0=rt1[r, :], in1=rt2[r, :], op=ALU.add),
                 reads=["rt1", "rt2"], writes=[("KTr", 0, t)])
            P.op("pool", lambda e: e.tensor_copy(out=KT[r, 1, col0:col0 + 512], in_=KT[r, 0, col0:col0 + 512]),
                 reads=[("KTr", 0, t)], writes=[("KTr", 1, t)])
        st.append(stage_proj)

        def stage_norms():
            for (nm, nchunk, sq, dst, dstn, dim) in (("q", 2, cq_sq, rstd_q, "rstd_q", 256), ("kv", 1, ckv_sq, rstd_kv, "rstd_kv", 128)):
                bk = pm.next()
                tile = ps_m[bk[1]]

                def mm(e, nchunk=nchunk, sq=sq, tile=tile):
                    ins = None
                    for c in range(nchunk):
                        rhs = sq[:, c, :] if nchunk == 2 else sq[:]
                        ins = e.matmul(out=tile[:], lhsT=onesb[:], rhs=rhs, start=(c == 0), stop=(c == nchunk - 1))
                    return ins
                rd = [("cq_sq", 0), ("cq_sq", 1)] if nm == "q" else ["ckv_sq"]
                P.op("pe", mm, reads=rd + ["onesb"], writes=[bk])
                P.op("act", lambda e, tile=tile, dst=dst, dim=dim: e.activation(out=dst[:], in_=tile[:], func=AF.Ln, scale=1.0 / dim, bias=EPS),
                     reads=[bk], writes=[dstn])
                P.op("act", lambda e, dst=dst: e.activation(out=dst[:], in_=dst[:], func=AF.Exp, scale=-0.5),
                     reads=[dstn], writes=[dstn])
            for c in range(2):
                P.op("dve", lambda e, c=c: e.tensor_tensor(out=cqn[:, c, :], in0=cq_sb[:, c, :], in1=rstd_q[:], op=ALU.mult),
                     reads=[("cq_sb", c), "rstd_q"], writes=[("cqn", c)])
            P.op("pool", lambda e: e.tensor_tensor(out=ckvn[:], in0=ckv_sb[:], in1=rstd_kv[:], op=ALU.mult),
                 reads=["ckv_sb", "rstd_kv"], writes=["ckvn"])
        st.append(stage_norms)

        def stage_q(h):
            def f():
                r = slice(64, 96)
                bq, bqr = pm.next(), pm.next()
                tq, tqr = ps_m[bq[1]], ps_m[bqr[1]]

                def mmq(e):
                    ins = None
                    for c in range(2):
                        ins = e.matmul(out=tq[0:96, :], lhsT=Wq[:, c, h, 0:96], rhs=cqn[:, c, :], start=(c == 0), stop=(c == 1))
                    return ins

                def mmr(e):
                    ins = None
                    for c in range(2):
                        ins = e.matmul(out=tqr[64:96, :], lhsT=Wq[:, c, h, 96:128], rhs=cqn[:, c, :], start=(c == 0), stop=(c == 1))
                    return ins
                P.op("pe", mmq, reads=Wqr + [("cqn", 0), ("cqn", 1)], writes=[bq])
                P.op("pe", mmr, reads=Wqr + [("cqn", 0), ("cqn", 1)], writes=[bqr])
                P.op("act", lambda e: e.activation(out=QTt[0:64, h, :], in_=tq[0:64, :], func=AF.Copy),
                     reads=[bq], writes=[("QTn", hb, h)])
                P.op("dve", lambda e: e.tensor_tensor(out=rt1[r, :], in0=tq[r, :], in1=cs[r, :], op=ALU.mult),
                     reads=[bq, "cs"], writes=["rt1"])
                P.op("dve", lambda e: e.tensor_tensor(out=rt2[r, :], in0=tqr[r, :], in1=sn[r, :], op=ALU.mult),
                     reads=[bqr, "sn"], writes=["rt2"])
                P.op("dve", lambda e: e.tensor_tensor(out=QTt[r, h, :], in0=rt1[r, :], in1=rt2[r, :], op=ALU.add),
                     reads=["rt1", "rt2"], writes=[("QTr", hb, h)])
            return f
        st.append(stage_q(0))
        st.append(stage_q(1))

        def stage_kv():
            bk = pm.next()
            tk = ps_m[bk[1]]

            def mmk(e):
                return e.matmul(out=tk[:, :], lhsT=Wkv[:, 0:128], rhs=ckvn[:], start=True, stop=True)
            P.op("pe", mmk, reads=["Wkv", "ckvn"], writes=[bk])
            P.op("act", lambda e: e.activation(out=KT[0:64, 0, col0:col0 + 512], in_=tk[0:64, :], func=AF.Copy),
                 reads=[bk], writes=[("KTn", 0, t)])
            bk2 = pm.next()
            tk2 = ps_m[bk2[1]]

            def mmk2(e):
                return e.matmul(out=tk2[0:64, :], lhsT=Wkv[:, 64:128], rhs=ckvn[:], start=True, stop=True)
            P.op("pe", mmk2, reads=["Wkv", "ckvn"], writes=[bk2])
            P.op("act", lambda e: e.activation(out=KT[0:64, 1, col0:col0 + 512], in_=tk2[0:64, :], func=AF.Copy),
                 reads=[bk2], writes=[("KTn", 1, t)])
            bv = pm.next()
            tv = ps_m[bv[1]]

            def mmv(e):
                ins = None
                for s in range(4):
                    ins = e.matmul(out=tv[:, s * 128:(s + 1) * 128], lhsT=ckvn[:, s * 128:(s + 1) * 128], rhs=Wkv[:, 128:256],
                                   start=True, stop=True)
                return ins
            P.op("pe", mmv, reads=["Wkv", "ckvn"], writes=[bv])
            P.op("dve", lambda e: e.tensor_copy(out=Vaug[:, 4 * t:4 * t + 4, :, 0:64],
                                                 in_=tv[:].rearrange("p (s h d) -> p s h d", s=4, h=2)),
                 reads=[bv], writes=[("V", t)])
        st.append(stage_kv)
        return st

    deferred = []

    def attention(t, pending):
        hb = t % 2
        QTt = QT[hb]
        nblk = 4 * t + 4
        blocks = [(h, kb) for h in range(2) for kb in range(nblk)]
        total = len(blocks)
        npend = len(pending)
        sbank = {}

        def emit_S(i):
            h, kb = blocks[i]
            sbk = sr.next()
            sbank[i] = sbk
            tS = ps_s[sbk]
            kt = kb // 4
            P.op("pe", lambda e, kb=kb, tS=tS, h=h: e.matmul(out=tS[:], lhsT=KT[0:96, h, kb * 128:(kb + 1) * 128], rhs=QTt[0:96, h, :],
                                                              start=True, stop=True),
                 reads=[("KTn", h, kt), ("KTr", h, kt), ("QTn", hb, h), ("QTr", hb, h)], writes=[("ps_s", sbk)])

        ticks_left = [NTICK]
        for d_ in deferred:
            d_[0] = 3
        LA = 2
        for i0 in range(min(LA, total)):
            emit_S(i0)
        for i in range(total):
            h, kb = blocks[i]
            ab = h
            acc = ps_acc[ab]
            kt = kb // 4
            if i + LA < total:
                emit_S(i + LA)
            sbk = sbank.pop(i)
            tS = ps_s[sbk]
            pb = pr.next()
            P.op("act", lambda e, tS=tS, pb=pb: e.activation(out=pT[pb][:], in_=tS[:], func=AF.Exp, scale=SCALE),
                 reads=[("ps_s", sbk)], writes=[("pT", pb)])
            if kb >= 4 * t:
                j = kb - 4 * t
                P.op("pool", lambda e, pb=pb, j=j: e.tensor_tensor(out=pT[pb][:], in0=pT[pb][:], in1=masks[:, j, :], op=ALU.mult),
                     reads=[("pT", pb), "masks"], writes=[("pT", pb)])
            P.op("pe", lambda e, kb=kb, pb=pb, h=h, acc=acc: e.matmul(out=acc[0:65, :], lhsT=Vaug[:, kb, h, :], rhs=pT[pb][:],
                                                                       start=(kb == 0), stop=(kb == nblk - 1)),
                 reads=[("pT", pb), ("V", kt), "vones"], writes=[("ps_acc", ab)])
            while pending and (i + 1) * npend >= (npend - len(pending) + 1) * total:
                pending.pop(0)()
            if tick1 is not None and ticks_left[0] > 0 and (i + 1) * NTICK >= (NTICK - ticks_left[0] + 1) * total:
                ticks_left[0] -= 1
                tick1()
            if kb == nblk - 1:
                P.op("dve", lambda e, acc=acc, ab=ab: e.reciprocal(out=o_sb[ab][64:65, :], in_=acc[64:65, :]),
                     reads=[("ps_acc", ab)], writes=[("rrow", ab)])
                P.op("act", lambda e, acc=acc, ab=ab: e.activation(out=o_sb[ab][0:64, :], in_=acc[0:64, :], func=AF.Copy),
                     reads=[("ps_acc", ab)], writes=[("o_sb", ab)])
                def fin_back(ab=ab, h=h, t=t):
                    bb = pm.next()
                    tb = ps_m[bb[1]]
                    P.op("pe", lambda e, ab=ab, tb=tb: e.matmul(out=tb[0:64, :], lhsT=ones32[64:65, 0:64], rhs=o_sb[ab][64:65, :],
                                                                 start=True, stop=True),
                         reads=[("rrow", ab), "ones32"], writes=[bb])
                    P.op("dve", lambda e, ab=ab, tb=tb: e.tensor_tensor(out=yT[ab][0:64, :], in0=o_sb[ab][0:64, :], in1=tb[0:64, :], op=ALU.mult),
                         reads=[("o_sb", ab), bb], writes=[("yT", ab)])
                    if "yatt_pieces" in io:
                        ydst = io["yatt_pieces"][t // 4][h * 64:(h + 1) * 64, (t % 4) * 512:(t % 4 + 1) * 512]
                    else:
                        ydst = io["yatt"][h * 64:(h + 1) * 64, t * 512:(t + 1) * 512]
                    P.dma("sp", ydst, yT[ab][0:64, :], reads=[("yT", ab)], writes=[("yatt", h, t)])
                deferred.append([i + 4, fin_back])
            while deferred and deferred[0][0] <= i:
                deferred.pop(0)[1]()
        while pending:
            pending.pop(0)()
        while tick1 is not None and ticks_left[0] > 0:
            ticks_left[0] -= 1
            tick1()

    load_x(0)
    if NT > 1:
        load_x(1)
    for f in prep_stages(0):
        f()
    for t in range(NT):
        pending = prep_stages(t + 1) if t + 1 < NT else []
        if t + 2 < NT:
            load_x(t + 2)
        if tick is not None:
            tick(t)
        attention(t, pending)
        if t == NT - 1 or after is not None and t % 4 == 3:
            while deferred:
                deferred.pop(0)[1]()
        if after is not None:
            after(t)
    return [("yatt", h, t) for h in range(2) for t in range(NT)]


def p1_host_inputs(core, inp, NT):
    b = core // 4
    hp = core % 4
    heads = (2 * hp, 2 * hp + 1)
    ident = np.eye(128, dtype=np.float32).astype(NPBF)
    k = np.arange(128)[:, None, None]
    j = np.arange(4)[None, :, None]
    q = np.arange(512)[None, None, :]
    masks = (q >= 128 * j + k).astype(np.float32).astype(NPBF)
    wq = np.concatenate([inp["ev_w_q_up"][0][:, h * 96:(h + 1) * 96] for h in heads], axis=1)
    wkvu = inp["ev_w_kv_up"][0]
    wkv = np.concatenate([wkvu[:, h * 128:h * 128 + 64] for h in heads] + [wkvu[:, h * 128 + 64:h * 128 + 128] for h in heads], axis=1)
    inv = (10000.0 ** (-np.arange(0, 32, 2, dtype=np.float32) / 32)).astype(np.float32)
    invf = np.zeros((128, 1), np.float32)
    invf[64:80, 0] = inv
    invf[80:96, 0] = inv
    pos = np.ascontiguousarray(np.broadcast_to(inp["positions"][b][None, :NT * 512], (32, NT * 512))).astype(np.int32)
    return {
        "x": np.ascontiguousarray(inp["x"][b][:NT * 512]),
        "pos": pos,
        "ident": ident, "masks": np.ascontiguousarray(masks),
        "w_in": np.ascontiguousarray(inp["ev_w_in"][0]),
        "wq": np.ascontiguousarray(wq), "wkv": np.ascontiguousarray(wkv),
        "ev_normT": np.ascontiguousarray(inp["ev_norm"][0].reshape(8, 128).T),
        "q_normT": np.ascontiguousarray(inp["ev_q_norm"][0].reshape(2, 128).T),
        "kv_normT": np.ascontiguousarray(inp["ev_kv_norm"][0].reshape(1, 128).T),
        "invf": invf,
    }


def build_p1_program(NT):
    nc = bass.Bass("TRN2", target_bir_lowering=False)
    dt = lambda name, shape, dtype, kind: nc.dram_tensor(name, shape, dtype, kind=kind).ap()
    io = {
        "x": dt("x", [NT * 512, D], F32, "ExternalInput"),
        "pos": dt("pos", [32, NT * 512], I32, "ExternalInput"),
        "ident": dt("ident", [128, 128], BF16, "ExternalInput"),
        "masks": dt("masks", [128, 4, 512], BF16, "ExternalInput"),
        "w_in": dt("w_in", [D, 928], F32, "ExternalInput"),
        "wq": dt("wq", [256, 192], F32, "ExternalInput"),
        "wkv": dt("wkv", [128, 256], F32, "ExternalInput"),
        "ev_normT": dt("ev_normT", [128, 8], F32, "ExternalInput"),
        "q_normT": dt("q_normT", [128, 2], F32, "ExternalInput"),
        "kv_normT": dt("kv_normT", [128, 1], F32, "ExternalInput"),
        "invf": dt("invf", [128, 1], F32, "ExternalInput"),
        "yatt": dt("yatt", [128, NT * 512], BF16, "ExternalOutput"),
    }
    with ExitStack() as es:
        P = Prog(nc, es)
        outs = build_p1(nc, P, es, io, NT)
        toks = [P.res[k][0] for k in outs]
        P.wait_all("sp", toks)
        P.emit()
    return nc


D_FF = 2816
NFC = 22


class Layers:
    def __init__(self, nc, P, es, io, NTL, mode="A1"):
        self.nc, self.P, self.es, self.io, self.NTL = nc, P, es, io, NTL
        self.mode = mode
        sb = lambda name, shape, dt: es.enter_context(nc.sbuf_tensor("sl_" + name, shape, dt))
        psf = lambda name, shape, dt: es.enter_context(nc.psum_tensor("pl_" + name, shape, dt))
        self.sb = sb
        self.pp = [psf("b%d" % i, [128, 512], F32) for i in range(8)]
        self.prot = Rot(range(8))
        self.wslot = [sb("wslot%d" % i, [128, 8192], BF16) for i in range(3)]
        self.wrot = Rot(range(3))
        self.identb = sb("identb", [128, 128], BF16)
        self.onesb = sb("onesb", [128, 128], BF16)
        self.F_X = sb("F_X", [128, 8, 512], F32)
        self.BF_A = sb("BF_A", [128, 8, 512], BF16)
        self.BF_B = sb("BF_B", [128, 8, 512], BF16)
        self.BF_C = sb("BF_C", [128, 8, 512], BF16)
        self.BF_D = sb("BF_D", [128, NFC if mode != "A2" else 2, 512], BF16)
        self.F_G = sb("F_G", [128, 4224], F32)
        self.F_T2 = sb("F_T2", [128, 2112], F32)
        self.rstd = sb("rstd", [128, 512], F32)
        self.rec = sb("rec", [128, 512], F32)
        self.sg = [sb("sg%d" % i, [128, 512], F32) for i in range(2)]
        self.PmT = [sb("PmT%d" % i, [128, 512], BF16) for i in range(4)]
        self.tmp = [sb("tmp%d" % i, [128, 512], F32) for i in range(6)]
        if mode != "A2":
            self.KmT = sb("KmT", [128, 4, 2, 256], BF16)
            self.Vm = sb("Vm", [128, 2, 1024], BF16)
        self.memtm = self.F_G[:, 0:1024]
        self.memn = self.BF_D[:, 0:2, :].rearrange("p a n -> p (a n)")
        self.mT = self.BF_A[:, 0:4, :].rearrange("p a n -> p (a n)").rearrange("p (j m) -> p j m", j=8)
        self.small = sb("small", [128, 64], F32)
        self.UH = sb("UH", [128, 4, 16], F32)
        self.gains = {}
        self.fused = False
        P.dma("pool", self.identb[:], io["ident"], writes=["identb"])
        P.op("pool", lambda e: e.memset(self.onesb[:], 1.0), writes=["onesbL"])
        self.scr = {}

    def param(self, name, n):
        if name in self.gains:
            return self.gains[name]
        t = self.sb("p_" + name, [128, n], F32)
        self.P.dma("pool", t[:], self.io[name], writes=[("param", name)])
        self.gains[name] = t
        return t

    def prep_weight(self, name, src, K, N):
        dst = self.nc.dram_tensor("wbf_" + name, [K, N], BF16).ap()
        rows = 128
        for r0 in range(0, K, rows):
            self.P.dma("pool", dst[r0:r0 + rows, :], src[r0:r0 + rows, :], writes=[("wscr", name, r0 // rows)])
        self.scr[name] = (dst, K, N)
        return dst

    def prep_weight_gu(self, name, src):
        dst = self.nc.dram_tensor("wbf_" + name, [1024, 2 * D_FF], BF16).ap()
        dv = dst.rearrange("k (c two n) -> k c two n", two=2, n=128)
        sv = src.rearrange("k (two c n) -> k two c n", two=2, n=128)
        for r0 in range(0, 1024, 128):
            for two in range(2):
                self.P.dma("pool", dv[r0:r0 + 128, :, two, :], sv[r0:r0 + 128, two, :, :], writes=[("wscr", name, (r0 // 128) * 2 + two)])
        self.scr[name] = (dst, 1024, 2 * D_FF)
        self.scr_keys = getattr(self, "scr_keys", {})
        self.scr_keys[name] = [("wscr", name, i) for i in range(16)]
        return dst

    def wres(self, name):
        if name in getattr(self, "scr_keys", {}):
            return self.scr_keys[name]
        dst, K, N = self.scr[name]
        return [("wscr", name, i) for i in range(K // 128)]

    def load_w(self, parts):
        sl = self.wrot.next()
        dstb, view, rd = parts
        self.P.dma("sp", dstb(self.wslot[sl]), view, reads=rd, writes=[("wslot", sl)])
        return sl

    @staticmethod
    def simple(n, nk, view, rd):
        return (lambda slot: slot[:, 0:nk * n].rearrange("p (k n) -> p k n", k=nk), view, rd)

    def wview(self, sl, c0, n, nk):
        return self.wslot[sl][:, c0:c0 + nk * n].rearrange("p (k n) -> p k n", k=nk)

    def set_prefetch(self, which):
        if which == "F_G":
            self.pf_buf = self.F_G[:, 0:4096].rearrange("p (c n) -> p c n", c=8)
            self.pf_keys = [("U", g) for g in range(4)] + [("T1", g) for g in range(4)] + [("XC", c) for c in range(8)]
        else:
            self.pf_buf = self.BF_D[:, 0:16, :].rearrange("p a n -> p (a n)").bitcast(F32).rearrange("p (c n) -> p c n", c=8)
            self.pf_keys = [("BF_D", i) for i in range(16)]

    def prefetch_x(self, view, t, extra_reads=()):
        cols = slice(t * 512, (t + 1) * 512)
        for q2 in range(2):
            self.P.dma("pool", self.pf_buf[:, 4 * q2:4 * q2 + 4, :], view[:, 4 * q2:4 * q2 + 4, cols], reads=list(extra_reads),
                       writes=self.pf_keys if q2 == 0 else [("pf_hi",)])

    def take_prefetched(self, Xr):
        engs = ["act", "dve", "pool", "dve"]
        for j in range(8):
            eng = engs[j % 4]
            rd = self.pf_keys + [("pf_hi",)]
            if eng == "act":
                self.P.op("act", lambda e, j=j: e.activation(out=self.F_X[:, j, :], in_=self.pf_buf[:, j, :], func=AF.Copy), reads=rd, writes=[Xr[j]], strict=True)
            else:
                self.P.op(eng, lambda e, j=j: e.tensor_copy(out=self.F_X[:, j, :], in_=self.pf_buf[:, j, :]), reads=rd, writes=[Xr[j]], strict=True)

    def bank(self):
        i = self.prot.next()
        return ("pl", i), self.pp[i]

    def proj(self, wv, wreads, rhs, rhs_reads, n_oc, evac, nk=8, N=512, oc_cols=None):
        P = self.P
        for oc in range(n_oc):
            bk, tile = self.bank()
            c0 = oc * 128 if oc_cols is None else oc_cols(oc)

            def mm(e, tile=tile, c0=c0):
                ins = None
                for k in range(nk):
                    ins = e.matmul(out=tile[:, 0:N], lhsT=wv[:, k, c0:c0 + 128], rhs=rhs(k), start=(k == 0), stop=(k == nk - 1))
                return ins
            P.op("pe", mm, reads=list(wreads) + list(rhs_reads), writes=[bk])
            evac(oc, bk, tile)

    def mem_kv(self, layer, wkv_name):
        P, io = self.P, self.io
        g = self.param("xa_norm_mem%d" % layer, 8)
        sm = self.small
        for mc in range(2):
            P.dma("pool", self.memtm, io["mem"][mc * 128:(mc + 1) * 128, :], writes=["memtm"])
            P.op("act", lambda e: e.activation(out=self.memn, in_=self.memtm, func=AF.Square, accum_out=sm[:, 0:1]),
                 reads=["memtm"], writes=["memn", "sm0"])
            P.op("act", lambda e: e.activation(out=sm[:, 1:2], in_=sm[:, 0:1], func=AF.Ln, scale=1.0 / D, bias=EPS),
                 reads=["sm0"], writes=["sm1"])
            P.op("act", lambda e: e.activation(out=sm[:, 1:2], in_=sm[:, 1:2], func=AF.Exp, scale=-0.5),
                 reads=["sm1"], writes=["sm1"])
            P.op("dve", lambda e: e.tensor_scalar(out=self.memn, in0=self.memtm, scalar1=sm[:, 1:2], scalar2=None, op0=ALU.mult),
                 reads=["memtm", "sm1", "memn"], writes=["memn"])
            bk, tile = self.bank()
            ptile = tile[:].bitcast(BF16).rearrange("p (j n) -> p j n", j=8)

            def tr(e, ptile=ptile):
                ins = None
                for j in range(8):
                    ins = e.transpose(out=ptile[:, j, :], in_=self.memn[:, j * 128:(j + 1) * 128], identity=self.identb[:])
                return ins
            P.op("pe", tr, reads=["memn", "identb"], writes=[bk])
            for j in range(8):
                P.op("dve", lambda e, j=j, mc=mc, ptile=ptile: e.tensor_scalar(out=self.mT[:, j, mc * 128:(mc + 1) * 128], in0=ptile[:, j, :],
                                                                              scalar1=g[:, j:j + 1], scalar2=None, op0=ALU.mult),
                     reads=[bk, ("param", "xa_norm_mem%d" % layer)], writes=[("mT", j, mc)])
        mTr = [("mT", j, mc) for j in range(8) for mc in range(2)]
        dst, K, N = self.scr[wkv_name]
        wr = self.wres(wkv_name)
        sl = self.load_w(self.simple(1024, 8, dst.rearrange("(k p) n -> p k n", p=128)[:, :, 0:1024], wr))
        wv = self.wview(sl, 0, 1024, 8)
        for hd in range(4):
            for dc in range(2):
                bk, tile = self.bank()
                c0 = hd * 256 + dc * 128

                def mm(e, tile=tile, c0=c0):
                    ins = None
                    for k in range(8):
                        ins = e.matmul(out=tile[:, 0:256], lhsT=wv[:, k, c0:c0 + 128], rhs=self.mT[:, k, :], start=(k == 0), stop=(k == 7))
                    return ins
                P.op("pe", mm, reads=[("wslot", sl)] + mTr, writes=[bk])
                P.op("act", lambda e, tile=tile, hd=hd, dc=dc: e.activation(out=self.KmT[:, hd, dc, :], in_=tile[:, 0:256], func=AF.Copy),
                     reads=[bk], writes=[("KmT", hd, dc)])
        sl = self.load_w(self.simple(1024, 8, dst.rearrange("(k p) n -> p k n", p=128)[:, :, 1024:2048], wr))
        wv2 = self.wview(sl, 0, 1024, 8)
        for mc in range(2):
            for ch in range(2):
                bk, tile = self.bank()

                def mm(e, tile=tile, mc=mc, ch=ch):
                    ins = None
                    for k in range(8):
                        ins = e.matmul(out=tile[:, :], lhsT=self.mT[:, k, mc * 128:(mc + 1) * 128], rhs=wv2[:, k, ch * 512:(ch + 1) * 512],
                                       start=(k == 0), stop=(k == 7))
                    return ins
                P.op("pe", mm, reads=[("wslot", sl)] + mTr, writes=[bk])
                P.op("act", lambda e, tile=tile, mc=mc, ch=ch: e.activation(out=self.Vm[:, mc, ch * 512:(ch + 1) * 512], in_=tile[:, :], func=AF.Copy),
                     reads=[bk], writes=[("Vm", mc, ch)])

    def xattn_steps(self, layer, steps):
        P = self.P
        X = self.F_X
        Xr = ["F_X%d" % j for j in range(8)]
        wq, wo = "xa_w_q%d" % layer, "xa_w_o%d" % layer
        KmTr = [("KmT", hd, dc) for hd in range(4) for dc in range(2)]
        Vmr = [("Vm", mc, ch) for mc in range(2) for ch in range(2)]

        def step_q(sl):
            hres = self.norm_x("xa_norm_x%d" % layer)
            wv = self.wview(sl, 0, 1024, 8)

            def evac(oc, bk, tile):
                P.op("act", lambda e: e.activation(out=self.BF_C[:, oc, :], in_=tile[:], func=AF.Copy), reads=[bk], writes=[("BF_C", oc)])
            self.proj(wv, [("wslot", sl)], lambda k: self.BF_A[:, k, :], hres, 8, evac)
            def emit_S(hd):
                pms = []
                for mc in range(2):
                    bk, tile = self.bank()

                    def mm(e, tile=tile, hd=hd, mc=mc):
                        ins = None
                        for dc in range(2):
                            ins = e.matmul(out=tile[:], lhsT=self.KmT[:, hd, dc, mc * 128:(mc + 1) * 128], rhs=self.BF_C[:, hd * 2 + dc, :],
                                           start=(dc == 0), stop=(dc == 1))
                        return ins
                    P.op("pe", mm, reads=KmTr + [("BF_C", hd * 2), ("BF_C", hd * 2 + 1)], writes=[bk])
                    pi = (hd * 2 + mc) % 4
                    P.op("act", lambda e, tile=tile, pi=pi: e.activation(out=self.PmT[pi][:], in_=tile[:], func=AF.Exp, scale=1.0 / 16.0),
                         reads=[bk], writes=[("PmT", pi)])
                    pms.append(pi)
                return pms

            def emit_rest(hd, pms):
                bkd, tden = self.bank()

                def mmd(e, tden=tden, pms=pms):
                    ins = None
                    for mc in range(2):
                        ins = e.matmul(out=tden[:], lhsT=self.onesb[:], rhs=self.PmT[pms[mc]][:], start=(mc == 0), stop=(mc == 1))
                    return ins
                P.op("pe", mmd, reads=[("PmT", p) for p in pms] + ["onesbL"], writes=[bkd])
                P.op("act", lambda e, tden=tden: e.activation(out=self.rec[:], in_=tden[:], func=AF.Ln), reads=[bkd], writes=["rec"])
                P.op("act", lambda e: e.activation(out=self.rec[:], in_=self.rec[:], func=AF.Exp, scale=-1.0), reads=["rec"], writes=["rec"])
                for dc in range(2):
                    bk, tile = self.bank()

                    def mmo(e, tile=tile, hd=hd, dc=dc, pms=pms):
                        ins = None
                        for mc in range(2):
                            ins = e.matmul(out=tile[:], lhsT=self.Vm[:, mc, hd * 256 + dc * 128:hd * 256 + dc * 128 + 128],
                                           rhs=self.PmT[pms[mc]][:], start=(mc == 0), stop=(mc == 1))
                        return ins
                    P.op("pe", mmo, reads=Vmr + [("PmT", p) for p in pms], writes=[bk])
                    P.op("dve", lambda e, tile=tile, hd=hd, dc=dc: e.tensor_tensor(out=self.BF_B[:, hd * 2 + dc, :], in0=tile[:], in1=self.rec[:], op=ALU.mult),
                         reads=[bk, "rec"], writes=[("BF_B", hd * 2 + dc)])
            cur = emit_S(0)
            for hd in range(4):
                nxt = emit_S(hd + 1) if hd + 1 < 4 else None
                emit_rest(hd, cur)
                cur = nxt
        dq, _, _ = self.scr[wq]
        steps.append((self.simple(1024, 8, dq.rearrange("(k p) n -> p k n", p=128), self.wres(wq)), step_q))

        def step_o(sl):
            wv = self.wview(sl, 0, 1024, 8)

            def evac(oc, bk, tile):
                P.op("dve", lambda e: e.tensor_tensor(out=X[:, oc, :], in0=tile[:], in1=X[:, oc, :], op=ALU.add),
                     reads=[bk, Xr[oc]], writes=[Xr[oc]])
            self.proj(wv, [("wslot", sl)], lambda k: self.BF_B[:, k, :], [("BF_B", k) for k in range(8)], 8, evac)
        do, _, _ = self.scr[wo]
        steps.append((self.simple(1024, 8, do.rearrange("(k p) n -> p k n", p=128), self.wres(wo)), step_o))

    def norm_x(self, gname):
        Xr = ["F_X%d" % j for j in range(8)]
        return self.norm_multi(self.F_X, Xr, gname, self.BF_A, "BF_A")

    def norm_multi(self, src, src_rs, gname, dst, dst_res, N=512):
        P = self.P
        g = self.param(gname, 8)
        sq = self.BF_B
        for j in range(8):
            if j % 2 == 0:
                P.op("act", lambda e, j=j: e.activation(out=sq[:, j, 0:N], in_=src[:, j, 0:N], func=AF.Square),
                     reads=[src_rs[j]], writes=[("BF_B", j)])
            else:
                P.op("pool", lambda e, j=j: e.tensor_tensor(out=sq[:, j, 0:N], in0=src[:, j, 0:N], in1=src[:, j, 0:N], op=ALU.mult),
                     reads=[src_rs[j]], writes=[("BF_B", j)])
        bk, tile = self.bank()

        def mm(e):
            ins = None
            for j in range(8):
                ins = e.matmul(out=tile[:, 0:N], lhsT=self.onesb[:], rhs=sq[:, j, 0:N], start=(j == 0), stop=(j == 7))
            return ins
        P.op("pe", mm, reads=[("BF_B", j) for j in range(8)] + ["onesbL"], writes=[bk])
        P.op("act", lambda e: e.activation(out=self.rstd[:, 0:N], in_=tile[:, 0:N], func=AF.Ln, scale=1.0 / D, bias=EPS),
             reads=[bk], writes=["rstd"])
        P.op("act", lambda e: e.activation(out=self.rstd[:, 0:N], in_=self.rstd[:, 0:N], func=AF.Exp, scale=-0.5),
             reads=["rstd"], writes=["rstd"])
        for j in range(8):
            P.op("dve", lambda e, j=j: e.scalar_tensor_tensor(out=dst[:, j, 0:N], in0=src[:, j, 0:N], scalar=g[:, j:j + 1],
                                                               in1=self.rstd[:, 0:N], op0=ALU.mult, op1=ALU.mult),
                 reads=[src_rs[j], "rstd", ("param", gname)], writes=[(dst_res, j)])
        return [(dst_res, j) for j in range(8)]

    def ffn_steps(self, layer, steps):
        P = self.P
        X = self.F_X
        Xr = ["F_X%d" % j for j in range(8)]
        wgu, wd = "ffn_gu%d" % layer, "ffn_d%d" % layer
        dgu, _, _ = self.scr[wgu]
        dd, _, _ = self.scr[wd]
        vgu = dgu.rearrange("(k p) n -> p k n", p=128)
        nblocks = (NFC + 3) // 4
        hres_box = {}
        for blk in range(nblocks):
            c_lo = blk * 4
            nch = min(4, NFC - c_lo)

            def step(sl, blk=blk, c_lo=c_lo, nch=nch):
                if blk == 0:
                    hres_box["h"] = self.norm_x("ffn_norm%d" % layer)
                hres = hres_box["h"]
                wgu_v = self.wview(sl, 0, nch * 256, 8)
                for ci in range(nch):
                    ch = c_lo + ci
                    bg, tg = self.bank()
                    bu, tu = self.bank()

                    def mmg(e, tg=tg, ci=ci):
                        ins = None
                        for k in range(8):
                            ins = e.matmul(out=tg[:], lhsT=wgu_v[:, k, ci * 256:ci * 256 + 128], rhs=self.BF_A[:, k, :], start=(k == 0), stop=(k == 7))
                        return ins

                    def mmu(e, tu=tu, ci=ci):
                        ins = None
                        for k in range(8):
                            ins = e.matmul(out=tu[:], lhsT=wgu_v[:, k, ci * 256 + 128:ci * 256 + 256], rhs=self.BF_A[:, k, :], start=(k == 0), stop=(k == 7))
                        return ins
                    P.op("pe", mmg, reads=[("wslot", sl)] + hres, writes=[bg])
                    P.op("pe", mmu, reads=[("wslot", sl)] + hres, writes=[bu])
                    si = ch % 2
                    P.op("act", lambda e, tg=tg, si=si: e.activation(out=self.sg[si][:], in_=tg[:], func=AF.Silu), reads=[bg], writes=[("sg", si)])
                    P.op("dve", lambda e, tu=tu, si=si, ch=ch: e.tensor_tensor(out=self.BF_D[:, ch, :], in0=tu[:], in1=self.sg[si][:], op=ALU.mult),
                         reads=[bu, ("sg", si)], writes=[("BF_D", ch)])
            parts = self.simple(nch * 256, 8, vgu[:, :, c_lo * 256:(c_lo + nch) * 256], self.wres(wgu))
            steps.append((parts, step))
        vd = dd.rearrange("(k p) n -> p k n", p=128)
        for ob in range(4):
            def stepd(sl, ob=ob):
                wv = self.wview(sl, 0, 256, NFC)

                def evac(oc, bk, tile):
                    o = ob * 2 + oc
                    P.op("dve", lambda e: e.tensor_tensor(out=X[:, o, :], in0=tile[:], in1=X[:, o, :], op=ALU.add),
                         reads=[bk, Xr[o]], writes=[Xr[o]])
                self.proj(wv, [("wslot", sl)], lambda k: self.BF_D[:, k, :], [("BF_D", k) for k in range(NFC)], 2, evac, nk=NFC)
            steps.append((self.simple(256, NFC, vd[:, :, ob * 256:(ob + 1) * 256], self.wres(wd)), stepd))

    def run_steps(self, steps):
        widx = [i for i, s_ in enumerate(steps) if s_[0] is not None]
        loaded = {}
        nxt = 0

        def ensure(n):
            nonlocal nxt
            while nxt < min(n, len(widx)):
                i = widx[nxt]
                loaded[i] = self.load_w(steps[i][0])
                nxt += 1
        wcount = 0
        for i, (parts, fn) in enumerate(steps):
            if parts is not None:
                ensure(wcount + 3)
                wcount += 1
                fn(loaded.pop(i))
            else:
                fn(None)


def a1_tile_steps(L, t, steps, first):
    P, io = L.P, L.io
    X = L.F_X
    Xr = ["F_X%d" % j for j in range(8)]
    U = L.F_G[:, 0:2112].rearrange("p (g n) -> p g n", g=4)
    T1 = L.F_G[:, 2112:4224].rearrange("p (g n) -> p g n", g=4)
    T2 = L.F_T2[:].rearrange("p (g n) -> p g n", g=4)
    dwin, _, _ = L.scr["ev_w_in"]
    vwin = dwin.rearrange("(k p) n -> p k n", p=128)[:, :, 0:512]
    win_parts = L.simple(512, 8, vwin, L.wres("ev_w_in"))
    cols = slice(t * 512, (t + 1) * 512)
    pscale = L.param("pool_scaleT", 4)

    if first:
        def step_halo(sl):
            XH = L.tmp[0][:, 0:128].rearrange("p (j n) -> p j n", j=8)
            P.dma("pool", XH, io["xhT"].rearrange("(j p) n -> p j n", p=128), writes=["XH"])
            hres = L.norm_multi(XH, ["XH"] * 8, "ev_normT", L.BF_A, "BF_A", N=16)
            wv = L.wview(sl, 0, 512, 8)

            def evac(oc, bk, tile):
                P.op("act", lambda e: e.activation(out=U[:, oc, 0:16], in_=tile[:, 0:16], func=AF.Copy), reads=[bk], writes=[("U", oc)])
            L.proj(wv, [("wslot", sl)], lambda k: L.BF_A[:, k, 0:16], hres, 4, evac, N=16)
        steps.append((win_parts, step_halo))

    def step_load(_):
        xv = io["xT"].rearrange("(j p) n -> p j n", p=128)
        if t == 0:
            for q4 in range(4):
                P.dma("pool", X[:, 2 * q4:2 * q4 + 2, :], xv[:, 2 * q4:2 * q4 + 2, cols], writes=Xr[2 * q4:2 * q4 + 2])
        else:
            L.take_prefetched(Xr)
        if not L.fused:
            P.dma("pool", L.BF_C[:, 4:8, :], io["yattT"].rearrange("(j p) n -> p j n", p=128)[:, :, cols], writes=[("BF_C", k) for k in range(4, 8)])
        else:
            sel1 = L.param("sel1", 4)
            for cc in range(4):
                g0 = cc * L.chunk + t * 512
                pi, off = g0 // 2048, g0 % 2048
                ya = io["yatt_all_pieces"][pi].rearrange("(j p) n -> p j n", p=128)
                P.dma("pool", L.BF_D[:, 4 * cc:4 * cc + 4, :], ya[:, :, off:off + 512],
                      reads=[("yatt_all", pi)], writes=[("BF_D", 4 * cc + i) for i in range(4)])
            P.op("dve", lambda e: e.tensor_scalar(out=L.BF_C[:, 4:8, :], in0=L.BF_D[:, 0:4, :], scalar1=sel1[:, 0:1], scalar2=None, op0=ALU.mult),
                 reads=[("BF_D", i) for i in range(4)] + [("param", "sel1")], writes=[("BF_C", k) for k in range(4, 8)])
            for cc in range(1, 4):
                P.op("dve", lambda e, cc=cc: e.scalar_tensor_tensor(out=L.BF_C[:, 4:8, :], in0=L.BF_D[:, 4 * cc:4 * cc + 4, :], scalar=sel1[:, cc:cc + 1],
                                                                     in1=L.BF_C[:, 4:8, :], op0=ALU.mult, op1=ALU.add),
                     reads=[("BF_D", 4 * cc + i) for i in range(4)] + [("param", "sel1")] + [("BF_C", k) for k in range(4, 8)],
                     writes=[("BF_C", k) for k in range(4, 8)])
        P.dma("pool", L.invc[:], io["invc"][:, :, cols], writes=["invc"])
    steps.append((None, step_load))

    def step_pool(sl):
        if t > 0:
            for g in range(4):
                P.op("pool", lambda e, g=g: e.tensor_copy(out=U[:, g, 0:16], in_=L.UH[:, g, :]), reads=[("UH", g), ("U", g)], writes=[("U", g)], strict=True)
        hres = L.norm_x("ev_normT")
        wv = L.wview(sl, 0, 512, 8)

        def evac(oc, bk, tile):
            P.op("act", lambda e: e.activation(out=U[:, oc, 16:528], in_=tile[:], func=AF.Copy), reads=[bk], writes=[("U", oc)])
        L.proj(wv, [("wslot", sl)], lambda k: L.BF_A[:, k, :], hres, 4, evac)
        for g in range(4):
            eng = "dve" if g < 2 else "pool"
            ur = ("U", g)
            t1r, t2r = ("T1", g), ("T2", g)
            P.op(eng, lambda e, g=g: e.tensor_tensor(out=T1[:, g, 1:528], in0=U[:, g, 1:528], in1=U[:, g, 0:527], op=ALU.add),
                 reads=[ur], writes=[t1r])
            R, rr = T1, t1r
            if g >= 1:
                P.op(eng, lambda e, g=g: e.tensor_tensor(out=T2[:, g, 3:528], in0=T1[:, g, 3:528], in1=T1[:, g, 1:526], op=ALU.add),
                     reads=[t1r], writes=[t2r])
                R, rr = T2, t2r
            if g >= 2:
                P.op(eng, lambda e, g=g: e.tensor_tensor(out=T1[:, g, 7:528], in0=T2[:, g, 7:528], in1=T2[:, g, 3:524], op=ALU.add),
                     reads=[t2r, t1r], writes=[t1r])
                R, rr = T1, t1r
            if g >= 3:
                P.op(eng, lambda e, g=g: e.tensor_tensor(out=T2[:, g, 15:528], in0=T1[:, g, 15:528], in1=T1[:, g, 7:520], op=ALU.add),
                     reads=[t1r, t2r], writes=[t2r])
                R, rr = T2, t2r
            P.op(eng, lambda e, g=g, R=R: e.tensor_tensor(out=R[:, g, 16:528], in0=R[:, g, 16:528], in1=L.invc[:, g, :], op=ALU.mult),
                 reads=[rr, "invc"], writes=[rr])
            P.op(eng, lambda e, g=g, R=R: e.tensor_tensor(out=L.BF_B[:, g, :], in0=R[:, g, 16:528], in1=U[:, g, 16:528], op=ALU.subtract),
                 reads=[rr, ur], writes=[("BF_B", g)])
            P.op("pool", lambda e, g=g: e.tensor_copy(out=L.UH[:, g, :], in_=U[:, g, 512:528]), reads=[ur], writes=[("UH", g)])
            bk, tile = L.bank()
            P.op("pe", lambda e, g=g, tile=tile: e.matmul(out=tile[:], lhsT=L.poolw[:, g, :], rhs=L.BF_B[:, g, :], start=True, stop=True),
                 reads=[("BF_B", g), "poolw"], writes=[bk])
            P.op("act", lambda e, g=g, tile=tile: e.activation(out=L.BF_C[:, g, :], in_=tile[:], func=AF.Copy, scale=pscale[:, g:g + 1]),
                 reads=[bk, ("param", "pool_scaleT")], writes=[("BF_C", g)])
    steps.append((win_parts, step_pool))

    def step_out(sl):
        wv = L.wview(sl, 0, 1024, 8)

        def evac(oc, bk, tile):
            P.op("dve", lambda e: e.tensor_tensor(out=X[:, oc, :], in0=tile[:], in1=X[:, oc, :], op=ALU.add),
                 reads=[bk, Xr[oc]], writes=[Xr[oc]])
        L.proj(wv, [("wslot", sl)], lambda k: L.BF_C[:, k, :], [("BF_C", k) for k in range(8)], 8, evac)
    dwo, _, _ = L.scr["ev_w_out"]
    steps.append((L.simple(1024, 8, dwo.rearrange("(k p) n -> p k n", p=128), L.wres("ev_w_out")), step_out))
    L.xattn_steps(0, steps)
    if t + 1 < L.NTL:
        steps.append((None, lambda _: L.prefetch_x(io["xT"].rearrange("(j p) n -> p j n", p=128), t + 1)))
    L.ffn_steps(0, steps)

    def step_store(_):
        P.dma("pool", io["x1T"].rearrange("(j p) n -> p j n", p=128)[:, :, cols], X[:], reads=Xr, writes=[("x1T", t)])
    steps.append((None, step_store))


def build_a1(nc, P, es, io, NTL):
    L = Layers(nc, P, es, io, NTL)
    L.invc = L.sb("invc", [128, 4, 512], F32)
    L.poolw = L.sb("poolw", [128, 4, 128], BF16)
    P.dma("pool", L.poolw[:], io["pool_w"].rearrange("g c d -> c g d"), writes=["poolw"])
    L.prep_weight("ev_w_in", io["ev_w_in"], 1024, 928)
    L.prep_weight("ev_w_out", io["ev_w_out"], 1024, 1024)
    L.prep_weight("xa_w_kv0", io["xa_w_kv0"], 1024, 2048)
    L.prep_weight("xa_w_q0", io["xa_w_q0"], 1024, 1024)
    L.prep_weight("xa_w_o0", io["xa_w_o0"], 1024, 1024)
    L.prep_weight_gu("ffn_gu0", io["ffn_gu0"])
    L.prep_weight("ffn_d0", io["ffn_d0"], D_FF, 1024)
    L.mem_kv(0, "xa_w_kv0")
    P.fence()
    L.set_prefetch("F_G")
    steps = []
    for t in range(NTL):
        a1_tile_steps(L, t, steps, first=(t == 0))
    L.run_steps(steps)
    return [("x1T", t) for t in range(NTL)]


def a1_host_inputs(core, inp, yattT_own, NTL):
    b, c = core // 4, core % 4
    n = NTL * 512
    t0 = c * TOK
    xT = np.ascontiguousarray(inp["x"][b][t0:t0 + n].T)
    if t0 == 0:
        xh = np.zeros((16, D), np.float32)
    else:
        xh = inp["x"][b][t0 - 16:t0]
    tg = np.arange(t0, t0 + n)
    invc = np.stack([1.0 / np.minimum(tg + 1, w) for w in (2, 4, 8, 16)]).astype(np.float32)
    T_ = lambda v, k: np.ascontiguousarray(np.asarray(v, np.float32).reshape(k, 128).T)
    return {
        "xT": xT, "xhT": np.ascontiguousarray(xh.T), "yattT": np.ascontiguousarray(yattT_own[:, :n]),
        "invc": np.ascontiguousarray(np.broadcast_to(invc[None], (128, 4, n))),
        "mem": np.ascontiguousarray(inp["mem"][b]),
        "ident": np.eye(128, dtype=np.float32).astype(NPBF),
        "ev_normT": T_(inp["ev_norm"][0], 8), "pool_scaleT": T_(inp["ev_pool_scale"][0], 4),
        "xa_norm_x0": T_(inp["xa_norm_x"][0], 8), "xa_norm_mem0": T_(inp["xa_norm_mem"][0], 8), "ffn_norm0": T_(inp["ffn_norm"][0], 8),
        "pool_w": np.ascontiguousarray(inp["ev_pool_w"][0]),
        "ev_w_in": np.ascontiguousarray(inp["ev_w_in"][0]), "ev_w_out": np.ascontiguousarray(inp["ev_w_out"][0]),
        "xa_w_q0": np.ascontiguousarray(inp["xa_w_q"][0]), "xa_w_kv0": np.ascontiguousarray(inp["xa_w_kv"][0]),
        "xa_w_o0": np.ascontiguousarray(inp["xa_w_o"][0]),
        "ffn_gu0": np.ascontiguousarray(inp["ffn_w_gate_up"][0]), "ffn_d0": np.ascontiguousarray(inp["ffn_w_down"][0]),
    }


def build_a1_program(NTL):
    nc = bass.Bass("TRN2", target_bir_lowering=False)
    n = NTL * 512
    dt = lambda name, shape, dtype, kind="ExternalInput": nc.dram_tensor(name, shape, dtype, kind=kind).ap()
    io = {
        "xT": dt("xT", [D, n], F32), "xhT": dt("xhT", [D, 16], F32), "yattT": dt("yattT", [512, n], BF16),
        "invc": dt("invc", [128, 4, n], F32), "mem": dt("mem", [256, D], F32), "ident": dt("ident", [128, 128], BF16),
        "ev_normT": dt("ev_normT", [128, 8], F32), "pool_scaleT": dt("pool_scaleT", [128, 4], F32),
        "xa_norm_x0": dt("xa_norm_x0", [128, 8], F32), "xa_norm_mem0": dt("xa_norm_mem0", [128, 8], F32),
        "ffn_norm0": dt("ffn_norm0", [128, 8], F32),
        "pool_w": dt("pool_w", [4, 128, 128], F32),
        "ev_w_in": dt("ev_w_in", [D, 928], F32), "ev_w_out": dt("ev_w_out", [D, D], F32),
        "xa_w_q0": dt("xa_w_q0", [D, D], F32), "xa_w_kv0": dt("xa_w_kv0", [D, 2 * D], F32), "xa_w_o0": dt("xa_w_o0", [D, D], F32),
        "ffn_gu0": dt("ffn_gu0", [D, 2 * D_FF], F32), "ffn_d0": dt("ffn_d0", [D_FF, D], F32),
        "x1T": dt("x1T", [D, n], F32, "ExternalOutput"),
    }
    with ExitStack() as es:
        P = Prog(nc, es)
        outs = build_a1(nc, P, es, io, NTL)
        P.wait_all("sp", [P.res[k][0] for k in outs])
        P.emit()
    return nc


def rglru_setup(L, mode):
    P, io = L.P, L.io
    sb = L.sb
    L.XB = sb("XB", [128, 8, 516], BF16)
    L.Dg = sb("Dg", [128, 4, 8, 128], BF16)
    L.Wr = sb("Wr", [128, 8, 256], BF16)
    L.Wi = sb("Wi", [128, 8, 256], BF16)
    L.ST = sb("ST", [128, 8], F32)
    L.STA = sb("STA", [128, 8], F32)
    L.cL = sb("cL", [128, 8], F32)
    L.posi = sb("posi", [128, 512], I32)
    L.nr = sb("nr", [128, 512], F32)
    L.omnr = sb("omnr", [128, 512], F32)
    L.zeros = sb("zeros", [128, 512], F32) if mode == "A2" else None
    convw = L.param("conv_wT", 32)
    lam = L.param("lamT", 8)
    P.dma("pool", L.Wr[:], io["w_rgate"].rearrange("h (cc p) d -> p (h cc) d", p=128), writes=["Wr"])
    P.dma("pool", L.Wi[:], io["w_igate"].rearrange("h (cc p) d -> p (h cc) d", p=128), writes=["Wi"])
    if mode == "A2":
        P.op("pool", lambda e: e.memset(L.zeros[:], 0.0), writes=["zeros"])
    P.op("pool", lambda e: e.memset(L.ST[:], 0.0), writes=["ST"])
    P.op("pool", lambda e: e.memset(L.STA[:], 1.0), writes=["STA"])
    for k in range(4):
        for c in range(8):
            P.op("dve", lambda e, k=k, c=c: e.tensor_scalar(out=L.Dg[:, k, c, :], in0=L.identb[:], scalar1=convw[:, k * 8 + c:k * 8 + c + 1],
                                                              scalar2=None, op0=ALU.mult),
                 reads=["identb", ("param", "conv_wT")], writes=[("Dg", k, c)])
    P.op("act", lambda e: e.activation(out=L.cL[:], in_=lam[:], func=AF.Exp, scale=-1.0), reads=[("param", "lamT")], writes=["cL"])
    P.op("act", lambda e: e.activation(out=L.cL[:], in_=L.cL[:], func=AF.Ln, bias=1.0), reads=["cL"], writes=["cL"])
    P.op("dve", lambda e: e.tensor_scalar(out=L.cL[:], in0=L.cL[:], scalar1=-8.0, scalar2=None, op0=ALU.mult), reads=["cL"], writes=["cL"])
    if mode != "F":
        L.prep_weight("od_w_in", io["od_w_in"], 1024, 2048)


def rglru_tile_steps(L, t, steps, mode, first):
    P, io = L.P, L.io
    X = L.F_X
    Xr = ["F_X%d" % j for j in range(8)]
    XC = L.F_G[:, 0:4096].rearrange("p (c n) -> p c n", c=8)
    cols = slice(t * 512, (t + 1) * 512)
    din, _, _ = L.scr["od_w_in"]
    vin = din.rearrange("(k p) n -> p k n", p=128)
    wr_in = L.wres("od_w_in")
    convb = L.param("conv_bT", 8)
    br = L.param("b_rT", 8)
    bi = L.param("b_iT", 8)
    Dgr = [("Dg", k, c) for k in range(4) for c in range(8)]
    tmp = L.tmp

    if first:
        def step_halo(sl):
            XH = L.sg[0][:, 0:32].rearrange("p (j n) -> p j n", j=8)
            if not L.fused:
                P.dma("pool", XH, io["x1hT"].rearrange("(j p) n -> p j n", p=128), writes=["XH"])
            else:
                selp = L.param("selp", 4)
                HA = L.sg[1][:, 0:128].rearrange("p (c m) -> p c m", c=4)
                P.dma("pool", HA, io["halo_all"].rearrange("(c p) m -> p c m", p=128), reads=["halo_all"], writes=["HA"])
                XHf = L.sg[0][:, 0:32]
                P.op("dve", lambda e: e.tensor_scalar(out=XHf, in0=HA[:, 0, :], scalar1=selp[:, 0:1], scalar2=None, op0=ALU.mult),
                     reads=["HA", ("param", "selp")], writes=["XH"])
                for cc in range(1, 4):
                    P.op("dve", lambda e, cc=cc: e.scalar_tensor_tensor(out=XHf, in0=HA[:, cc, :], scalar=selp[:, cc:cc + 1], in1=XHf, op0=ALU.mult, op1=ALU.add),
                         reads=["HA", "XH", ("param", "selp")], writes=["XH"])
            hres = L.norm_multi(XH, ["XH"] * 8, "od_normT", L.BF_A, "BF_A", N=4)
            wv = L.wview(sl, 0, 1024, 8)

            def evac(oc, bk, tile):
                P.op("act", lambda e: e.activation(out=L.XB[:, oc, 1:4], in_=tile[:, 0:3], func=AF.Copy), reads=[bk], writes=[("XB", oc)])
            L.proj(wv, [("wslot", sl)], lambda k: L.BF_A[:, k, 1:4], hres, 8, evac, N=3)
        steps.append((L.simple(1024, 8, vin[:, :, 1024:2048], wr_in), step_halo))

    def step_load(_):
        xv = io["x1T"].rearrange("(j p) n -> p j n", p=128)
        use_pf = not getattr(L, "pf_none", False)
        if t == 0 or not use_pf:
            for q4 in range(4):
                P.dma("pool", X[:, 2 * q4:2 * q4 + 2, :], xv[:, 2 * q4:2 * q4 + 2, cols], reads=[("x1T", t)], writes=Xr[2 * q4:2 * q4 + 2])
        else:
            L.take_prefetched(Xr)
        if mode == "A2" and use_pf and t + 1 < L.NTL:
            L.prefetch_x(xv, t + 1, extra_reads=[("x1T", t + 1)])
        P.dma("pool", L.posi[:], io["pos_rep"][:, cols], writes=["posi"])
        P.op("dve", lambda e: e.tensor_copy(out=L.nr[:], in_=L.posi[:]), reads=["posi"], writes=["nr"])
        P.op("dve", lambda e: e.tensor_scalar(out=L.nr[:], in0=L.nr[:], scalar1=0.0, scalar2=None, op0=ALU.not_equal), reads=["nr"], writes=["nr"])
        P.op("dve", lambda e: e.tensor_scalar(out=L.omnr[:], in0=L.nr[:], scalar1=-1.0, scalar2=1.0, op0=ALU.mult, op1=ALU.add),
             reads=["nr"], writes=["omnr"])
    steps.append((None, step_load))
    hbox = {}

    if mode == "B":
        def step_gate(sl):
            hbox["h"] = L.norm_x("od_normT")
            wv = L.wview(sl, 0, 1024, 8)

            def evac(oc, bk, tile):
                P.op("act", lambda e: e.activation(out=L.BF_D[:, oc, :], in_=tile[:], func=AF.Gelu_apprx_tanh), reads=[bk], writes=[("BF_D", oc)])
            L.proj(wv, [("wslot", sl)], lambda k: L.BF_A[:, k, :], hbox["h"], 8, evac)
        steps.append((L.simple(1024, 8, vin[:, :, 0:1024], wr_in), step_gate))

    def step_xb(sl):
        if mode != "B":
            hbox["h"] = L.norm_x("od_normT")
        wv = L.wview(sl, 0, 1024, 8)

        def evac(oc, bk, tile):
            P.op("act", lambda e: e.activation(out=L.XB[:, oc, 4:516], in_=tile[:], func=AF.Copy), reads=[bk], writes=[("XB", oc)])
        if DBG < 3:
            return
        L.proj(wv, [("wslot", sl)], lambda k: L.BF_A[:, k, :], hbox["h"], 8, evac)
        for c in range(8 if DBG >= 4 else 0):
            bk, tile = L.bank()

            def mmc(e, c=c, tile=tile):
                ins = None
                for k in range(4):
                    ins = e.matmul(out=tile[:], lhsT=L.Dg[:, k, c, :], rhs=L.XB[:, c, 1 + k:1 + k + 512], start=(k == 0), stop=(k == 3))
                return ins
            P.op("pe", mmc, reads=[("XB", c)] + Dgr, writes=[bk])
            P.op("act", lambda e, c=c, tile=tile: e.activation(out=XC[:, c, :], in_=tile[:], func=AF.Identity, bias=convb[:, c:c + 1]),
                 reads=[bk, ("param", "conv_bT")], writes=[("XC", c)])
            P.op("pool", lambda e, c=c: e.tensor_copy(out=L.BF_B[:, c, :], in_=XC[:, c, :]), reads=[("XC", c)], writes=[("BF_B", c)])
            P.op("pool", lambda e, c=c: e.tensor_copy(out=L.XB[:, c, 1:4], in_=L.XB[:, c, 513:516]), reads=[("XB", c)], writes=[("XB", c)])
        for oc in range(8 if DBG >= 5 else 0):
            hd = oc // 2
            o0 = (oc % 2) * 128
            tr_, ti_, ta_, tm_, tb_, th_ = [tmp[i] for i in range(6)]
            for (W, wn, bias, bn, dst, dn) in ((L.Wr, "Wr", br, "b_rT", tr_, "t0"), (L.Wi, "Wi", bi, "b_iT", ti_, "t1")):
                bk, tile = L.bank()

                def mmg(e, W=W, tile=tile, hd=hd, o0=o0):
                    ins = None
                    for kk in range(2):
                        kc = 2 * hd + kk
                        ins = e.matmul(out=tile[:], lhsT=W[:, kc, o0:o0 + 128], rhs=L.BF_B[:, kc, :], start=(kk == 0), stop=(kk == 1))
                    return ins
                P.op("pe", mmg, reads=[wn, ("BF_B", 2 * hd), ("BF_B", 2 * hd + 1)], writes=[bk])
                P.op("act", lambda e, tile=tile, bias=bias, dst=dst, oc=oc: e.activation(out=dst[:], in_=tile[:], func=AF.Sigmoid, bias=bias[:, oc:oc + 1]),
                     reads=[bk, ("param", bn)], writes=[dn])
            P.op("act", lambda e, oc=oc: e.activation(out=ta_[:], in_=tr_[:], func=AF.Exp, scale=L.cL[:, oc:oc + 1]), reads=["t0", "cL"], writes=["t2"])
            P.op("pool", lambda e: e.tensor_tensor(out=ta_[:], in0=ta_[:], in1=L.nr[:], op=ALU.mult), reads=["t2", "nr"], writes=["t2"])
            P.op("dve", lambda e: e.tensor_tensor(out=tm_[:], in0=ta_[:], in1=ta_[:], op=ALU.mult), reads=["t2"], writes=["t3"])
            P.op("act", lambda e: e.activation(out=tm_[:], in_=tm_[:], func=AF.Sqrt, scale=-1.0, bias=1.0), reads=["t3"], writes=["t3"])
            P.op("dve", lambda e, oc=oc: e.tensor_tensor(out=tb_[:], in0=ti_[:], in1=XC[:, oc, :], op=ALU.mult), reads=["t1", ("XC", oc)], writes=["t4"])
            P.op("pool", lambda e: e.tensor_tensor(out=tb_[:], in0=tb_[:], in1=tm_[:], op=ALU.mult), reads=["t4", "t3"], writes=["t4"])
            P.op("dve", lambda e, oc=oc: e.tensor_tensor_scan(out=th_[:], data0=ta_[:], data1=tb_[:], initial=L.ST[:, oc:oc + 1], op0=ALU.mult, op1=ALU.add),
                 reads=["t2", "t4", "ST"], writes=["t5"])
            P.op("dve", lambda e, oc=oc: e.tensor_copy(out=L.ST[:, oc:oc + 1], in_=th_[:, 511:512]), reads=["t5"], writes=["ST"])
            if mode == "A2":
                P.op("dve", lambda e, oc=oc: e.tensor_tensor_scan(out=tb_[:], data0=ta_[:], data1=L.zeros[:], initial=L.STA[:, oc:oc + 1], op0=ALU.mult, op1=ALU.add),
                     reads=["t2", "t4", "STA", "zeros"], writes=["t4"])
                P.op("dve", lambda e, oc=oc: e.tensor_copy(out=L.STA[:, oc:oc + 1], in_=tb_[:, 511:512]), reads=["t4"], writes=["STA"])
            else:
                P.op("pool", lambda e, oc=oc: e.tensor_tensor(out=L.BF_C[:, oc, :], in0=L.BF_D[:, oc, :], in1=th_[:], op=ALU.mult),
                     reads=[("BF_D", oc), "t5"], writes=[("BF_C", oc)])
    steps.append((L.simple(1024, 8, vin[:, :, 1024:2048], wr_in), step_xb))


def b_tile_steps(L, t, steps, first):
    P, io = L.P, L.io
    X = L.F_X
    Xr = ["F_X%d" % j for j in range(8)]
    cols = slice(t * 512, (t + 1) * 512)
    rglru_tile_steps(L, t, steps, "B", first)

    def step_out(sl):
        wv = L.wview(sl, 0, 1024, 8)

        def evac(oc, bk, tile):
            P.op("dve", lambda e: e.tensor_tensor(out=X[:, oc, :], in0=tile[:], in1=X[:, oc, :], op=ALU.add),
                 reads=[bk, Xr[oc]], writes=[Xr[oc]])
        L.proj(wv, [("wslot", sl)], lambda k: L.BF_C[:, k, :], [("BF_C", k) for k in range(8)], 8, evac)
    dwo, _, _ = L.scr["od_w_out"]
    steps.append((L.simple(1024, 8, dwo.rearrange("(k p) n -> p k n", p=128), L.wres("od_w_out")), step_out))
    L.xattn_steps(1, steps)
    if t + 1 < L.NTL:
        steps.append((None, lambda _: L.prefetch_x(io["x1T"].rearrange("(j p) n -> p j n", p=128), t + 1, extra_reads=[("x1T", t + 1)])))
    L.ffn_steps(1, steps)

    def step_final(_):
        g = L.param("final_normT", 8)
        OUT = L.BF_D[:, 0:16, :].rearrange("p a n -> p (a n)").bitcast(F32).rearrange("p (c n) -> p c n", c=8)
        okeys = [("BF_D", i) for i in range(16)]
        sq = L.BF_B
        for j in range(8):
            if j % 2 == 0:
                P.op("act", lambda e, j=j: e.activation(out=sq[:, j, :], in_=X[:, j, :], func=AF.Square), reads=[Xr[j]], writes=[("BF_B", j)])
            else:
                P.op("pool", lambda e, j=j: e.tensor_tensor(out=sq[:, j, :], in0=X[:, j, :], in1=X[:, j, :], op=ALU.mult), reads=[Xr[j]], writes=[("BF_B", j)])
        bk, tile = L.bank()

        def mm(e):
            ins = None
            for j in range(8):
                ins = e.matmul(out=tile[:], lhsT=L.onesb[:], rhs=sq[:, j, :], start=(j == 0), stop=(j == 7))
            return ins
        P.op("pe", mm, reads=[("BF_B", j) for j in range(8)] + ["onesbL"], writes=[bk])
        P.op("act", lambda e: e.activation(out=L.rstd[:], in_=tile[:], func=AF.Ln, scale=1.0 / D, bias=EPS), reads=[bk], writes=["rstd"])
        P.op("act", lambda e: e.activation(out=L.rstd[:], in_=L.rstd[:], func=AF.Exp, scale=-0.5), reads=["rstd"], writes=["rstd"])
        for j in range(8):
            P.op("dve", lambda e, j=j: e.scalar_tensor_tensor(out=OUT[:, j, :], in0=X[:, j, :], scalar=g[:, j:j + 1], in1=L.rstd[:],
                                                               op0=ALU.mult, op1=ALU.mult),
                 reads=[Xr[j], "rstd", ("param", "final_normT")], writes=okeys[2 * j:2 * j + 2])
        P.dma("pool", io["yT"].rearrange("(j p) n -> p j n", p=128)[:, :, cols], OUT, reads=okeys, writes=[("yT", t)])
    steps.append((None, step_final))


def build_a2(nc, P, es, io, NTL):
    L = Layers(nc, P, es, io, NTL, "A2")
    rglru_setup(L, "A2")
    L.set_prefetch("F_G")
    L.pf_none = True
    steps = []
    for t in range(NTL if DBG >= 1 else 0):
        rglru_tile_steps(L, t, steps, "A2", first=(t == 0))
    L.run_steps(steps)
    P.dma("pool", io["summA"], L.STA[:], reads=["STA"], writes=["summA"])
    P.dma("pool", io["summH"], L.ST[:], reads=["ST"], writes=["summH"])
    return ["summA", "summH"]


def build_b(nc, P, es, io, NTL):
    L = Layers(nc, P, es, io, NTL, "B")
    rglru_setup(L, "B")
    L.prep_weight("od_w_out", io["od_w_out"], 1024, 1024)
    L.prep_weight("xa_w_kv1", io["xa_w_kv1"], 1024, 2048)
    L.prep_weight("xa_w_q1", io["xa_w_q1"], 1024, 1024)
    L.prep_weight("xa_w_o1", io["xa_w_o1"], 1024, 1024)
    L.prep_weight_gu("ffn_gu1", io["ffn_gu1"])
    L.prep_weight("ffn_d1", io["ffn_d1"], D_FF, 1024)
    sA = L.sb("sA", [128, 4, 8], F32)
    sH = L.sb("sH", [128, 4, 8], F32)
    sel = L.param("sel", 4)
    P.dma("pool", sA[:], io["sumA_all"], writes=["sA"])
    P.dma("pool", sH[:], io["sumH_all"], writes=["sH"])
    t8 = L.small
    for cc in range(4):
        P.op("dve", lambda e, cc=cc: e.tensor_tensor(out=t8[:, 16:24], in0=sA[:, cc, :], in1=L.ST[:], op=ALU.mult), reads=["sA", "ST"], writes=["t8"])
        P.op("dve", lambda e, cc=cc: e.tensor_tensor(out=t8[:, 16:24], in0=t8[:, 16:24], in1=sH[:, cc, :], op=ALU.add), reads=["sH", "t8"], writes=["t8"])
        P.op("dve", lambda e: e.tensor_tensor(out=t8[:, 16:24], in0=t8[:, 16:24], in1=L.ST[:], op=ALU.subtract), reads=["t8", "ST"], writes=["t8"])
        P.op("dve", lambda e, cc=cc: e.scalar_tensor_tensor(out=L.ST[:], in0=t8[:, 16:24], scalar=sel[:, cc:cc + 1], in1=L.ST[:], op0=ALU.mult, op1=ALU.add),
             reads=["t8", "ST", ("param", "sel")], writes=["ST"])
    L.mem_kv(1, "xa_w_kv1")
    P.fence()
    L.set_prefetch("F_G")
    steps = []
    for t in range(NTL):
        b_tile_steps(L, t, steps, first=(t == 0))
    L.run_steps(steps)
    return [("yT", t) for t in range(NTL)]


def l1_host_inputs(core, inp, x1T_own, x1hT, NTL, mode, summ=None):
    b, c = core // 4, core % 4
    n = NTL * 512
    t0 = c * TOK
    T_ = lambda v, k: np.ascontiguousarray(np.asarray(v, np.float32).reshape(k, 128).T)
    m = {
        "x1T": np.ascontiguousarray(x1T_own[:, :n]), "x1hT": np.ascontiguousarray(x1hT),
        "pos_rep": np.ascontiguousarray(np.broadcast_to(inp["positions"][b][None, t0:t0 + n], (128, n))).astype(np.int32),
        "ident": np.eye(128, dtype=np.float32).astype(NPBF),
        "od_normT": T_(inp["od_norm"][0], 8),
        "conv_wT": np.ascontiguousarray(inp["od_conv_w"][0].reshape(4, 8, 128).transpose(2, 0, 1).reshape(128, 32)),
        "conv_bT": T_(inp["od_conv_b"][0], 8), "b_rT": T_(inp["od_b_rgate"][0], 8), "b_iT": T_(inp["od_b_igate"][0], 8),
        "lamT": T_(inp["od_lambda"][0], 8),
        "w_rgate": np.ascontiguousarray(inp["od_w_rgate"][0]), "w_igate": np.ascontiguousarray(inp["od_w_igate"][0]),
        "od_w_in": np.ascontiguousarray(inp["od_w_in"][0]),
    }
    if mode == "B":
        sA = np.stack([summ[b * 4 + cc][0] for cc in range(4)], axis=1)
        sH = np.stack([summ[b * 4 + cc][1] for cc in range(4)], axis=1)
        sel = np.zeros((128, 4), np.float32)
        sel[:, :c] = 1.0
        m.update({
            "sumA_all": np.ascontiguousarray(sA), "sumH_all": np.ascontiguousarray(sH), "sel": sel,
            "mem": np.ascontiguousarray(inp["mem"][b]),
            "xa_norm_x1": T_(inp["xa_norm_x"][1], 8), "xa_norm_mem1": T_(inp["xa_norm_mem"][1], 8), "ffn_norm1": T_(inp["ffn_norm"][1], 8),
            "final_normT": T_(inp["final_norm"], 8),
            "od_w_out": np.ascontiguousarray(inp["od_w_out"][0]),
            "xa_w_q1": np.ascontiguousarray(inp["xa_w_q"][1]), "xa_w_kv1": np.ascontiguousarray(inp["xa_w_kv"][1]),
            "xa_w_o1": np.ascontiguousarray(inp["xa_w_o"][1]),
            "ffn_gu1": np.ascontiguousarray(inp["ffn_w_gate_up"][1]), "ffn_d1": np.ascontiguousarray(inp["ffn_w_down"][1]),
        })
    return m


def build_l1_program(NTL, mode):
    nc = bass.Bass("TRN2", target_bir_lowering=False)
    n = NTL * 512
    dt = lambda name, shape, dtype, kind="ExternalInput": nc.dram_tensor(name, shape, dtype, kind=kind).ap()
    io = {
        "x1T": dt("x1T", [D, n], F32), "x1hT": dt("x1hT", [D, 4], F32), "pos_rep": dt("pos_rep", [128, n], I32),
        "ident": dt("ident", [128, 128], BF16), "od_normT": dt("od_normT", [128, 8], F32),
        "conv_wT": dt("conv_wT", [128, 32], F32), "conv_bT": dt("conv_bT", [128, 8], F32),
        "b_rT": dt("b_rT", [128, 8], F32), "b_iT": dt("b_iT", [128, 8], F32), "lamT": dt("lamT", [128, 8], F32),
        "w_rgate": dt("w_rgate", [4, 256, 256], F32), "w_igate": dt("w_igate", [4, 256, 256], F32),
        "od_w_in": dt("od_w_in", [D, 2 * D], F32),
    }
    if mode == "A2":
        io["summA"] = dt("summA", [128, 8], F32, "ExternalOutput")
        io["summH"] = dt("summH", [128, 8], F32, "ExternalOutput")
    else:
        io.update({
            "sumA_all": dt("sumA_all", [128, 4, 8], F32), "sumH_all": dt("sumH_all", [128, 4, 8], F32), "sel": dt("sel", [128, 4], F32),
            "mem": dt("mem", [256, D], F32),
            "xa_norm_x1": dt("xa_norm_x1", [128, 8], F32), "xa_norm_mem1": dt("xa_norm_mem1", [128, 8], F32),
            "ffn_norm1": dt("ffn_norm1", [128, 8], F32), "final_normT": dt("final_normT", [128, 8], F32),
            "od_w_out": dt("od_w_out", [D, D], F32),
            "xa_w_q1": dt("xa_w_q1", [D, D], F32), "xa_w_kv1": dt("xa_w_kv1", [D, 2 * D], F32), "xa_w_o1": dt("xa_w_o1", [D, D], F32),
            "ffn_gu1": dt("ffn_gu1", [D, 2 * D_FF], F32), "ffn_d1": dt("ffn_d1", [D_FF, D], F32),
            "yT": dt("yT", [D, n], F32, "ExternalOutput"),
        })
    with ExitStack() as es:
        P = Prog(nc, es)
        outs = build_a2(nc, P, es, io, NTL) if mode == "A2" else build_b(nc, P, es, io, NTL)
        P.wait_all("sp", [P.res[k][0] for k in outs])
        P.emit()
    return nc


_PROGS = {}


def _prog(key, fn):
    if key not in _PROGS:
        _PROGS[key] = fn()
    return _PROGS[key]


def kernel_multilaunch(inp):
    cores = list(range(NCORE))
    NT, NTL = S // 512, TOK // 512
    nc1 = _prog("p1", lambda: build_p1_program(NT))
    r1 = run_bass_kernel_spmd(nc1, [p1_host_inputs(c, inp, NT) for c in cores], core_ids=cores).results
    yatt = [np.asarray(r["yatt"]) for r in r1]
    nc2 = _prog("a1", lambda: build_a1_program(NTL))
    maps = []
    for core in cores:
        b, c = core // 4, core % 4
        yown = np.concatenate([yatt[b * 4 + hp][:, c * TOK:(c + 1) * TOK] for hp in range(4)], axis=0)
        maps.append(a1_host_inputs(core, inp, yown, NTL))
    r2 = run_bass_kernel_spmd(nc2, maps, core_ids=cores).results
    x1T = [np.asarray(r["x1T"]) for r in r2]
    halos = []
    for core in cores:
        c = core % 4
        halos.append(np.zeros((D, 4), np.float32) if c == 0 else np.ascontiguousarray(x1T[core - 1][:, -4:]))
    nc3 = _prog("a2", lambda: build_l1_program(NTL, "A2"))
    r3 = run_bass_kernel_spmd(nc3, [l1_host_inputs(c, inp, x1T[c], halos[c], NTL, "A2") for c in cores], core_ids=cores).results
    summ = [(np.asarray(r["summA"]), np.asarray(r["summH"])) for r in r3]
    nc4 = _prog("b", lambda: build_l1_program(NTL, "B"))
    r4 = run_bass_kernel_spmd(nc4, [l1_host_inputs(c, inp, x1T[c], halos[c], NTL, "B", summ) for c in cores], core_ids=cores).results
    out = np.empty((B, S, D), np.float32)
    for core in cores:
        b, c = core // 4, core % 4
        out[b, c * TOK:(c + 1) * TOK, :] = np.asarray(r4[core]["yT"]).T
    return out


def kernel(**inputs):
    inp = {k: np.asarray(v) for k, v in inputs.items()}
    if os.environ.get("KMULTI", "0") == "1":
        return kernel_multilaunch(inp)
    return kernel_fused(inp)


RG4 = [[0, 1, 2, 3], [4, 5, 6, 7]]

WEIGHTS = [("ev_w_in", 1024, 928), ("ev_w_out", 1024, 1024), ("xa_w_kv0", 1024, 2048), ("xa_w_q0", 1024, 1024), ("xa_w_o0", 1024, 1024),
           ("ffn_gu0", 1024, 2 * D_FF), ("ffn_d0", D_FF, 1024), ("od_w_in", 1024, 2048), ("od_w_out", 1024, 1024),
           ("xa_w_kv1", 1024, 2048), ("xa_w_q1", 1024, 1024), ("xa_w_o1", 1024, 1024), ("ffn_gu1", 1024, 2 * D_FF), ("ffn_d1", D_FF, 1024)]


class Scratch:
    def __init__(self, nc, P, io):
        self.scr = {}
        self.keys = {}
        self.todo = []
        for name, K, N in WEIGHTS:
            dst = nc.dram_tensor("wbf_" + name, [K, N], BF16).ap()
            src = io[name]
            self.scr[name] = (dst, K, N)
            if name.startswith("ffn_gu"):
                dv = dst.rearrange("k (c two n) -> k c two n", two=2, n=128)
                sv = src.rearrange("k (two c n) -> k two c n", two=2, n=128)
                ks = []
                for r0 in range(0, K, 128):
                    for two in range(2):
                        key = ("wscr", name, (r0 // 128) * 2 + two)
                        ks.append(key)
                        self.todo.append(lambda r0=r0, two=two, dv=dv, sv=sv, key=key: P.dma("pool", dv[r0:r0 + 128, :, two, :], sv[r0:r0 + 128, two, :, :], writes=[key]))
                self.keys[name] = ks
            else:
                ks = []
                for r0 in range(0, K, 128):
                    key = ("wscr", name, r0 // 128)
                    ks.append(key)
                    self.todo.append(lambda r0=r0, dst=dst, src=src, key=key: P.dma("pool", dst[r0:r0 + 128, :], src[r0:r0 + 128, :], writes=[key]))
                self.keys[name] = ks

    def issue(self, n):
        for _ in range(min(n, len(self.todo))):
            self.todo.pop(0)()


def build_fused_program(NT=S // 512, NTL=TOK // 512):
    nc = bass.Bass("TRN2", target_bir_lowering=False)
    n = NTL * 512
    dt = lambda name, shape, dtype, kind="ExternalInput": nc.dram_tensor(name, shape, dtype, kind=kind).ap()
    itn = lambda name, shape, dtype: nc.dram_tensor(name, shape, dtype).ap()
    io = {
        "x": dt("x", [NT * 512, D], F32), "pos": dt("pos", [32, NT * 512], I32), "ident": dt("ident", [128, 128], BF16),
        "masks": dt("masks", [128, 4, 512], BF16), "wq": dt("wq", [256, 192], F32), "wkv": dt("wkv", [128, 256], F32),
        "ev_normT": dt("ev_normT", [128, 8], F32), "q_normT": dt("q_normT", [128, 2], F32), "kv_normT": dt("kv_normT", [128, 1], F32),
        "invf": dt("invf", [128, 1], F32),
        "xT": dt("xT", [D, n], F32), "xhT": dt("xhT", [D, 16], F32), "invc": dt("invc", [128, 4, n], F32), "mem": dt("mem", [256, D], F32),
        "pool_scaleT": dt("pool_scaleT", [128, 4], F32), "xa_norm_x0": dt("xa_norm_x0", [128, 8], F32),
        "xa_norm_mem0": dt("xa_norm_mem0", [128, 8], F32), "ffn_norm0": dt("ffn_norm0", [128, 8], F32),
        "pool_w": dt("pool_w", [4, 128, 128], F32),
        "pos_rep": dt("pos_rep", [128, n], I32), "od_normT": dt("od_normT", [128, 8], F32),
        "conv_wT": dt("conv_wT", [128, 32], F32), "conv_bT": dt("conv_bT", [128, 8], F32),
        "b_rT": dt("b_rT", [128, 8], F32), "b_iT": dt("b_iT", [128, 8], F32), "lamT": dt("lamT", [128, 8], F32),
        "w_rgate": dt("w_rgate", [4, 256, 256], F32), "w_igate": dt("w_igate", [4, 256, 256], F32),
        "xa_norm_x1": dt("xa_norm_x1", [128, 8], F32), "xa_norm_mem1": dt("xa_norm_mem1", [128, 8], F32),
        "ffn_norm1": dt("ffn_norm1", [128, 8], F32), "final_normT": dt("final_normT", [128, 8], F32),
        "sel": dt("sel", [128, 4], F32), "selp": dt("selp", [128, 4], F32), "sel1": dt("sel1", [128, 4], F32),
        "yT": dt("yT", [D, n], F32, "ExternalOutput"),
        "yatt_pieces": [itn("yatt_loc%d" % i, [128, 2048], BF16) for i in range(NT // 4)],
        "yatt_all_pieces": [itn("yatt_all%d" % i, [512, 2048], BF16) for i in range(NT // 4)],
        "x1T": itn("x1T_int", [D, n], F32),
        "halo_loc": itn("halo_loc", [128, 32], F32), "halo_all": itn("halo_all", [512, 32], F32),
        "summ_loc": itn("summ_loc", [128, 16], F32), "summ_all": itn("summ_all", [512, 16], F32),
    }
    for name, K, N in WEIGHTS:
        io[name] = dt(name, [K, N], F32)
    io["w_in"] = io["ev_w_in"]
    with ExitStack() as es0:
        P = Prog(nc, es0)
        SC = Scratch(nc, P, io)
        with ExitStack() as es1:
            def gather_piece(t):
                if t % 4 != 3:
                    return
                pi = t // 4
                P.custom("pool", lambda e, pi=pi: e.collective_compute("AllGather", ALU.bypass, replica_groups=RG4,
                                                                         ins=[io["yatt_pieces"][pi]], outs=[io["yatt_all_pieces"][pi]]),
                         "cc1", reads=[("yatt", h, tt) for h in range(2) for tt in range(4 * pi, 4 * pi + 4)], writes=[("yatt_all", pi)])
            outs1 = build_p1(nc, P, es1, io, NT, tick=None, after=gather_piece, tick1=lambda: SC.issue(1))
            SC.issue(10 ** 6)
            P.emit()
        with ExitStack() as es2:
            L = Layers(nc, P, es2, io, NTL, "B")
            L.fused = True
            L.chunk = NT * 512 // 4
            L.scr = SC.scr
            L.scr_keys = SC.keys
            L.poolw = L.sb("poolw", [128, 4, 128], BF16)
            P.dma("pool", L.poolw[:], io["pool_w"].rearrange("g c d -> c g d"), writes=["poolw"])
            rglru_setup(L, "F")
            L.invc = L.XB[:].rearrange("p a n -> p (a n)").bitcast(F32)[:, 0:2048].rearrange("p (g n) -> p g n", g=4)
            L.zeros = L.rec
            L.mem_kv(0, "xa_w_kv0")
            P.fence()
            L.set_prefetch("F_G")
            steps = []
            for t in range(NTL):
                a1_tile_steps(L, t, steps, first=(t == 0))
            L.run_steps(steps)
            P.dma("pool", io["halo_loc"].rearrange("p (j m) -> p j m", j=8), L.F_X[:, :, 508:512], reads=["F_X%d" % j for j in range(8)], writes=["halo_loc"])
            P.fence()
            P.custom("pool", lambda e: e.collective_compute("AllGather", ALU.bypass, replica_groups=RG4, ins=[io["halo_loc"]], outs=[io["halo_all"]]),
                     "cc2", reads=["halo_loc"], writes=["halo_all"])
            P.op("pool", lambda e: e.memset(L.zeros[:], 0.0), reads=["rec"], writes=["rec", "zeros"])
            L.pf_none = True
            steps = []
            for t in range(NTL):
                rglru_tile_steps(L, t, steps, "A2", first=(t == 0))
            L.run_steps(steps)
            L.pf_none = False
            P.dma("pool", io["summ_loc"][:, 0:8], L.STA[:], reads=["STA"], writes=["summ_locA"])
            P.dma("pool", io["summ_loc"][:, 8:16], L.ST[:], reads=["ST"], writes=["summ_locH"])
            P.fence()
            P.custom("pool", lambda e: e.collective_compute("AllGather", ALU.bypass, replica_groups=RG4, ins=[io["summ_loc"]], outs=[io["summ_all"]]),
                     "cc3", reads=["summ_locA", "summ_locH"], writes=["summ_all"])
            sAH = L.sg[1][:, 128:192].rearrange("p (c m) -> p c m", c=4)
            sel = L.param("sel", 4)
            P.dma("pool", sAH, io["summ_all"].rearrange("(c p) m -> p c m", p=128), reads=["summ_all"], writes=["sAH"])
            P.op("pool", lambda e: e.memset(L.ST[:], 0.0), reads=["ST"], writes=["ST"])
            t8 = L.small
            for cc in range(4):
                P.op("dve", lambda e, cc=cc: e.tensor_tensor(out=t8[:, 16:24], in0=sAH[:, cc, 0:8], in1=L.ST[:], op=ALU.mult), reads=["sAH", "ST"], writes=["t8"])
                P.op("dve", lambda e, cc=cc: e.tensor_tensor(out=t8[:, 16:24], in0=t8[:, 16:24], in1=sAH[:, cc, 8:16], op=ALU.add), reads=["sAH", "t8"], writes=["t8"])
                P.op("dve", lambda e: e.tensor_tensor(out=t8[:, 16:24], in0=t8[:, 16:24], in1=L.ST[:], op=ALU.subtract), reads=["t8", "ST"], writes=["t8"])
                P.op("dve", lambda e, cc=cc: e.scalar_tensor_tensor(out=L.ST[:], in0=t8[:, 16:24], scalar=sel[:, cc:cc + 1], in1=L.ST[:], op0=ALU.mult, op1=ALU.add),
                     reads=["t8", "ST", ("param", "sel")], writes=["ST"])
            L.mem_kv(1, "xa_w_kv1")
            P.fence()
            L.set_prefetch("F_G")
            steps = []
            for t in range(NTL):
                b_tile_steps(L, t, steps, first=(t == 0))
            L.run_steps(steps)
            P.wait_all("sp", [P.res[("yT", t)][0] for t in range(NTL)])
            P.emit()
    return nc


def fused_host_inputs(core, inp, NT=S // 512, NTL=TOK // 512):
    b, c = core // 4, core % 4
    m = {}
    m.update(p1_host_inputs(core, inp, NT))
    m.pop("w_in")
    a1 = a1_host_inputs(core, inp, np.zeros((512, NTL * 512), NPBF), NTL)
    a1.pop("yattT")
    m.update(a1)
    l1 = l1_host_inputs(core, inp, np.zeros((D, NTL * 512), np.float32), np.zeros((D, 4), np.float32), NTL, "B",
                        [(np.zeros((128, 8), np.float32),) * 2] * 8)
    for k in ("x1T", "x1hT", "sumA_all", "sumH_all"):
        l1.pop(k)
    m.update(l1)
    selp = np.zeros((128, 4), np.float32)
    if c > 0:
        selp[:, c - 1] = 1.0
    sel1 = np.zeros((128, 4), np.float32)
    sel1[:, c] = 1.0
    m["selp"] = selp
    m["sel1"] = sel1
    return m


def kernel_fused(inp):
    cores = list(range(NCORE))
    nc = _prog("fused", build_fused_program)
    r = run_bass_kernel_spmd(nc, [fused_host_inputs(c, inp) for c in cores], core_ids=cores).results
    out = np.empty((B, S, D), np.float32)
    for core in cores:
        b, c = core // 4, core % 4
        out[b, c * TOK:(c + 1) * TOK, :] = np.asarray(r[core]["yT"]).T
    return out
```

```python
import os
import numpy as np
from contextlib import ExitStack
DBG = int(os.environ.get('KDBG', '9'))
STRICT = os.environ.get('KSTRICT', '1') == '1'
import ml_dtypes
import concourse.bass as bass
import concourse.mybir as mybir
from concourse.bass_utils import run_bass_kernel_spmd

F32 = mybir.dt.float32
BF16 = mybir.dt.bfloat16
I32 = mybir.dt.int32
AF = mybir.ActivationFunctionType
ALU = mybir.AluOpType
NPBF = ml_dtypes.bfloat16

D = 1024
S = 16384
B = 2
NCORE = 8
TOK = 4096
T = 512
EPS = 1e-6
MAGIC = 12582912.0
TWO_PI = 6.283185307179586
C1 = 6.28125
C2 = TWO_PI - C1


class Prog:
    COMPUTE = ("pe", "act", "dve", "pool")
    ENGS = ("pe", "act", "dve", "pool", "sp")
    NDMA = 24

    def __init__(self, nc, es):
        self.nc = nc
        self.es = es
        self.ops = {e: [] for e in self.ENGS}
        self.sems = {e: es.enter_context(nc.semaphore("s_" + e)) for e in self.COMPUTE}
        self.cnt = {e: 0 for e in self.COMPUTE}
        self.seen = {e: {} for e in self.ENGS}
        self.res = {}
        self.dsem = {q: [es.enter_context(nc.semaphore("d%s%d" % (q, i))) for i in range(self.NDMA)] for q in ("sp", "pool")}
        self.duse = {q: [0] * self.NDMA for q in ("sp", "pool")}
        self.dnext = {"sp": 0, "pool": 0}
        self.extra_sems = {}

    def _deps(self, eng, reads, writes, strict=False):
        deps = []
        for r in reads:
            st = self.res.get(r)
            if st is not None and st[0] is not None:
                tk = st[0]
                if tk[3] == eng and eng in ("pe", "sp"):
                    continue
                deps.append(tk)
        for w in writes:
            st = self.res.get(w)
            if st is not None:
                for tk in ([st[0]] if st[0] is not None else []) + st[1]:
                    if tk[3] == eng and not ((STRICT or strict) and eng != "pe"):
                        continue
                    deps.append(tk)
        return deps

    def _filter(self, eng, deps):
        waits = []
        best = {}
        for (key, sem, val, _e) in deps:
            if key not in best or best[key][1] < val:
                best[key] = (sem, val)
        for key, (sem, val) in best.items():
            if self.waited[eng].get(key, 0) < val:
                self.waited[eng][key] = val
                waits.append((sem, val))
        return waits

    waited = None

    def _update(self, tok, reads, writes):
        for r in reads:
            st = self.res.get(r)
            if st is None:
                self.res[r] = (None, [tok])
            else:
                st[1].append(tok)
        for w in writes:
            self.res[w] = (tok, [])

    def op(self, eng, fn, reads=(), writes=(), strict=False):
        if self.waited is None:
            self.waited = {e: {} for e in self.ENGS}
        deps = self._deps(eng, reads, writes, strict)
        waits = self._filter(eng, deps)
        self.cnt[eng] += 1
        tok = ("c_" + eng, self.sems[eng], self.cnt[eng], eng)
        self.ops[eng].append((fn, waits, (self.sems[eng], 1)))
        self._update(tok, reads, writes)
        return tok

    def dma(self, q, out, in_, reads=(), writes=(), **kw):
        if self.waited is None:
            self.waited = {e: {} for e in self.ENGS}
        k = self.dnext[q]
        self.dnext[q] = (k + 1) % self.NDMA
        prev = self.duse[q][k]
        self.duse[q][k] += 1
        dsem = self.dsem[q][k]
        dkey = "d%s%d" % (q, k)
        deps = self._deps(None, reads, writes)
        if prev > 0:
            deps.append((dkey, dsem, 16 * prev, "dma"))
        waits = self._filter(q, deps)
        tok = (dkey, dsem, 16 * self.duse[q][k], "dma")

        def fn(e, out=out, in_=in_, kw=kw):
            return e.dma_start(out=out, in_=in_, **kw)
        self.ops[q].append((fn, waits, (dsem, 16)))
        self._update(tok, reads, writes)
        return tok

    def custom(self, q, fn, sem_name, reads=(), writes=()):
        if sem_name not in self.extra_sems:
            self.extra_sems[sem_name] = [self.es.enter_context(self.nc.semaphore(sem_name)), 0]
        ent = self.extra_sems[sem_name]
        deps = self._deps(None, reads, writes)
        waits = self._filter(q, deps)
        ent[1] += 1
        tok = (sem_name, ent[0], ent[1], "x")
        self.ops[q].append((fn, waits, (ent[0], 1)))
        self._update(tok, reads, writes)
        return tok

    def fence(self):
        if self.waited is None:
            self.waited = {e: {} for e in self.ENGS}
        deps = [("c_" + e, self.sems[e], self.cnt[e], e) for e in self.COMPUTE if self.cnt[e] > 0]
        for q in ("sp", "pool"):
            for k in range(self.NDMA):
                if self.duse[q][k] > 0:
                    deps.append(("d%s%d" % (q, k), self.dsem[q][k], 16 * self.duse[q][k], "dma"))
        for name, ent in self.extra_sems.items():
            if ent[1] > 0:
                deps.append((name, ent[0], ent[1], "x"))
        for q in self.ENGS:
            own = [d for d in deps if d[3] != q]
            self.ops[q].append((None, self._filter(q, own), None))

    def wait_all(self, q, toks):
        waits = self._filter(q, list(toks))
        self.ops[q].append((None, waits, None))

    def emit(self):
        nc = self.nc
        with nc.Block() as block:
            def run(name, e):
                for (fn, waits, inc) in self.ops[name]:
                    for (sem, val) in waits:
                        e.wait_ge(sem, val)
                    if fn is None:
                        continue
                    ins = fn(e)
                    if inc is not None:
                        ins.then_inc(inc[0], inc[1])

            @block.tensor
            def _(e):
                run("pe", e)

            @block.scalar
            def _(e):
                run("act", e)

            @block.vector
            def _(e):
                run("dve", e)

            @block.gpsimd
            def _(e):
                run("pool", e)

            @block.sync
            def _(e):
                run("sp", e)
        self.ops = {e: [] for e in self.ENGS}


class Rot:
    def __init__(self, items):
        self.items = list(items)
        self.i = 0

    def next(self):
        it = self.items[self.i % len(self.items)]
        self.i += 1
        return it


def build_p1(nc, P, es, io, NT, tick=None, after=None):
    sb = lambda name, shape, dt: es.enter_context(nc.sbuf_tensor("sb_" + name, shape, dt))
    ps = lambda name, shape, dt: es.enter_context(nc.psum_tensor(name, shape, dt))
    SCALE = 96.0 ** -0.5
    NKB = 4 * NT

    ident = sb("ident", [128, 128], BF16)
    masks = sb("masks", [128, 4, 512], BF16)
    onesb = sb("onesb", [128, 128], BF16)
    ones32 = sb("ones32", [128, 64], F32)
    W1 = sb("W1", [128, 8, 448], BF16)
    Wq = sb("Wq", [128, 2, 2, 128], BF16)
    Wkv = sb("Wkv", [128, 256], BF16)
    gT = sb("gT", [128, 8], F32)
    gq = sb("gq", [128, 2], F32)
    gkv = sb("gkv", [128, 1], F32)
    invf = sb("invf", [128, 1], F32)
    xtm = [sb("xtm%d" % i, [128, 1024], F32) for i in range(8)]
    hn = [sb("hn%d" % i, [128, 1024], BF16) for i in range(2)]
    junk = sb("junk", [128, 1024], BF16)
    ssq = [sb("ssq%d" % i, [128, 1], F32) for i in range(4)]
    rstd = [sb("rstd%d" % i, [128, 1], F32) for i in range(4)]
    hT = [sb("hT%d" % i, [128, 8, 512], BF16) for i in range(1)]
    cq_sb = sb("cq_sb", [128, 2, 512], F32)
    cq_sq = sb("cq_sq", [128, 2, 512], BF16)
    ckv_sb = sb("ckv_sb", [128, 512], F32)
    ckv_sq = sb("ckv_sq", [128, 512], BF16)
    rstd_q = sb("rstd_q", [128, 512], F32)
    rstd_kv = sb("rstd_kv", [128, 512], F32)
    cqn = sb("cqn", [128, 2, 512], BF16)
    ckvn = sb("ckvn", [128, 512], BF16)
    cs = sb("cs", [128, 512], F32)
    sn = sb("sn", [128, 512], F32)
    pos_i = [sb("pos_i%d" % i, [128, 512], I32) for i in range(2)]
    ang = sb("ang", [128, 512], F32)
    ang2 = sb("ang2", [128, 512], F32)
    rr_k = sb("rr_k", [128, 512], F32)
    rr_r = sb("rr_r", [128, 512], F32)
    rt1 = sb("rt1", [128, 512], F32)
    rt2 = sb("rt2", [128, 512], F32)
    QT = [sb("QT%d" % i, [128, 2, 512], BF16) for i in range(2)]
    KT = sb("KT", [128, 2, NKB * 128], BF16)
    Vaug = sb("Vaug", [128, NKB, 2, 65], BF16)
    pT = [sb("pT%d" % i, [128, 512], BF16) for i in range(4)]
    o_sb = [sb("o_sb%d" % i, [128, 512], F32) for i in range(2)]
    yT = [sb("yT%d" % i, [128, 512], BF16) for i in range(2)]

    ps_m = [ps("ps_m%d" % i, [128, 512], F32) for i in range(3)]
    ps_s = [ps("ps_s%d" % i, [128, 512], F32) for i in range(3)]
    ps_acc = [ps("ps_acc%d" % i, [128, 512], F32) for i in range(2)]
    pm = Rot([("ps_m", i) for i in range(3)])
    sr = Rot([0, 1, 2])
    pr = Rot([0, 1, 2, 3])

    P.dma("sp", ident[:], io["ident"], writes=["ident"])
    P.dma("sp", masks[:], io["masks"], writes=["masks"])
    for j in range(8):
        P.dma("sp", xtm[j][:, 0:416], io["w_in"][j * 128:(j + 1) * 128, 512:928], writes=[("xtm", j)])
    for c in range(2):
        P.dma("sp", xtm[c][:, 512:704], io["wq"][c * 128:(c + 1) * 128, :], writes=[("xtm", c)])
    P.dma("sp", xtm[2][:, 512:768], io["wkv"], writes=[("xtm", 2)])
    P.dma("sp", gT[:], io["ev_normT"], writes=["gT"])
    P.dma("sp", gq[:], io["q_normT"], writes=["gq"])
    P.dma("sp", gkv[:], io["kv_normT"], writes=["gkv"])
    P.dma("sp", invf[:], io["invf"], writes=["invf"])
    P.op("pool", lambda e: e.memset(onesb[:], 1.0), writes=["onesb"])
    P.op("pool", lambda e: e.memset(ones32[:], 1.0), writes=["ones32"])
    P.op("pool", lambda e: e.memset(Vaug[:, :, :, 64:65], 1.0), writes=["vones"])
    for j in range(8):
        P.op("dve", lambda e, j=j: e.tensor_scalar(out=W1[:, j, 0:416], in0=xtm[j][:, 0:416], scalar1=gT[:, j:j + 1],
                                                    scalar2=None, op0=ALU.mult),
             reads=[("xtm", j), "gT"], writes=[("W1", j)])
        P.op("dve", lambda e, j=j: e.tensor_scalar(out=W1[:, j, 416:432], in0=xtm[j][:, 400:416], scalar1=gT[:, j:j + 1],
                                                    scalar2=-1.0, op0=ALU.mult, op1=ALU.mult),
             reads=[("xtm", j), "gT"], writes=[("W1", j)])
        P.op("dve", lambda e, j=j: e.tensor_scalar(out=W1[:, j, 432:448], in0=xtm[j][:, 384:400], scalar1=gT[:, j:j + 1],
                                                    scalar2=None, op0=ALU.mult),
             reads=[("xtm", j), "gT"], writes=[("W1", j)])
    for c in range(2):
        for h in range(2):
            b0 = h * 96
            P.op("dve", lambda e, c=c, h=h, b0=b0: e.tensor_scalar(out=Wq[:, c, h, 0:96], in0=xtm[c][:, 512 + b0:512 + b0 + 96],
                                                                   scalar1=gq[:, c:c + 1], scalar2=None, op0=ALU.mult),
                 reads=[("xtm", c), "gq"], writes=[("Wq", c, h)])
            P.op("dve", lambda e, c=c, h=h, b0=b0: e.tensor_scalar(out=Wq[:, c, h, 96:112], in0=xtm[c][:, 512 + b0 + 80:512 + b0 + 96],
                                                                   scalar1=gq[:, c:c + 1], scalar2=-1.0, op0=ALU.mult, op1=ALU.mult),
                 reads=[("xtm", c), "gq"], writes=[("Wq", c, h)])
            P.op("dve", lambda e, c=c, h=h, b0=b0: e.tensor_scalar(out=Wq[:, c, h, 112:128], in0=xtm[c][:, 512 + b0 + 64:512 + b0 + 80],
                                                                   scalar1=gq[:, c:c + 1], scalar2=None, op0=ALU.mult),
                 reads=[("xtm", c), "gq"], writes=[("Wq", c, h)])
    P.op("dve", lambda e: e.tensor_scalar(out=Wkv[:], in0=xtm[2][:, 512:768], scalar1=gkv[:, 0:1], scalar2=None, op0=ALU.mult),
         reads=[("xtm", 2), "gkv"], writes=["Wkv"])
    W1r = [("W1", j) for j in range(8)]
    Wqr = [("Wq", c, h) for c in range(2) for h in range(2)]

    xs = Rot(range(8))
    x_slots = {}

    def load_x(t):
        for s in range(4):
            sl = xs.next()
            x_slots[(t, s)] = sl
            r0 = t * 512 + s * 128
            P.dma("sp", xtm[sl][:], io["x"][r0:r0 + 128, :], writes=[("xtm", sl)])
        P.dma("sp", pos_i[t % 2][64:96, :], io["pos"][:, t * 512:(t + 1) * 512], writes=[("pos_i", t % 2)])

    def prep_stages(t):
        hb = t % 2
        hTt = hT[0]
        QTt = QT[hb]
        col0 = t * 512
        st = []

        def stage_norm_front(s):
            def f():
                sl = x_slots[(t, s)]
                a = s % 4
                hb2 = s % 2
                P.op("act", lambda e: e.activation(out=junk[:], in_=xtm[sl][:], func=AF.Square, accum_out=ssq[a][:]),
                     reads=[("xtm", sl)], writes=["junk", ("ssq", a)])
                P.op("act", lambda e: e.activation(out=rstd[a][:], in_=ssq[a][:], func=AF.Ln, scale=1.0 / D, bias=EPS),
                     reads=[("ssq", a)], writes=[("rstd", a)])
                P.op("act", lambda e: e.activation(out=rstd[a][:], in_=rstd[a][:], func=AF.Exp, scale=-0.5),
                     reads=[("rstd", a)], writes=[("rstd", a)])
                P.op("dve", lambda e: e.tensor_scalar(out=hn[hb2][:], in0=xtm[sl][:], scalar1=rstd[a][:, 0:1], scalar2=None,
                                                       op0=ALU.mult),
                     reads=[("xtm", sl), ("rstd", a)], writes=[("hn", hb2)])
            return f

        def stage_norm_back(s):
            def f():
                hb2 = s % 2
                btr = pm.next()
                ptr_ = ps_m[btr[1]][:].bitcast(BF16).rearrange("p (j n) -> p j n", j=8)

                def tr(e):
                    ins = None
                    for j in range(8):
                        ins = e.transpose(out=ptr_[:, j, :], in_=hn[hb2][:, j * 128:(j + 1) * 128], identity=ident[:])
                    return ins
                P.op("pe", tr, reads=[("hn", hb2), "ident"], writes=[btr])
                P.op("dve", lambda e: e.tensor_copy(out=hTt[:, :, s * 128:(s + 1) * 128], in_=ptr_),
                     reads=[btr], writes=[("hT", 0, s)])
            return f
        st.append(stage_norm_front(0))
        st.append(stage_norm_front(1))
        st.append(stage_norm_back(0))
        st.append(stage_norm_front(2))
        st.append(stage_norm_back(1))
        st.append(stage_norm_front(3))
        st.append(stage_norm_back(2))
        hTr = [("hT", 0, s) for s in range(4)]

        def stage_rope_tables():
            r = slice(64, 96)
            P.op("dve", lambda e: e.tensor_copy(out=ang[r, :], in_=pos_i[hb][r, :]), reads=[("pos_i", hb)], writes=["ang"])
            P.op("dve", lambda e: e.tensor_scalar(out=ang[r, :], in0=ang[r, :], scalar1=invf[r, 0:1], scalar2=None, op0=ALU.mult),
                 reads=["ang", "invf"], writes=["ang"])
            P.op("dve", lambda e: e.tensor_scalar(out=ang2[r, :], in0=ang[r, :], scalar1=float(np.float32(np.pi / 2)), scalar2=None,
                                                   op0=ALU.add), reads=["ang"], writes=["ang2"])
            for (src, srcn, dst, dstn) in ((ang, "ang", sn, "sn"), (ang2, "ang2", cs, "cs")):
                P.op("dve", lambda e, src=src: e.tensor_scalar(out=rr_k[r, :], in0=src[r, :], scalar1=float(1.0 / TWO_PI), scalar2=MAGIC,
                                                                 op0=ALU.mult, op1=ALU.add), reads=[srcn], writes=["rr_k"])
                P.op("dve", lambda e: e.tensor_scalar(out=rr_k[r, :], in0=rr_k[r, :], scalar1=-MAGIC, scalar2=None, op0=ALU.add),
                     reads=["rr_k"], writes=["rr_k"])
                P.op("dve", lambda e, src=src: e.scalar_tensor_tensor(out=rr_r[r, :], in0=rr_k[r, :], scalar=-C1, in1=src[r, :],
                                                                        op0=ALU.mult, op1=ALU.add), reads=["rr_k", srcn], writes=["rr_r"])
                P.op("dve", lambda e: e.scalar_tensor_tensor(out=rr_r[r, :], in0=rr_k[r, :], scalar=-C2, in1=rr_r[r, :],
                                                              op0=ALU.mult, op1=ALU.add), reads=["rr_k", "rr_r"], writes=["rr_r"])
                P.op("dve", lambda e: e.tensor_scalar(out=rr_r[r, :], in0=rr_r[r, :], scalar1=3.1415925, scalar2=-3.1415925, op0=ALU.min, op1=ALU.max),
                     reads=["rr_r"], writes=["rr_r"])
                P.op("act", lambda e, dst=dst: e.activation(out=dst[r, :], in_=rr_r[r, :], func=AF.Sin),
                     reads=["rr_r"], writes=[dstn])
        st.append(stage_rope_tables)
        st.append(stage_norm_back(3))

        def stage_proj():
            banks = {}
            for name, c0, m, prow in (("cq0", 0, 128, 0), ("cq1", 128, 128, 0), ("ckv", 256, 128, 0),
                                      ("kr", 384, 32, 64), ("krr", 416, 32, 64)):
                bk = pm.next()
                banks[name] = bk
                tile = ps_m[bk[1]]

                def mm(e, c0=c0, m=m, prow=prow, tile=tile):
                    ins = None
                    for j in range(8):
                        ins = e.matmul(out=tile[prow:prow + m, :], lhsT=W1[:, j, c0:c0 + m], rhs=hTt[:, j, :],
                                       start=(j == 0), stop=(j == 7))
                    return ins
                P.op("pe", mm, reads=W1r + hTr, writes=[bk])
                if name in ("cq0", "cq1"):
                    c = 0 if name == "cq0" else 1
                    P.op("act", lambda e, c=c, tile=tile: e.activation(out=cq_sb[:, c, :], in_=tile[:], func=AF.Copy),
                         reads=[bk], writes=[("cq_sb", c)])
                    P.op("pool", lambda e, c=c: e.tensor_tensor(out=cq_sq[:, c, :], in0=cq_sb[:, c, :], in1=cq_sb[:, c, :], op=ALU.mult),
                         reads=[("cq_sb", c)], writes=[("cq_sq", c)])
                elif name == "ckv":
                    P.op("act", lambda e, tile=tile: e.activation(out=ckv_sb[:], in_=tile[:], func=AF.Copy),
                         reads=[bk], writes=["ckv_sb"])
                    P.op("pool", lambda e: e.tensor_tensor(out=ckv_sq[:], in0=ckv_sb[:], in1=ckv_sb[:], op=ALU.mult),
                         reads=["ckv_sb"], writes=["ckv_sq"])
            r = slice(64, 96)
            tk, tkr = ps_m[banks["kr"][1]], ps_m[banks["krr"][1]]
            P.op("dve", lambda e: e.tensor_tensor(out=rt1[r, :], in0=tk[r, :], in1=cs[r, :], op=ALU.mult),
                 reads=[banks["kr"], "cs"], writes=["rt1"])
            P.op("dve", lambda e: e.tensor_tensor(out=rt2[r, :], in0=tkr[r, :], in1=sn[r, :], op=ALU.mult),
                 reads=[banks["krr"], "sn"], writes=["rt2"])
            P.op("dve", lambda e: e.tensor_tensor(out=KT[r, 0, col0:col0 + 512], in0=rt1[r, :], in1=rt2[r, :], op=ALU.add),
                 reads=["rt1", "rt2"], writes=[("KTr", 0, t)])
            P.op("pool", lambda e: e.tensor_copy(out=KT[r, 1, col0:col0 + 512], in_=KT[r, 0, col0:col0 + 512]),
                 reads=[("KTr", 0, t)], writes=[("KTr", 1, t)])
        st.append(stage_proj)

        def stage_norms():
            for (nm, nchunk, sq, dst, dstn, dim) in (("q", 2, cq_sq, rstd_q, "rstd_q", 256), ("kv", 1, ckv_sq, rstd_kv, "rstd_kv", 128)):
                bk = pm.next()
                tile = ps_m[bk[1]]

                def mm(e, nchunk=nchunk, sq=sq, tile=tile):
                    ins = None
                    for c in range(nchunk):
                        rhs = sq[:, c, :] if nchunk == 2 else sq[:]
                        ins = e.matmul(out=tile[:], lhsT=onesb[:], rhs=rhs, start=(c == 0), stop=(c == nchunk - 1))
                    return ins
                rd = [("cq_sq", 0), ("cq_sq", 1)] if nm == "q" else ["ckv_sq"]
                P.op("pe", mm, reads=rd + ["onesb"], writes=[bk])
                P.op("act", lambda e, tile=tile, dst=dst, dim=dim: e.activation(out=dst[:], in_=tile[:], func=AF.Ln, scale=1.0 / dim, bias=EPS),
                     reads=[bk], writes=[dstn])
                P.op("act", lambda e, dst=dst: e.activation(out=dst[:], in_=dst[:], func=AF.Exp, scale=-0.5),
                     reads=[dstn], writes=[dstn])
            for c in range(2):
                P.op("dve", lambda e, c=c: e.tensor_tensor(out=cqn[:, c, :], in0=cq_sb[:, c, :], in1=rstd_q[:], op=ALU.mult),
                     reads=[("cq_sb", c), "rstd_q"], writes=[("cqn", c)])
            P.op("pool", lambda e: e.tensor_tensor(out=ckvn[:], in0=ckv_sb[:], in1=rstd_kv[:], op=ALU.mult),
                 reads=["ckv_sb", "rstd_kv"], writes=["ckvn"])
        st.append(stage_norms)

        def stage_q(h):
            def f():
                r = slice(64, 96)
                bq, bqr = pm.next(), pm.next()
                tq, tqr = ps_m[bq[1]], ps_m[bqr[1]]

                def mmq(e):
                    ins = None
                    for c in range(2):
                        ins = e.matmul(out=tq[0:96, :], lhsT=Wq[:, c, h, 0:96], rhs=cqn[:, c, :], start=(c == 0), stop=(c == 1))
                    return ins

                def mmr(e):
                    ins = None
                    for c in range(2):
                        ins = e.matmul(out=tqr[64:96, :], lhsT=Wq[:, c, h, 96:128], rhs=cqn[:, c, :], start=(c == 0), stop=(c == 1))
                    return ins
                P.op("pe", mmq, reads=Wqr + [("cqn", 0), ("cqn", 1)], writes=[bq])
                P.op("pe", mmr, reads=Wqr + [("cqn", 0), ("cqn", 1)], writes=[bqr])
                P.op("act", lambda e: e.activation(out=QTt[0:64, h, :], in_=tq[0:64, :], func=AF.Copy),
                     reads=[bq], writes=[("QTn", hb, h)])
                P.op("dve", lambda e: e.tensor_tensor(out=rt1[r, :], in0=tq[r, :], in1=cs[r, :], op=ALU.mult),
                     reads=[bq, "cs"], writes=["rt1"])
                P.op("dve", lambda e: e.tensor_tensor(out=rt2[r, :], in0=tqr[r, :], in1=sn[r, :], op=ALU.mult),
                     reads=[bqr, "sn"], writes=["rt2"])
                P.op("dve", lambda e: e.tensor_tensor(out=QTt[r, h, :], in0=rt1[r, :], in1=rt2[r, :], op=ALU.add),
                     reads=["rt1", "rt2"], writes=[("QTr", hb, h)])
            return f
        st.append(stage_q(0))
        st.append(stage_q(1))

        def stage_kv():
            bk = pm.next()
            tk = ps_m[bk[1]]

            def mmk(e):
                return e.matmul(out=tk[:, :], lhsT=Wkv[:, 0:128], rhs=ckvn[:], start=True, stop=True)
            P.op("pe", mmk, reads=["Wkv", "ckvn"], writes=[bk])
            P.op("act", lambda e: e.activation(out=KT[0:64, 0, col0:col0 + 512], in_=tk[0:64, :], func=AF.Copy),
                 reads=[bk], writes=[("KTn", 0, t)])
            bk2 = pm.next()
            tk2 = ps_m[bk2[1]]

            def mmk2(e):
                return e.matmul(out=tk2[0:64, :], lhsT=Wkv[:, 64:128], rhs=ckvn[:], start=True, stop=True)
            P.op("pe", mmk2, reads=["Wkv", "ckvn"], writes=[bk2])
            P.op("act", lambda e: e.activation(out=KT[0:64, 1, col0:col0 + 512], in_=tk2[0:64, :], func=AF.Copy),
                 reads=[bk2], writes=[("KTn", 1, t)])
            bv = pm.next()
            tv = ps_m[bv[1]]

            def mmv(e):
                ins = None
                for s in range(4):
                    ins = e.matmul(out=tv[:, s * 128:(s + 1) * 128], lhsT=ckvn[:, s * 128:(s + 1) * 128], rhs=Wkv[:, 128:256],
                                   start=True, stop=True)
                return ins
            P.op("pe", mmv, reads=["Wkv", "ckvn"], writes=[bv])
            P.op("dve", lambda e: e.tensor_copy(out=Vaug[:, 4 * t:4 * t + 4, :, 0:64],
                                                 in_=tv[:].rearrange("p (s h d) -> p s h d", s=4, h=2)),
                 reads=[bv], writes=[("V", t)])
        st.append(stage_kv)
        return st

    deferred = []

    def attention(t, pending):
        hb = t % 2
        QTt = QT[hb]
        nblk = 4 * t + 4
        blocks = [(h, kb) for h in range(2) for kb in range(nblk)]
        total = len(blocks)
        npend = len(pending)
        sbank = {}

        def emit_S(i):
            h, kb = blocks[i]
            sbk = sr.next()
            sbank[i] = sbk
            tS = ps_s[sbk]
            kt = kb // 4
            P.op("pe", lambda e, kb=kb, tS=tS, h=h: e.matmul(out=tS[:], lhsT=KT[0:96, h, kb * 128:(kb + 1) * 128], rhs=QTt[0:96, h, :],
                                                              start=True, stop=True),
                 reads=[("KTn", h, kt), ("KTr", h, kt), ("QTn", hb, h), ("QTr", hb, h)], writes=[("ps_s", sbk)])

        for d_ in deferred:
            d_[0] = 3
        LA = 2
        for i0 in range(min(LA, total)):
            emit_S(i0)
        for i in range(total):
            h, kb = blocks[i]
            ab = h
            acc = ps_acc[ab]
            kt = kb // 4
            if i + LA < total:
                emit_S(i + LA)
            sbk = sbank.pop(i)
            tS = ps_s[sbk]
            pb = pr.next()
            P.op("act", lambda e, tS=tS, pb=pb: e.activation(out=pT[pb][:], in_=tS[:], func=AF.Exp, scale=SCALE),
                 reads=[("ps_s", sbk)], writes=[("pT", pb)])
            if kb >= 4 * t:
                j = kb - 4 * t
                P.op("pool", lambda e, pb=pb, j=j: e.tensor_tensor(out=pT[pb][:], in0=pT[pb][:], in1=masks[:, j, :], op=ALU.mult),
                     reads=[("pT", pb), "masks"], writes=[("pT", pb)])
            P.op("pe", lambda e, kb=kb, pb=pb, h=h, acc=acc: e.matmul(out=acc[0:65, :], lhsT=Vaug[:, kb, h, :], rhs=pT[pb][:],
                                                                       start=(kb == 0), stop=(kb == nblk - 1)),
                 reads=[("pT", pb), ("V", kt), "vones"], writes=[("ps_acc", ab)])
            while pending and (i + 1) * npend >= (npend - len(pending) + 1) * total:
                pending.pop(0)()
            if kb == nblk - 1:
                P.op("dve", lambda e, acc=acc, ab=ab: e.reciprocal(out=o_sb[ab][64:65, :], in_=acc[64:65, :]),
                     reads=[("ps_acc", ab)], writes=[("rrow", ab)])
                P.op("act", lambda e, acc=acc, ab=ab: e.activation(out=o_sb[ab][0:64, :], in_=acc[0:64, :], func=AF.Copy),
                     reads=[("ps_acc", ab)], writes=[("o_sb", ab)])
                def fin_back(ab=ab, h=h, t=t):
                    bb = pm.next()
                    tb = ps_m[bb[1]]
                    P.op("pe", lambda e, ab=ab, tb=tb: e.matmul(out=tb[0:64, :], lhsT=ones32[64:65, 0:64], rhs=o_sb[ab][64:65, :],
                                                                 start=True, stop=True),
                         reads=[("rrow", ab), "ones32"], writes=[bb])
                    P.op("dve", lambda e, ab=ab, tb=tb: e.tensor_tensor(out=yT[ab][0:64, :], in0=o_sb[ab][0:64, :], in1=tb[0:64, :], op=ALU.mult),
                         reads=[("o_sb", ab), bb], writes=[("yT", ab)])
                    if "yatt_pieces" in io:
                        ydst = io["yatt_pieces"][t // 4][h * 64:(h + 1) * 64, (t % 4) * 512:(t % 4 + 1) * 512]
                    else:
                        ydst = io["yatt"][h * 64:(h + 1) * 64, t * 512:(t + 1) * 512]
                    P.dma("sp", ydst, yT[ab][0:64, :], reads=[("yT", ab)], writes=[("yatt", h, t)])
                deferred.append([i + 4, fin_back])
            while deferred and deferred[0][0] <= i:
                deferred.pop(0)[1]()
        while pending:
            pending.pop(0)()

    load_x(0)
    if NT > 1:
        load_x(1)
    for f in prep_stages(0):
        f()
    for t in range(NT):
        pending = prep_stages(t + 1) if t + 1 < NT else []
        if t + 2 < NT:
            load_x(t + 2)
        if tick is not None:
            tick(t)
        attention(t, pending)
        if t == NT - 1 or after is not None and t % 4 == 3:
            while deferred:
                deferred.pop(0)[1]()
        if after is not None:
            after(t)
    return [("yatt", h, t) for h in range(2) for t in range(NT)]


def p1_host_inputs(core, inp, NT):
    b = core // 4
    hp = core % 4
    heads = (2 * hp, 2 * hp + 1)
    ident = np.eye(128, dtype=np.float32).astype(NPBF)
    k = np.arange(128)[:, None, None]
    j = np.arange(4)[None, :, None]
    q = np.arange(512)[None, None, :]
    masks = (q >= 128 * j + k).astype(np.float32).astype(NPBF)
    wq = np.concatenate([inp["ev_w_q_up"][0][:, h * 96:(h + 1) * 96] for h in heads], axis=1)
    wkvu = inp["ev_w_kv_up"][0]
    wkv = np.concatenate([wkvu[:, h * 128:h * 128 + 64] for h in heads] + [wkvu[:, h * 128 + 64:h * 128 + 128] for h in heads], axis=1)
    inv = (10000.0 ** (-np.arange(0, 32, 2, dtype=np.float32) / 32)).astype(np.float32)
    invf = np.zeros((128, 1), np.float32)
    invf[64:80, 0] = inv
    invf[80:96, 0] = inv
    pos = np.ascontiguousarray(np.broadcast_to(inp["positions"][b][None, :NT * 512], (32, NT * 512))).astype(np.int32)
    return {
        "x": np.ascontiguousarray(inp["x"][b][:NT * 512]),
        "pos": pos,
        "ident": ident, "masks": np.ascontiguousarray(masks),
        "w_in": np.ascontiguousarray(inp["ev_w_in"][0]),
        "wq": np.ascontiguousarray(wq), "wkv": np.ascontiguousarray(wkv),
        "ev_normT": np.ascontiguousarray(inp["ev_norm"][0].reshape(8, 128).T),
        "q_normT": np.ascontiguousarray(inp["ev_q_norm"][0].reshape(2, 128).T),
        "kv_normT": np.ascontiguousarray(inp["ev_kv_norm"][0].reshape(1, 128).T),
        "invf": invf,
    }


def build_p1_program(NT):
    nc = bass.Bass("TRN2", target_bir_lowering=False)
    dt = lambda name, shape, dtype, kind: nc.dram_tensor(name, shape, dtype, kind=kind).ap()
    io = {
        "x": dt("x", [NT * 512, D], F32, "ExternalInput"),
        "pos": dt("pos", [32, NT * 512], I32, "ExternalInput"),
        "ident": dt("ident", [128, 128], BF16, "ExternalInput"),
        "masks": dt("masks", [128, 4, 512], BF16, "ExternalInput"),
        "w_in": dt("w_in", [D, 928], F32, "ExternalInput"),
        "wq": dt("wq", [256, 192], F32, "ExternalInput"),
        "wkv": dt("wkv", [128, 256], F32, "ExternalInput"),
        "ev_normT": dt("ev_normT", [128, 8], F32, "ExternalInput"),
        "q_normT": dt("q_normT", [128, 2], F32, "ExternalInput"),
        "kv_normT": dt("kv_normT", [128, 1], F32, "ExternalInput"),
        "invf": dt("invf", [128, 1], F32, "ExternalInput"),
        "yatt": dt("yatt", [128, NT * 512], BF16, "ExternalOutput"),
    }
    with ExitStack() as es:
        P = Prog(nc, es)
        outs = build_p1(nc, P, es, io, NT)
        toks = [P.res[k][0] for k in outs]
        P.wait_all("sp", toks)
        P.emit()
    return nc


D_FF = 2816
NFC = 22


class Layers:
    def __init__(self, nc, P, es, io, NTL, mode="A1"):
        self.nc, self.P, self.es, self.io, self.NTL = nc, P, es, io, NTL
        self.mode = mode
        sb = lambda name, shape, dt: es.enter_context(nc.sbuf_tensor("sl_" + name, shape, dt))
        psf = lambda name, shape, dt: es.enter_context(nc.psum_tensor("pl_" + name, shape, dt))
        self.sb = sb
        self.pp = [psf("b%d" % i, [128, 512], F32) for i in range(8)]
        self.prot = Rot(range(8))
        self.wslot = [sb("wslot%d" % i, [128, 8192], BF16) for i in range(3)]
        self.wrot = Rot(range(3))
        self.identb = sb("identb", [128, 128], BF16)
        self.onesb = sb("onesb", [128, 128], BF16)
        self.F_X = sb("F_X", [128, 8, 512], F32)
        self.BF_A = sb("BF_A", [128, 8, 512], BF16)
        self.BF_B = sb("BF_B", [128, 8, 512], BF16)
        self.BF_C = sb("BF_C", [128, 8, 512], BF16)
        self.BF_D = sb("BF_D", [128, NFC if mode != "A2" else 2, 512], BF16)
        self.F_G = sb("F_G", [128, 4224], F32)
        self.F_T2 = sb("F_T2", [128, 2112], F32)
        self.rstd = sb("rstd", [128, 512], F32)
        self.rec = sb("rec", [128, 512], F32)
        self.sg = [sb("sg%d" % i, [128, 512], F32) for i in range(2)]
        self.PmT = [sb("PmT%d" % i, [128, 512], BF16) for i in range(4)]
        self.tmp = [sb("tmp%d" % i, [128, 512], F32) for i in range(6)]
        if mode != "A2":
            self.KmT = sb("KmT", [128, 4, 2, 256], BF16)
            self.Vm = sb("Vm", [128, 2, 1024], BF16)
        self.memtm = self.F_G[:, 0:1024]
        self.memn = self.BF_D[:, 0:2, :].rearrange("p a n -> p (a n)")
        self.mT = self.BF_A[:, 0:4, :].rearrange("p a n -> p (a n)").rearrange("p (j m) -> p j m", j=8)
        self.small = sb("small", [128, 64], F32)
        self.UH = sb("UH", [128, 4, 16], F32)
        self.gains = {}
        self.fused = False
        P.dma("pool", self.identb[:], io["ident"], writes=["identb"])
        P.op("pool", lambda e: e.memset(self.onesb[:], 1.0), writes=["onesbL"])
        self.scr = {}

    def param(self, name, n):
        if name in self.gains:
            return self.gains[name]
        t = self.sb("p_" + name, [128, n], F32)
        self.P.dma("pool", t[:], self.io[name], writes=[("param", name)])
        self.gains[name] = t
        return t

    def prep_weight(self, name, src, K, N):
        dst = self.nc.dram_tensor("wbf_" + name, [K, N], BF16).ap()
        rows = 128
        for r0 in range(0, K, rows):
            self.P.dma("pool", dst[r0:r0 + rows, :], src[r0:r0 + rows, :], writes=[("wscr", name, r0 // rows)])
        self.scr[name] = (dst, K, N)
        return dst

    def prep_weight_gu(self, name, src):
        dst = self.nc.dram_tensor("wbf_" + name, [1024, 2 * D_FF], BF16).ap()
        dv = dst.rearrange("k (c two n) -> k c two n", two=2, n=128)
        sv = src.rearrange("k (two c n) -> k two c n", two=2, n=128)
        for r0 in range(0, 1024, 128):
            for two in range(2):
                self.P.dma("pool", dv[r0:r0 + 128, :, two, :], sv[r0:r0 + 128, two, :, :], writes=[("wscr", name, (r0 // 128) * 2 + two)])
        self.scr[name] = (dst, 1024, 2 * D_FF)
        self.scr_keys = getattr(self, "scr_keys", {})
        self.scr_keys[name] = [("wscr", name, i) for i in range(16)]
        return dst

    def wres(self, name):
        if name in getattr(self, "scr_keys", {}):
            return self.scr_keys[name]
        dst, K, N = self.scr[name]
        return [("wscr", name, i) for i in range(K // 128)]

    def load_w(self, parts):
        sl = self.wrot.next()
        dstb, view, rd = parts
        self.P.dma("sp", dstb(self.wslot[sl]), view, reads=rd, writes=[("wslot", sl)])
        return sl

    @staticmethod
    def simple(n, nk, view, rd):
        return (lambda slot: slot[:, 0:nk * n].rearrange("p (k n) -> p k n", k=nk), view, rd)

    def wview(self, sl, c0, n, nk):
        return self.wslot[sl][:, c0:c0 + nk * n].rearrange("p (k n) -> p k n", k=nk)

    def set_prefetch(self, which):
        if which == "F_G":
            self.pf_buf = self.F_G[:, 0:4096].rearrange("p (c n) -> p c n", c=8)
            self.pf_keys = [("U", g) for g in range(4)] + [("T1", g) for g in range(4)] + [("XC", c) for c in range(8)]
        else:
            self.pf_buf = self.BF_D[:, 0:16, :].rearrange("p a n -> p (a n)").bitcast(F32).rearrange("p (c n) -> p c n", c=8)
            self.pf_keys = [("BF_D", i) for i in range(16)]

    def prefetch_x(self, view, t, extra_reads=()):
        cols = slice(t * 512, (t + 1) * 512)
        for q2 in range(2):
            self.P.dma("pool", self.pf_buf[:, 4 * q2:4 * q2 + 4, :], view[:, 4 * q2:4 * q2 + 4, cols], reads=list(extra_reads),
                       writes=self.pf_keys if q2 == 0 else [("pf_hi",)])

    def take_prefetched(self, Xr):
        engs = ["act", "dve", "pool", "dve"]
        for j in range(8):
            eng = engs[j % 4]
            rd = self.pf_keys + [("pf_hi",)]
            if eng == "act":
                self.P.op("act", lambda e, j=j: e.activation(out=self.F_X[:, j, :], in_=self.pf_buf[:, j, :], func=AF.Copy), reads=rd, writes=[Xr[j]], strict=True)
            else:
                self.P.op(eng, lambda e, j=j: e.tensor_copy(out=self.F_X[:, j, :], in_=self.pf_buf[:, j, :]), reads=rd, writes=[Xr[j]], strict=True)

    def bank(self):
        i = self.prot.next()
        return ("pl", i), self.pp[i]

    def proj(self, wv, wreads, rhs, rhs_reads, n_oc, evac, nk=8, N=512, oc_cols=None):
        P = self.P
        for oc in range(n_oc):
            bk, tile = self.bank()
            c0 = oc * 128 if oc_cols is None else oc_cols(oc)

            def mm(e, tile=tile, c0=c0):
                ins = None
                for k in range(nk):
                    ins = e.matmul(out=tile[:, 0:N], lhsT=wv[:, k, c0:c0 + 128], rhs=rhs(k), start=(k == 0), stop=(k == nk - 1))
                return ins
            P.op("pe", mm, reads=list(wreads) + list(rhs_reads), writes=[bk])
            evac(oc, bk, tile)

    def mem_kv(self, layer, wkv_name):
        P, io = self.P, self.io
        g = self.param("xa_norm_mem%d" % layer, 8)
        sm = self.small
        for mc in range(2):
            P.dma("pool", self.memtm, io["mem"][mc * 128:(mc + 1) * 128, :], writes=["memtm"])
            P.op("act", lambda e: e.activation(out=self.memn, in_=self.memtm, func=AF.Square, accum_out=sm[:, 0:1]),
                 reads=["memtm"], writes=["memn", "sm0"])
            P.op("act", lambda e: e.activation(out=sm[:, 1:2], in_=sm[:, 0:1], func=AF.Ln, scale=1.0 / D, bias=EPS),
                 reads=["sm0"], writes=["sm1"])
            P.op("act", lambda e: e.activation(out=sm[:, 1:2], in_=sm[:, 1:2], func=AF.Exp, scale=-0.5),
                 reads=["sm1"], writes=["sm1"])
            P.op("dve", lambda e: e.tensor_scalar(out=self.memn, in0=self.memtm, scalar1=sm[:, 1:2], scalar2=None, op0=ALU.mult),
                 reads=["memtm", "sm1", "memn"], writes=["memn"])
            bk, tile = self.bank()
            ptile = tile[:].bitcast(BF16).rearrange("p (j n) -> p j n", j=8)

            def tr(e, ptile=ptile):
                ins = None
                for j in range(8):
                    ins = e.transpose(out=ptile[:, j, :], in_=self.memn[:, j * 128:(j + 1) * 128], identity=self.identb[:])
                return ins
            P.op("pe", tr, reads=["memn", "identb"], writes=[bk])
            for j in range(8):
                P.op("dve", lambda e, j=j, mc=mc, ptile=ptile: e.tensor_scalar(out=self.mT[:, j, mc * 128:(mc + 1) * 128], in0=ptile[:, j, :],
                                                                              scalar1=g[:, j:j + 1], scalar2=None, op0=ALU.mult),
                     reads=[bk, ("param", "xa_norm_mem%d" % layer)], writes=[("mT", j, mc)])
        mTr = [("mT", j, mc) for j in range(8) for mc in range(2)]
        dst, K, N = self.scr[wkv_name]
        wr = self.wres(wkv_name)
        sl = self.load_w(self.simple(1024, 8, dst.rearrange("(k p) n -> p k n", p=128)[:, :, 0:1024], wr))
        wv = self.wview(sl, 0, 1024, 8)
        for hd in range(4):
            for dc in range(2):
                bk, tile = self.bank()
                c0 = hd * 256 + dc * 128

                def mm(e, tile=tile, c0=c0):
                    ins = None
                    for k in range(8):
                        ins = e.matmul(out=tile[:, 0:256], lhsT=wv[:, k, c0:c0 + 128], rhs=self.mT[:, k, :], start=(k == 0), stop=(k == 7))
                    return ins
                P.op("pe", mm, reads=[("wslot", sl)] + mTr, writes=[bk])
                P.op("act", lambda e, tile=tile, hd=hd, dc=dc: e.activation(out=self.KmT[:, hd, dc, :], in_=tile[:, 0:256], func=AF.Copy),
                     reads=[bk], writes=[("KmT", hd, dc)])
        sl = self.load_w(self.simple(1024, 8, dst.rearrange("(k p) n -> p k n", p=128)[:, :, 1024:2048], wr))
        wv2 = self.wview(sl, 0, 1024, 8)
        for mc in range(2):
            for ch in range(2):
                bk, tile = self.bank()

                def mm(e, tile=tile, mc=mc, ch=ch):
                    ins = None
                    for k in range(8):
                        ins = e.matmul(out=tile[:, :], lhsT=self.mT[:, k, mc * 128:(mc + 1) * 128], rhs=wv2[:, k, ch * 512:(ch + 1) * 512],
                                       start=(k == 0), stop=(k == 7))
                    return ins
                P.op("pe", mm, reads=[("wslot", sl)] + mTr, writes=[bk])
                P.op("act", lambda e, tile=tile, mc=mc, ch=ch: e.activation(out=self.Vm[:, mc, ch * 512:(ch + 1) * 512], in_=tile[:, :], func=AF.Copy),
                     reads=[bk], writes=[("Vm", mc, ch)])

    def xattn_steps(self, layer, steps):
        P = self.P
        X = self.F_X
        Xr = ["F_X%d" % j for j in range(8)]
        wq, wo = "xa_w_q%d" % layer, "xa_w_o%d" % layer
        KmTr = [("KmT", hd, dc) for hd in range(4) for dc in range(2)]
        Vmr = [("Vm", mc, ch) for mc in range(2) for ch in range(2)]

        def step_q(sl):
            hres = self.norm_x("xa_norm_x%d" % layer)
            wv = self.wview(sl, 0, 1024, 8)

            def evac(oc, bk, tile):
                P.op("act", lambda e: e.activation(out=self.BF_C[:, oc, :], in_=tile[:], func=AF.Copy), reads=[bk], writes=[("BF_C", oc)])
            self.proj(wv, [("wslot", sl)], lambda k: self.BF_A[:, k, :], hres, 8, evac)
            def emit_S(hd):
                pms = []
                for mc in range(2):
                    bk, tile = self.bank()

                    def mm(e, tile=tile, hd=hd, mc=mc):
                        ins = None
                        for dc in range(2):
                            ins = e.matmul(out=tile[:], lhsT=self.KmT[:, hd, dc, mc * 128:(mc + 1) * 128], rhs=self.BF_C[:, hd * 2 + dc, :],
                                           start=(dc == 0), stop=(dc == 1))
                        return ins
                    P.op("pe", mm, reads=KmTr + [("BF_C", hd * 2), ("BF_C", hd * 2 + 1)], writes=[bk])
                    pi = (hd * 2 + mc) % 4
                    P.op("act", lambda e, tile=tile, pi=pi: e.activation(out=self.PmT[pi][:], in_=tile[:], func=AF.Exp, scale=1.0 / 16.0),
                         reads=[bk], writes=[("PmT", pi)])
                    pms.append(pi)
                return pms

            def emit_rest(hd, pms):
                bkd, tden = self.bank()

                def mmd(e, tden=tden, pms=pms):
                    ins = None
                    for mc in range(2):
                        ins = e.matmul(out=tden[:], lhsT=self.onesb[:], rhs=self.PmT[pms[mc]][:], start=(mc == 0), stop=(mc == 1))
                    return ins
                P.op("pe", mmd, reads=[("PmT", p) for p in pms] + ["onesbL"], writes=[bkd])
                P.op("act", lambda e, tden=tden: e.activation(out=self.rec[:], in_=tden[:], func=AF.Ln), reads=[bkd], writes=["rec"])
                P.op("act", lambda e: e.activation(out=self.rec[:], in_=self.rec[:], func=AF.Exp, scale=-1.0), reads=["rec"], writes=["rec"])
                for dc in range(2):
                    bk, tile = self.bank()

                    def mmo(e, tile=tile, hd=hd, dc=dc, pms=pms):
                        ins = None
                        for mc in range(2):
                            ins = e.matmul(out=tile[:], lhsT=self.Vm[:, mc, hd * 256 + dc * 128:hd * 256 + dc * 128 + 128],
                                           rhs=self.PmT[pms[mc]][:], start=(mc == 0), stop=(mc == 1))
                        return ins
                    P.op("pe", mmo, reads=Vmr + [("PmT", p) for p in pms], writes=[bk])
                    P.op("dve", lambda e, tile=tile, hd=hd, dc=dc: e.tensor_tensor(out=self.BF_B[:, hd * 2 + dc, :], in0=tile[:], in1=self.rec[:], op=ALU.mult),
                         reads=[bk, "rec"], writes=[("BF_B", hd * 2 + dc)])
            cur = emit_S(0)
            for hd in range(4):
                nxt = emit_S(hd + 1) if hd + 1 < 4 else None
                emit_rest(hd, cur)
                cur = nxt
        dq, _, _ = self.scr[wq]
        steps.append((self.simple(1024, 8, dq.rearrange("(k p) n -> p k n", p=128), self.wres(wq)), step_q))

        def step_o(sl):
            wv = self.wview(sl, 0, 1024, 8)

            def evac(oc, bk, tile):
                P.op("dve", lambda e: e.tensor_tensor(out=X[:, oc, :], in0=tile[:], in1=X[:, oc, :], op=ALU.add),
                     reads=[bk, Xr[oc]], writes=[Xr[oc]])
            self.proj(wv, [("wslot", sl)], lambda k: self.BF_B[:, k, :], [("BF_B", k) for k in range(8)], 8, evac)
        do, _, _ = self.scr[wo]
        steps.append((self.simple(1024, 8, do.rearrange("(k p) n -> p k n", p=128), self.wres(wo)), step_o))

    def norm_x(self, gname):
        Xr = ["F_X%d" % j for j in range(8)]
        return self.norm_multi(self.F_X, Xr, gname, self.BF_A, "BF_A")

    def norm_multi(self, src, src_rs, gname, dst, dst_res, N=512):
        P = self.P
        g = self.param(gname, 8)
        sq = self.BF_B
        for j in range(8):
            if j % 2 == 0:
                P.op("act", lambda e, j=j: e.activation(out=sq[:, j, 0:N], in_=src[:, j, 0:N], func=AF.Square),
                     reads=[src_rs[j]], writes=[("BF_B", j)])
            else:
                P.op("pool", lambda e, j=j: e.tensor_tensor(out=sq[:, j, 0:N], in0=src[:, j, 0:N], in1=src[:, j, 0:N], op=ALU.mult),
                     reads=[src_rs[j]], writes=[("BF_B", j)])
        bk, tile = self.bank()

        def mm(e):
            ins = None
            for j in range(8):
                ins = e.matmul(out=tile[:, 0:N], lhsT=self.onesb[:], rhs=sq[:, j, 0:N], start=(j == 0), stop=(j == 7))
            return ins
        P.op("pe", mm, reads=[("BF_B", j) for j in range(8)] + ["onesbL"], writes=[bk])
        P.op("act", lambda e: e.activation(out=self.rstd[:, 0:N], in_=tile[:, 0:N], func=AF.Ln, scale=1.0 / D, bias=EPS),
             reads=[bk], writes=["rstd"])
        P.op("act", lambda e: e.activation(out=self.rstd[:, 0:N], in_=self.rstd[:, 0:N], func=AF.Exp, scale=-0.5),
             reads=["rstd"], writes=["rstd"])
        for j in range(8):
            P.op("dve", lambda e, j=j: e.scalar_tensor_tensor(out=dst[:, j, 0:N], in0=src[:, j, 0:N], scalar=g[:, j:j + 1],
                                                               in1=self.rstd[:, 0:N], op0=ALU.mult, op1=ALU.mult),
                 reads=[src_rs[j], "rstd", ("param", gname)], writes=[(dst_res, j)])
        return [(dst_res, j) for j in range(8)]

    def ffn_steps(self, layer, steps):
        P = self.P
        X = self.F_X
        Xr = ["F_X%d" % j for j in range(8)]
        wgu, wd = "ffn_gu%d" % layer, "ffn_d%d" % layer
        dgu, _, _ = self.scr[wgu]
        dd, _, _ = self.scr[wd]
        vgu = dgu.rearrange("(k p) n -> p k n", p=128)
        nblocks = (NFC + 3) // 4
        hres_box = {}
        for blk in range(nblocks):
            c_lo = blk * 4
            nch = min(4, NFC - c_lo)

            def step(sl, blk=blk, c_lo=c_lo, nch=nch):
                if blk == 0:
                    hres_box["h"] = self.norm_x("ffn_norm%d" % layer)
                hres = hres_box["h"]
                wgu_v = self.wview(sl, 0, nch * 256, 8)
                for ci in range(nch):
                    ch = c_lo + ci
                    bg, tg = self.bank()
                    bu, tu = self.bank()

                    def mmg(e, tg=tg, ci=ci):
                        ins = None
                        for k in range(8):
                            ins = e.matmul(out=tg[:], lhsT=wgu_v[:, k, ci * 256:ci * 256 + 128], rhs=self.BF_A[:, k, :], start=(k == 0), stop=(k == 7))
                        return ins

                    def mmu(e, tu=tu, ci=ci):
                        ins = None
                        for k in range(8):
                            ins = e.matmul(out=tu[:], lhsT=wgu_v[:, k, ci * 256 + 128:ci * 256 + 256], rhs=self.BF_A[:, k, :], start=(k == 0), stop=(k == 7))
                        return ins
                    P.op("pe", mmg, reads=[("wslot", sl)] + hres, writes=[bg])
                    P.op("pe", mmu, reads=[("wslot", sl)] + hres, writes=[bu])
                    si = ch % 2
                    P.op("act", lambda e, tg=tg, si=si: e.activation(out=self.sg[si][:], in_=tg[:], func=AF.Silu), reads=[bg], writes=[("sg", si)])
                    P.op("dve", lambda e, tu=tu, si=si, ch=ch: e.tensor_tensor(out=self.BF_D[:, ch, :], in0=tu[:], in1=self.sg[si][:], op=ALU.mult),
                         reads=[bu, ("sg", si)], writes=[("BF_D", ch)])
            parts = self.simple(nch * 256, 8, vgu[:, :, c_lo * 256:(c_lo + nch) * 256], self.wres(wgu))
            steps.append((parts, step))
        vd = dd.rearrange("(k p) n -> p k n", p=128)
        for ob in range(4):
            def stepd(sl, ob=ob):
                wv = self.wview(sl, 0, 256, NFC)

                def evac(oc, bk, tile):
                    o = ob * 2 + oc
                    P.op("dve", lambda e: e.tensor_tensor(out=X[:, o, :], in0=tile[:], in1=X[:, o, :], op=ALU.add),
                         reads=[bk, Xr[o]], writes=[Xr[o]])
                self.proj(wv, [("wslot", sl)], lambda k: self.BF_D[:, k, :], [("BF_D", k) for k in range(NFC)], 2, evac, nk=NFC)
            steps.append((self.simple(256, NFC, vd[:, :, ob * 256:(ob + 1) * 256], self.wres(wd)), stepd))

    def run_steps(self, steps):
        widx = [i for i, s_ in enumerate(steps) if s_[0] is not None]
        loaded = {}
        nxt = 0

        def ensure(n):
            nonlocal nxt
            while nxt < min(n, len(widx)):
                i = widx[nxt]
                loaded[i] = self.load_w(steps[i][0])
                nxt += 1
        wcount = 0
        for i, (parts, fn) in enumerate(steps):
            if parts is not None:
                ensure(wcount + 3)
                wcount += 1
                fn(loaded.pop(i))
            else:
                fn(None)


def a1_tile_steps(L, t, steps, first):
    P, io = L.P, L.io
    X = L.F_X
    Xr = ["F_X%d" % j for j in range(8)]
    U = L.F_G[:, 0:2112].rearrange("p (g n) -> p g n", g=4)
    T1 = L.F_G[:, 2112:4224].rearrange("p (g n) -> p g n", g=4)
    T2 = L.F_T2[:].rearrange("p (g n) -> p g n", g=4)
    dwin, _, _ = L.scr["ev_w_in"]
    vwin = dwin.rearrange("(k p) n -> p k n", p=128)[:, :, 0:512]
    win_parts = L.simple(512, 8, vwin, L.wres("ev_w_in"))
    cols = slice(t * 512, (t + 1) * 512)
    pscale = L.param("pool_scaleT", 4)

    if first:
        def step_halo(sl):
            XH = L.tmp[0][:, 0:128].rearrange("p (j n) -> p j n", j=8)
            P.dma("pool", XH, io["xhT"].rearrange("(j p) n -> p j n", p=128), writes=["XH"])
            hres = L.norm_multi(XH, ["XH"] * 8, "ev_normT", L.BF_A, "BF_A", N=16)
            wv = L.wview(sl, 0, 512, 8)

            def evac(oc, bk, tile):
                P.op("act", lambda e: e.activation(out=U[:, oc, 0:16], in_=tile[:, 0:16], func=AF.Copy), reads=[bk], writes=[("U", oc)])
            L.proj(wv, [("wslot", sl)], lambda k: L.BF_A[:, k, 0:16], hres, 4, evac, N=16)
        steps.append((win_parts, step_halo))

    def step_load(_):
        xv = io["xT"].rearrange("(j p) n -> p j n", p=128)
        if t == 0:
            for q4 in range(4):
                P.dma("pool", X[:, 2 * q4:2 * q4 + 2, :], xv[:, 2 * q4:2 * q4 + 2, cols], writes=Xr[2 * q4:2 * q4 + 2])
        else:
            L.take_prefetched(Xr)
        if not L.fused:
            P.dma("pool", L.BF_C[:, 4:8, :], io["yattT"].rearrange("(j p) n -> p j n", p=128)[:, :, cols], writes=[("BF_C", k) for k in range(4, 8)])
        else:
            sel1 = L.param("sel1", 4)
            for cc in range(4):
                g0 = cc * L.chunk + t * 512
                pi, off = g0 // 2048, g0 % 2048
                ya = io["yatt_all_pieces"][pi].rearrange("(j p) n -> p j n", p=128)
                P.dma("pool", L.BF_D[:, 4 * cc:4 * cc + 4, :], ya[:, :, off:off + 512],
                      reads=[("yatt_all", pi)], writes=[("BF_D", 4 * cc + i) for i in range(4)])
            P.op("dve", lambda e: e.tensor_scalar(out=L.BF_C[:, 4:8, :], in0=L.BF_D[:, 0:4, :], scalar1=sel1[:, 0:1], scalar2=None, op0=ALU.mult),
                 reads=[("BF_D", i) for i in range(4)] + [("param", "sel1")], writes=[("BF_C", k) for k in range(4, 8)])
            for cc in range(1, 4):
                P.op("dve", lambda e, cc=cc: e.scalar_tensor_tensor(out=L.BF_C[:, 4:8, :], in0=L.BF_D[:, 4 * cc:4 * cc + 4, :], scalar=sel1[:, cc:cc + 1],
                                                                     in1=L.BF_C[:, 4:8, :], op0=ALU.mult, op1=ALU.add),
                     reads=[("BF_D", 4 * cc + i) for i in range(4)] + [("param", "sel1")] + [("BF_C", k) for k in range(4, 8)],
                     writes=[("BF_C", k) for k in range(4, 8)])
        P.dma("pool", L.invc[:], io["invc"][:, :, cols], writes=["invc"])
    steps.append((None, step_load))

    def step_pool(sl):
        if t > 0:
            for g in range(4):
                P.op("pool", lambda e, g=g: e.tensor_copy(out=U[:, g, 0:16], in_=L.UH[:, g, :]), reads=[("UH", g), ("U", g)], writes=[("U", g)], strict=True)
        hres = L.norm_x("ev_normT")
        wv = L.wview(sl, 0, 512, 8)

        def evac(oc, bk, tile):
            P.op("act", lambda e: e.activation(out=U[:, oc, 16:528], in_=tile[:], func=AF.Copy), reads=[bk], writes=[("U", oc)])
        L.proj(wv, [("wslot", sl)], lambda k: L.BF_A[:, k, :], hres, 4, evac)
        for g in range(4):
            eng = "dve" if g < 2 else "pool"
            ur = ("U", g)
            t1r, t2r = ("T1", g), ("T2", g)
            P.op(eng, lambda e, g=g: e.tensor_tensor(out=T1[:, g, 1:528], in0=U[:, g, 1:528], in1=U[:, g, 0:527], op=ALU.add),
                 reads=[ur], writes=[t1r])
            R, rr = T1, t1r
            if g >= 1:
                P.op(eng, lambda e, g=g: e.tensor_tensor(out=T2[:, g, 3:528], in0=T1[:, g, 3:528], in1=T1[:, g, 1:526], op=ALU.add),
                     reads=[t1r], writes=[t2r])
                R, rr = T2, t2r
            if g >= 2:
                P.op(eng, lambda e, g=g: e.tensor_tensor(out=T1[:, g, 7:528], in0=T2[:, g, 7:528], in1=T2[:, g, 3:524], op=ALU.add),
                     reads=[t2r, t1r], writes=[t1r])
                R, rr = T1, t1r
            if g >= 3:
                P.op(eng, lambda e, g=g: e.tensor_tensor(out=T2[:, g, 15:528], in0=T1[:, g, 15:528], in1=T1[:, g, 7:520], op=ALU.add),
                     reads=[t1r, t2r], writes=[t2r])
                R, rr = T2, t2r
            P.op(eng, lambda e, g=g, R=R: e.tensor_tensor(out=R[:, g, 16:528], in0=R[:, g, 16:528], in1=L.invc[:, g, :], op=ALU.mult),
                 reads=[rr, "invc"], writes=[rr])
            P.op(eng, lambda e, g=g, R=R: e.tensor_tensor(out=L.BF_B[:, g, :], in0=R[:, g, 16:528], in1=U[:, g, 16:528], op=ALU.subtract),
                 reads=[rr, ur], writes=[("BF_B", g)])
            P.op("pool", lambda e, g=g: e.tensor_copy(out=L.UH[:, g, :], in_=U[:, g, 512:528]), reads=[ur], writes=[("UH", g)])
            bk, tile = L.bank()
            P.op("pe", lambda e, g=g, tile=tile: e.matmul(out=tile[:], lhsT=L.poolw[:, g, :], rhs=L.BF_B[:, g, :], start=True, stop=True),
                 reads=[("BF_B", g), "poolw"], writes=[bk])
            P.op("act", lambda e, g=g, tile=tile: e.activation(out=L.BF_C[:, g, :], in_=tile[:], func=AF.Copy, scale=pscale[:, g:g + 1]),
                 reads=[bk, ("param", "pool_scaleT")], writes=[("BF_C", g)])
    steps.append((win_parts, step_pool))

    def step_out(sl):
        wv = L.wview(sl, 0, 1024, 8)

        def evac(oc, bk, tile):
            P.op("dve", lambda e: e.tensor_tensor(out=X[:, oc, :], in0=tile[:], in1=X[:, oc, :], op=ALU.add),
                 reads=[bk, Xr[oc]], writes=[Xr[oc]])
        L.proj(wv, [("wslot", sl)], lambda k: L.BF_C[:, k, :], [("BF_C", k) for k in range(8)], 8, evac)
    dwo, _, _ = L.scr["ev_w_out"]
    steps.append((L.simple(1024, 8, dwo.rearrange("(k p) n -> p k n", p=128), L.wres("ev_w_out")), step_out))
    L.xattn_steps(0, steps)
    if t + 1 < L.NTL:
        steps.append((None, lambda _: L.prefetch_x(io["xT"].rearrange("(j p) n -> p j n", p=128), t + 1)))
    L.ffn_steps(0, steps)

    def step_store(_):
        P.dma("pool", io["x1T"].rearrange("(j p) n -> p j n", p=128)[:, :, cols], X[:], reads=Xr, writes=[("x1T", t)])
    steps.append((None, step_store))


def build_a1(nc, P, es, io, NTL):
    L = Layers(nc, P, es, io, NTL)
    L.invc = L.sb("invc", [128, 4, 512], F32)
    L.poolw = L.sb("poolw", [128, 4, 128], BF16)
    P.dma("pool", L.poolw[:], io["pool_w"].rearrange("g c d -> c g d"), writes=["poolw"])
    L.prep_weight("ev_w_in", io["ev_w_in"], 1024, 928)
    L.prep_weight("ev_w_out", io["ev_w_out"], 1024, 1024)
    L.prep_weight("xa_w_kv0", io["xa_w_kv0"], 1024, 2048)
    L.prep_weight("xa_w_q0", io["xa_w_q0"], 1024, 1024)
    L.prep_weight("xa_w_o0", io["xa_w_o0"], 1024, 1024)
    L.prep_weight_gu("ffn_gu0", io["ffn_gu0"])
    L.prep_weight("ffn_d0", io["ffn_d0"], D_FF, 1024)
    L.mem_kv(0, "xa_w_kv0")
    P.fence()
    L.set_prefetch("F_G")
    steps = []
    for t in range(NTL):
        a1_tile_steps(L, t, steps, first=(t == 0))
    L.run_steps(steps)
    return [("x1T", t) for t in range(NTL)]


def a1_host_inputs(core, inp, yattT_own, NTL):
    b, c = core // 4, core % 4
    n = NTL * 512
    t0 = c * TOK
    xT = np.ascontiguousarray(inp["x"][b][t0:t0 + n].T)
    if t0 == 0:
        xh = np.zeros((16, D), np.float32)
    else:
        xh = inp["x"][b][t0 - 16:t0]
    tg = np.arange(t0, t0 + n)
    invc = np.stack([1.0 / np.minimum(tg + 1, w) for w in (2, 4, 8, 16)]).astype(np.float32)
    T_ = lambda v, k: np.ascontiguousarray(np.asarray(v, np.float32).reshape(k, 128).T)
    return {
        "xT": xT, "xhT": np.ascontiguousarray(xh.T), "yattT": np.ascontiguousarray(yattT_own[:, :n]),
        "invc": np.ascontiguousarray(np.broadcast_to(invc[None], (128, 4, n))),
        "mem": np.ascontiguousarray(inp["mem"][b]),
        "ident": np.eye(128, dtype=np.float32).astype(NPBF),
        "ev_normT": T_(inp["ev_norm"][0], 8), "pool_scaleT": T_(inp["ev_pool_scale"][0], 4),
        "xa_norm_x0": T_(inp["xa_norm_x"][0], 8), "xa_norm_mem0": T_(inp["xa_norm_mem"][0], 8), "ffn_norm0": T_(inp["ffn_norm"][0], 8),
        "pool_w": np.ascontiguousarray(inp["ev_pool_w"][0]),
        "ev_w_in": np.ascontiguousarray(inp["ev_w_in"][0]), "ev_w_out": np.ascontiguousarray(inp["ev_w_out"][0]),
        "xa_w_q0": np.ascontiguousarray(inp["xa_w_q"][0]), "xa_w_kv0": np.ascontiguousarray(inp["xa_w_kv"][0]),
        "xa_w_o0": np.ascontiguousarray(inp["xa_w_o"][0]),
        "ffn_gu0": np.ascontiguousarray(inp["ffn_w_gate_up"][0]), "ffn_d0": np.ascontiguousarray(inp["ffn_w_down"][0]),
    }


def build_a1_program(NTL):
    nc = bass.Bass("TRN2", target_bir_lowering=False)
    n = NTL * 512
    dt = lambda name, shape, dtype, kind="ExternalInput": nc.dram_tensor(name, shape, dtype, kind=kind).ap()
    io = {
        "xT": dt("xT", [D, n], F32), "xhT": dt("xhT", [D, 16], F32), "yattT": dt("yattT", [512, n], BF16),
        "invc": dt("invc", [128, 4, n], F32), "mem": dt("mem", [256, D], F32), "ident": dt("ident", [128, 128], BF16),
        "ev_normT": dt("ev_normT", [128, 8], F32), "pool_scaleT": dt("pool_scaleT", [128, 4], F32),
        "xa_norm_x0": dt("xa_norm_x0", [128, 8], F32), "xa_norm_mem0": dt("xa_norm_mem0", [128, 8], F32),
        "ffn_norm0": dt("ffn_norm0", [128, 8], F32),
        "pool_w": dt("pool_w", [4, 128, 128], F32),
        "ev_w_in": dt("ev_w_in", [D, 928], F32), "ev_w_out": dt("ev_w_out", [D, D], F32),
        "xa_w_q0": dt("xa_w_q0", [D, D], F32), "xa_w_kv0": dt("xa_w_kv0", [D, 2 * D], F32), "xa_w_o0": dt("xa_w_o0", [D, D], F32),
        "ffn_gu0": dt("ffn_gu0", [D, 2 * D_FF], F32), "ffn_d0": dt("ffn_d0", [D_FF, D], F32),
        "x1T": dt("x1T", [D, n], F32, "ExternalOutput"),
    }
    with ExitStack() as es:
        P = Prog(nc, es)
        outs = build_a1(nc, P, es, io, NTL)
        P.wait_all("sp", [P.res[k][0] for k in outs])
        P.emit()
    return nc


def rglru_setup(L, mode):
    P, io = L.P, L.io
    sb = L.sb
    L.XB = sb("XB", [128, 8, 516], BF16)
    L.Dg = sb("Dg", [128, 4, 8, 128], BF16)
    L.Wr = sb("Wr", [128, 8, 256], BF16)
    L.Wi = sb("Wi", [128, 8, 256], BF16)
    L.ST = sb("ST", [128, 8], F32)
    L.STA = sb("STA", [128, 8], F32)
    L.cL = sb("cL", [128, 8], F32)
    L.posi = sb("posi", [128, 512], I32)
    L.nr = sb("nr", [128, 512], F32)
    L.omnr = sb("omnr", [128, 512], F32)
    L.zeros = sb("zeros", [128, 512], F32) if mode == "A2" else None
    convw = L.param("conv_wT", 32)
    lam = L.param("lamT", 8)
    P.dma("pool", L.Wr[:], io["w_rgate"].rearrange("h (cc p) d -> p (h cc) d", p=128), writes=["Wr"])
    P.dma("pool", L.Wi[:], io["w_igate"].rearrange("h (cc p) d -> p (h cc) d", p=128), writes=["Wi"])
    if mode == "A2":
        P.op("pool", lambda e: e.memset(L.zeros[:], 0.0), writes=["zeros"])
    P.op("pool", lambda e: e.memset(L.ST[:], 0.0), writes=["ST"])
    P.op("pool", lambda e: e.memset(L.STA[:], 1.0), writes=["STA"])
    for k in range(4):
        for c in range(8):
            P.op("dve", lambda e, k=k, c=c: e.tensor_scalar(out=L.Dg[:, k, c, :], in0=L.identb[:], scalar1=convw[:, k * 8 + c:k * 8 + c + 1],
                                                              scalar2=None, op0=ALU.mult),
                 reads=["identb", ("param", "conv_wT")], writes=[("Dg", k, c)])
    P.op("act", lambda e: e.activation(out=L.cL[:], in_=lam[:], func=AF.Exp, scale=-1.0), reads=[("param", "lamT")], writes=["cL"])
    P.op("act", lambda e: e.activation(out=L.cL[:], in_=L.cL[:], func=AF.Ln, bias=1.0), reads=["cL"], writes=["cL"])
    P.op("dve", lambda e: e.tensor_scalar(out=L.cL[:], in0=L.cL[:], scalar1=-8.0, scalar2=None, op0=ALU.mult), reads=["cL"], writes=["cL"])
    if mode != "F":
        L.prep_weight("od_w_in", io["od_w_in"], 1024, 2048)


def rglru_tile_steps(L, t, steps, mode, first):
    P, io = L.P, L.io
    X = L.F_X
    Xr = ["F_X%d" % j for j in range(8)]
    XC = L.F_G[:, 0:4096].rearrange("p (c n) -> p c n", c=8)
    cols = slice(t * 512, (t + 1) * 512)
    din, _, _ = L.scr["od_w_in"]
    vin = din.rearrange("(k p) n -> p k n", p=128)
    wr_in = L.wres("od_w_in")
    convb = L.param("conv_bT", 8)
    br = L.param("b_rT", 8)
    bi = L.param("b_iT", 8)
    Dgr = [("Dg", k, c) for k in range(4) for c in range(8)]
    tmp = L.tmp

    if first:
        def step_halo(sl):
            XH = L.sg[0][:, 0:32].rearrange("p (j n) -> p j n", j=8)
            if not L.fused:
                P.dma("pool", XH, io["x1hT"].rearrange("(j p) n -> p j n", p=128), writes=["XH"])
            else:
                selp = L.param("selp", 4)
                HA = L.sg[1][:, 0:128].rearrange("p (c m) -> p c m", c=4)
                P.dma("pool", HA, io["halo_all"].rearrange("(c p) m -> p c m", p=128), reads=["halo_all"], writes=["HA"])
                XHf = L.sg[0][:, 0:32]
                P.op("dve", lambda e: e.tensor_scalar(out=XHf, in0=HA[:, 0, :], scalar1=selp[:, 0:1], scalar2=None, op0=ALU.mult),
                     reads=["HA", ("param", "selp")], writes=["XH"])
                for cc in range(1, 4):
                    P.op("dve", lambda e, cc=cc: e.scalar_tensor_tensor(out=XHf, in0=HA[:, cc, :], scalar=selp[:, cc:cc + 1], in1=XHf, op0=ALU.mult, op1=ALU.add),
                         reads=["HA", "XH", ("param", "selp")], writes=["XH"])
            hres = L.norm_multi(XH, ["XH"] * 8, "od_normT", L.BF_A, "BF_A", N=4)
            wv = L.wview(sl, 0, 1024, 8)

            def evac(oc, bk, tile):
                P.op("act", lambda e: e.activation(out=L.XB[:, oc, 1:4], in_=tile[:, 0:3], func=AF.Copy), reads=[bk], writes=[("XB", oc)])
            L.proj(wv, [("wslot", sl)], lambda k: L.BF_A[:, k, 1:4], hres, 8, evac, N=3)
        steps.append((L.simple(1024, 8, vin[:, :, 1024:2048], wr_in), step_halo))

    def step_load(_):
        xv = io["x1T"].rearrange("(j p) n -> p j n", p=128)
        use_pf = not getattr(L, "pf_none", False)
        if t == 0 or not use_pf:
            for q4 in range(4):
                P.dma("pool", X[:, 2 * q4:2 * q4 + 2, :], xv[:, 2 * q4:2 * q4 + 2, cols], reads=[("x1T", t)], writes=Xr[2 * q4:2 * q4 + 2])
        else:
            L.take_prefetched(Xr)
        if mode == "A2" and use_pf and t + 1 < L.NTL:
            L.prefetch_x(xv, t + 1, extra_reads=[("x1T", t + 1)])
        P.dma("pool", L.posi[:], io["pos_rep"][:, cols], writes=["posi"])
        P.op("dve", lambda e: e.tensor_copy(out=L.nr[:], in_=L.posi[:]), reads=["posi"], writes=["nr"])
        P.op("dve", lambda e: e.tensor_scalar(out=L.nr[:], in0=L.nr[:], scalar1=0.0, scalar2=None, op0=ALU.not_equal), reads=["nr"], writes=["nr"])
        P.op("dve", lambda e: e.tensor_scalar(out=L.omnr[:], in0=L.nr[:], scalar1=-1.0, scalar2=1.0, op0=ALU.mult, op1=ALU.add),
             reads=["nr"], writes=["omnr"])
    steps.append((None, step_load))
    hbox = {}

    if mode == "B":
        def step_gate(sl):
            hbox["h"] = L.norm_x("od_normT")
            wv = L.wview(sl, 0, 1024, 8)

            def evac(oc, bk, tile):
                P.op("act", lambda e: e.activation(out=L.BF_D[:, oc, :], in_=tile[:], func=AF.Gelu_apprx_tanh), reads=[bk], writes=[("BF_D", oc)])
            L.proj(wv, [("wslot", sl)], lambda k: L.BF_A[:, k, :], hbox["h"], 8, evac)
        steps.append((L.simple(1024, 8, vin[:, :, 0:1024], wr_in), step_gate))

    def step_xb(sl):
        if mode != "B":
            hbox["h"] = L.norm_x("od_normT")
        wv = L.wview(sl, 0, 1024, 8)

        def evac(oc, bk, tile):
            P.op("act", lambda e: e.activation(out=L.XB[:, oc, 4:516], in_=tile[:], func=AF.Copy), reads=[bk], writes=[("XB", oc)])
        if DBG < 3:
            return
        L.proj(wv, [("wslot", sl)], lambda k: L.BF_A[:, k, :], hbox["h"], 8, evac)
        for c in range(8 if DBG >= 4 else 0):
            bk, tile = L.bank()

            def mmc(e, c=c, tile=tile):
                ins = None
                for k in range(4):
                    ins = e.matmul(out=tile[:], lhsT=L.Dg[:, k, c, :], rhs=L.XB[:, c, 1 + k:1 + k + 512], start=(k == 0), stop=(k == 3))
                return ins
            P.op("pe", mmc, reads=[("XB", c)] + Dgr, writes=[bk])
            P.op("act", lambda e, c=c, tile=tile: e.activation(out=XC[:, c, :], in_=tile[:], func=AF.Identity, bias=convb[:, c:c + 1]),
                 reads=[bk, ("param", "conv_bT")], writes=[("XC", c)])
            P.op("pool", lambda e, c=c: e.tensor_copy(out=L.BF_B[:, c, :], in_=XC[:, c, :]), reads=[("XC", c)], writes=[("BF_B", c)])
            P.op("pool", lambda e, c=c: e.tensor_copy(out=L.XB[:, c, 1:4], in_=L.XB[:, c, 513:516]), reads=[("XB", c)], writes=[("XB", c)])
        for oc in range(8 if DBG >= 5 else 0):
            hd = oc // 2
            o0 = (oc % 2) * 128
            tr_, ti_, ta_, tm_, tb_, th_ = [tmp[i] for i in range(6)]
            for (W, wn, bias, bn, dst, dn) in ((L.Wr, "Wr", br, "b_rT", tr_, "t0"), (L.Wi, "Wi", bi, "b_iT", ti_, "t1")):
                bk, tile = L.bank()

                def mmg(e, W=W, tile=tile, hd=hd, o0=o0):
                    ins = None
                    for kk in range(2):
                        kc = 2 * hd + kk
                        ins = e.matmul(out=tile[:], lhsT=W[:, kc, o0:o0 + 128], rhs=L.BF_B[:, kc, :], start=(kk == 0), stop=(kk == 1))
                    return ins
                P.op("pe", mmg, reads=[wn, ("BF_B", 2 * hd), ("BF_B", 2 * hd + 1)], writes=[bk])
                P.op("act", lambda e, tile=tile, bias=bias, dst=dst, oc=oc: e.activation(out=dst[:], in_=tile[:], func=AF.Sigmoid, bias=bias[:, oc:oc + 1]),
                     reads=[bk, ("param", bn)], writes=[dn])
            P.op("act", lambda e, oc=oc: e.activation(out=ta_[:], in_=tr_[:], func=AF.Exp, scale=L.cL[:, oc:oc + 1]), reads=["t0", "cL"], writes=["t2"])
            P.op("pool", lambda e: e.tensor_tensor(out=ta_[:], in0=ta_[:], in1=L.nr[:], op=ALU.mult), reads=["t2", "nr"], writes=["t2"])
            P.op("dve", lambda e: e.tensor_tensor(out=tm_[:], in0=ta_[:], in1=ta_[:], op=ALU.mult), reads=["t2"], writes=["t3"])
            P.op("act", lambda e: e.activation(out=tm_[:], in_=tm_[:], func=AF.Sqrt, scale=-1.0, bias=1.0), reads=["t3"], writes=["t3"])
            P.op("dve", lambda e, oc=oc: e.tensor_tensor(out=tb_[:], in0=ti_[:], in1=XC[:, oc, :], op=ALU.mult), reads=["t1", ("XC", oc)], writes=["t4"])
            P.op("pool", lambda e: e.tensor_tensor(out=tb_[:], in0=tb_[:], in1=tm_[:], op=ALU.mult), reads=["t4", "t3"], writes=["t4"])
            P.op("dve", lambda e, oc=oc: e.tensor_tensor_scan(out=th_[:], data0=ta_[:], data1=tb_[:], initial=L.ST[:, oc:oc + 1], op0=ALU.mult, op1=ALU.add),
                 reads=["t2", "t4", "ST"], writes=["t5"])
            P.op("dve", lambda e, oc=oc: e.tensor_copy(out=L.ST[:, oc:oc + 1], in_=th_[:, 511:512]), reads=["t5"], writes=["ST"])
            if mode == "A2":
                P.op("dve", lambda e, oc=oc: e.tensor_tensor_scan(out=tb_[:], data0=ta_[:], data1=L.zeros[:], initial=L.STA[:, oc:oc + 1], op0=ALU.mult, op1=ALU.add),
                     reads=["t2", "t4", "STA", "zeros"], writes=["t4"])
                P.op("dve", lambda e, oc=oc: e.tensor_copy(out=L.STA[:, oc:oc + 1], in_=tb_[:, 511:512]), reads=["t4"], writes=["STA"])
            else:
                P.op("pool", lambda e, oc=oc: e.tensor_tensor(out=L.BF_C[:, oc, :], in0=L.BF_D[:, oc, :], in1=th_[:], op=ALU.mult),
                     reads=[("BF_D", oc), "t5"], writes=[("BF_C", oc)])
    steps.append((L.simple(1024, 8, vin[:, :, 1024:2048], wr_in), step_xb))


def b_tile_steps(L, t, steps, first):
    P, io = L.P, L.io
    X = L.F_X
    Xr = ["F_X%d" % j for j in range(8)]
    cols = slice(t * 512, (t + 1) * 512)
    rglru_tile_steps(L, t, steps, "B", first)

    def step_out(sl):
        wv = L.wview(sl, 0, 1024, 8)

        def evac(oc, bk, tile):
            P.op("dve", lambda e: e.tensor_tensor(out=X[:, oc, :], in0=tile[:], in1=X[:, oc, :], op=ALU.add),
                 reads=[bk, Xr[oc]], writes=[Xr[oc]])
        L.proj(wv, [("wslot", sl)], lambda k: L.BF_C[:, k, :], [("BF_C", k) for k in range(8)], 8, evac)
    dwo, _, _ = L.scr["od_w_out"]
    steps.append((L.simple(1024, 8, dwo.rearrange("(k p) n -> p k n", p=128), L.wres("od_w_out")), step_out))
    L.xattn_steps(1, steps)
    if t + 1 < L.NTL:
        steps.append((None, lambda _: L.prefetch_x(io["x1T"].rearrange("(j p) n -> p j n", p=128), t + 1, extra_reads=[("x1T", t + 1)])))
    L.ffn_steps(1, steps)

    def step_final(_):
        g = L.param("final_normT", 8)
        OUT = L.BF_D[:, 0:16, :].rearrange("p a n -> p (a n)").bitcast(F32).rearrange("p (c n) -> p c n", c=8)
        okeys = [("BF_D", i) for i in range(16)]
        sq = L.BF_B
        for j in range(8):
            if j % 2 == 0:
                P.op("act", lambda e, j=j: e.activation(out=sq[:, j, :], in_=X[:, j, :], func=AF.Square), reads=[Xr[j]], writes=[("BF_B", j)])
            else:
                P.op("pool", lambda e, j=j: e.tensor_tensor(out=sq[:, j, :], in0=X[:, j, :], in1=X[:, j, :], op=ALU.mult), reads=[Xr[j]], writes=[("BF_B", j)])
        bk, tile = L.bank()

        def mm(e):
            ins = None
            for j in range(8):
                ins = e.matmul(out=tile[:], lhsT=L.onesb[:], rhs=sq[:, j, :], start=(j == 0), stop=(j == 7))
            return ins
        P.op("pe", mm, reads=[("BF_B", j) for j in range(8)] + ["onesbL"], writes=[bk])
        P.op("act", lambda e: e.activation(out=L.rstd[:], in_=tile[:], func=AF.Ln, scale=1.0 / D, bias=EPS), reads=[bk], writes=["rstd"])
        P.op("act", lambda e: e.activation(out=L.rstd[:], in_=L.rstd[:], func=AF.Exp, scale=-0.5), reads=["rstd"], writes=["rstd"])
        for j in range(8):
            P.op("dve", lambda e, j=j: e.scalar_tensor_tensor(out=OUT[:, j, :], in0=X[:, j, :], scalar=g[:, j:j + 1], in1=L.rstd[:],
                                                               op0=ALU.mult, op1=ALU.mult),
                 reads=[Xr[j], "rstd", ("param", "final_normT")], writes=okeys[2 * j:2 * j + 2])
        P.dma("pool", io["yT"].rearrange("(j p) n -> p j n", p=128)[:, :, cols], OUT, reads=okeys, writes=[("yT", t)])
    steps.append((None, step_final))


def build_a2(nc, P, es, io, NTL):
    L = Layers(nc, P, es, io, NTL, "A2")
    rglru_setup(L, "A2")
    L.set_prefetch("F_G")
    L.pf_none = True
    steps = []
    for t in range(NTL if DBG >= 1 else 0):
        rglru_tile_steps(L, t, steps, "A2", first=(t == 0))
    L.run_steps(steps)
    P.dma("pool", io["summA"], L.STA[:], reads=["STA"], writes=["summA"])
    P.dma("pool", io["summH"], L.ST[:], reads=["ST"], writes=["summH"])
    return ["summA", "summH"]


def build_b(nc, P, es, io, NTL):
    L = Layers(nc, P, es, io, NTL, "B")
    rglru_setup(L, "B")
    L.prep_weight("od_w_out", io["od_w_out"], 1024, 1024)
    L.prep_weight("xa_w_kv1", io["xa_w_kv1"], 1024, 2048)
    L.prep_weight("xa_w_q1", io["xa_w_q1"], 1024, 1024)
    L.prep_weight("xa_w_o1", io["xa_w_o1"], 1024, 1024)
    L.prep_weight_gu("ffn_gu1", io["ffn_gu1"])
    L.prep_weight("ffn_d1", io["ffn_d1"], D_FF, 1024)
    sA = L.sb("sA", [128, 4, 8], F32)
    sH = L.sb("sH", [128, 4, 8], F32)
    sel = L.param("sel", 4)
    P.dma("pool", sA[:], io["sumA_all"], writes=["sA"])
    P.dma("pool", sH[:], io["sumH_all"], writes=["sH"])
    t8 = L.small
    for cc in range(4):
        P.op("dve", lambda e, cc=cc: e.tensor_tensor(out=t8[:, 16:24], in0=sA[:, cc, :], in1=L.ST[:], op=ALU.mult), reads=["sA", "ST"], writes=["t8"])
        P.op("dve", lambda e, cc=cc: e.tensor_tensor(out=t8[:, 16:24], in0=t8[:, 16:24], in1=sH[:, cc, :], op=ALU.add), reads=["sH", "t8"], writes=["t8"])
        P.op("dve", lambda e: e.tensor_tensor(out=t8[:, 16:24], in0=t8[:, 16:24], in1=L.ST[:], op=ALU.subtract), reads=["t8", "ST"], writes=["t8"])
        P.op("dve", lambda e, cc=cc: e.scalar_tensor_tensor(out=L.ST[:], in0=t8[:, 16:24], scalar=sel[:, cc:cc + 1], in1=L.ST[:], op0=ALU.mult, op1=ALU.add),
             reads=["t8", "ST", ("param", "sel")], writes=["ST"])
    L.mem_kv(1, "xa_w_kv1")
    P.fence()
    L.set_prefetch("F_G")
    steps = []
    for t in range(NTL):
        b_tile_steps(L, t, steps, first=(t == 0))
    L.run_steps(steps)
    return [("yT", t) for t in range(NTL)]


def l1_host_inputs(core, inp, x1T_own, x1hT, NTL, mode, summ=None):
    b, c = core // 4, core % 4
    n = NTL * 512
    t0 = c * TOK
    T_ = lambda v, k: np.ascontiguousarray(np.asarray(v, np.float32).reshape(k, 128).T)
    m = {
        "x1T": np.ascontiguousarray(x1T_own[:, :n]), "x1hT": np.ascontiguousarray(x1hT),
        "pos_rep": np.ascontiguousarray(np.broadcast_to(inp["positions"][b][None, t0:t0 + n], (128, n))).astype(np.int32),
        "ident": np.eye(128, dtype=np.float32).astype(NPBF),
        "od_normT": T_(inp["od_norm"][0], 8),
        "conv_wT": np.ascontiguousarray(inp["od_conv_w"][0].reshape(4, 8, 128).transpose(2, 0, 1).reshape(128, 32)),
        "conv_bT": T_(inp["od_conv_b"][0], 8), "b_rT": T_(inp["od_b_rgate"][0], 8), "b_iT": T_(inp["od_b_igate"][0], 8),
        "lamT": T_(inp["od_lambda"][0], 8),
        "w_rgate": np.ascontiguousarray(inp["od_w_rgate"][0]), "w_igate": np.ascontiguousarray(inp["od_w_igate"][0]),
        "od_w_in": np.ascontiguousarray(inp["od_w_in"][0]),
    }
    if mode == "B":
        sA = np.stack([summ[b * 4 + cc][0] for cc in range(4)], axis=1)
        sH = np.stack([summ[b * 4 + cc][1] for cc in range(4)], axis=1)
        sel = np.zeros((128, 4), np.float32)
        sel[:, :c] = 1.0
        m.update({
            "sumA_all": np.ascontiguousarray(sA), "sumH_all": np.ascontiguousarray(sH), "sel": sel,
            "mem": np.ascontiguousarray(inp["mem"][b]),
            "xa_norm_x1": T_(inp["xa_norm_x"][1], 8), "xa_norm_mem1": T_(inp["xa_norm_mem"][1], 8), "ffn_norm1": T_(inp["ffn_norm"][1], 8),
            "final_normT": T_(inp["final_norm"], 8),
            "od_w_out": np.ascontiguousarray(inp["od_w_out"][0]),
            "xa_w_q1": np.ascontiguousarray(inp["xa_w_q"][1]), "xa_w_kv1": np.ascontiguousarray(inp["xa_w_kv"][1]),
            "xa_w_o1": np.ascontiguousarray(inp["xa_w_o"][1]),
            "ffn_gu1": np.ascontiguousarray(inp["ffn_w_gate_up"][1]), "ffn_d1": np.ascontiguousarray(inp["ffn_w_down"][1]),
        })
    return m


def build_l1_program(NTL, mode):
    nc = bass.Bass("TRN2", target_bir_lowering=False)
    n = NTL * 512
    dt = lambda name, shape, dtype, kind="ExternalInput": nc.dram_tensor(name, shape, dtype, kind=kind).ap()
    io = {
        "x1T": dt("x1T", [D, n], F32), "x1hT": dt("x1hT", [D, 4], F32), "pos_rep": dt("pos_rep", [128, n], I32),
        "ident": dt("ident", [128, 128], BF16), "od_normT": dt("od_normT", [128, 8], F32),
        "conv_wT": dt("conv_wT", [128, 32], F32), "conv_bT": dt("conv_bT", [128, 8], F32),
        "b_rT": dt("b_rT", [128, 8], F32), "b_iT": dt("b_iT", [128, 8], F32), "lamT": dt("lamT", [128, 8], F32),
        "w_rgate": dt("w_rgate", [4, 256, 256], F32), "w_igate": dt("w_igate", [4, 256, 256], F32),
        "od_w_in": dt("od_w_in", [D, 2 * D], F32),
    }
    if mode == "A2":
        io["summA"] = dt("summA", [128, 8], F32, "ExternalOutput")
        io["summH"] = dt("summH", [128, 8], F32, "ExternalOutput")
    else:
        io.update({
            "sumA_all": dt("sumA_all", [128, 4, 8], F32), "sumH_all": dt("sumH_all", [128, 4, 8], F32), "sel": dt("sel", [128, 4], F32),
            "mem": dt("mem", [256, D], F32),
            "xa_norm_x1": dt("xa_norm_x1", [128, 8], F32), "xa_norm_mem1": dt("xa_norm_mem1", [128, 8], F32),
            "ffn_norm1": dt("ffn_norm1", [128, 8], F32), "final_normT": dt("final_normT", [128, 8], F32),
            "od_w_out": dt("od_w_out", [D, D], F32),
            "xa_w_q1": dt("xa_w_q1", [D, D], F32), "xa_w_kv1": dt("xa_w_kv1", [D, 2 * D], F32), "xa_w_o1": dt("xa_w_o1", [D, D], F32),
            "ffn_gu1": dt("ffn_gu1", [D, 2 * D_FF], F32), "ffn_d1": dt("ffn_d1", [D_FF, D], F32),
            "yT": dt("yT", [D, n], F32, "ExternalOutput"),
        })
    with ExitStack() as es:
        P = Prog(nc, es)
        outs = build_a2(nc, P, es, io, NTL) if mode == "A2" else build_b(nc, P, es, io, NTL)
        P.wait_all("sp", [P.res[k][0] for k in outs])
        P.emit()
    return nc


_PROGS = {}


def _prog(key, fn):
    if key not in _PROGS:
        _PROGS[key] = fn()
    return _PROGS[key]


def kernel_multilaunch(inp):
    cores = list(range(NCORE))
    NT, NTL = S // 512, TOK // 512
    nc1 = _prog("p1", lambda: build_p1_program(NT))
    r1 = run_bass_kernel_spmd(nc1, [p1_host_inputs(c, inp, NT) for c in cores], core_ids=cores).results
    yatt = [np.asarray(r["yatt"]) for r in r1]
    nc2 = _prog("a1", lambda: build_a1_program(NTL))
    maps = []
    for core in cores:
        b, c = core // 4, core % 4
        yown = np.concatenate([yatt[b * 4 + hp][:, c * TOK:(c + 1) * TOK] for hp in range(4)], axis=0)
        maps.append(a1_host_inputs(core, inp, yown, NTL))
    r2 = run_bass_kernel_spmd(nc2, maps, core_ids=cores).results
    x1T = [np.asarray(r["x1T"]) for r in r2]
    halos = []
    for core in cores:
        c = core % 4
        halos.append(np.zeros((D, 4), np.float32) if c == 0 else np.ascontiguousarray(x1T[core - 1][:, -4:]))
    nc3 = _prog("a2", lambda: build_l1_program(NTL, "A2"))
    r3 = run_bass_kernel_spmd(nc3, [l1_host_inputs(c, inp, x1T[c], halos[c], NTL, "A2") for c in cores], core_ids=cores).results
    summ = [(np.asarray(r["summA"]), np.asarray(r["summH"])) for r in r3]
    nc4 = _prog("b", lambda: build_l1_program(NTL, "B"))
    r4 = run_bass_kernel_spmd(nc4, [l1_host_inputs(c, inp, x1T[c], halos[c], NTL, "B", summ) for c in cores], core_ids=cores).results
    out = np.empty((B, S, D), np.float32)
    for core in cores:
        b, c = core // 4, core % 4
        out[b, c * TOK:(c + 1) * TOK, :] = np.asarray(r4[core]["yT"]).T
    return out


def kernel(**inputs):
    inp = {k: np.asarray(v) for k, v in inputs.items()}
    if os.environ.get("KMULTI", "0") == "1":
        return kernel_multilaunch(inp)
    return kernel_fused(inp)


RG4 = [[0, 1, 2, 3], [4, 5, 6, 7]]

WEIGHTS = [("ev_w_in", 1024, 928), ("ev_w_out", 1024, 1024), ("xa_w_kv0", 1024, 2048), ("xa_w_q0", 1024, 1024), ("xa_w_o0", 1024, 1024),
           ("ffn_gu0", 1024, 2 * D_FF), ("ffn_d0", D_FF, 1024), ("od_w_in", 1024, 2048), ("od_w_out", 1024, 1024),
           ("xa_w_kv1", 1024, 2048), ("xa_w_q1", 1024, 1024), ("xa_w_o1", 1024, 1024), ("ffn_gu1", 1024, 2 * D_FF), ("ffn_d1", D_FF, 1024)]


class Scratch:
    def __init__(self, nc, P, io):
        self.scr = {}
        self.keys = {}
        self.todo = []
        for name, K, N in WEIGHTS:
            dst = nc.dram_tensor("wbf_" + name, [K, N], BF16).ap()
            src = io[name]
            self.scr[name] = (dst, K, N)
            if name.startswith("ffn_gu"):
                dv = dst.rearrange("k (c two n) -> k c two n", two=2, n=128)
                sv = src.rearrange("k (two c n) -> k two c n", two=2, n=128)
                ks = []
                for r0 in range(0, K, 128):
                    for two in range(2):
                        key = ("wscr", name, (r0 // 128) * 2 + two)
                        ks.append(key)
                        self.todo.append(lambda r0=r0, two=two, dv=dv, sv=sv, key=key: P.dma("pool", dv[r0:r0 + 128, :, two, :], sv[r0:r0 + 128, two, :, :], writes=[key]))
                self.keys[name] = ks
            else:
                ks = []
                for r0 in range(0, K, 128):
                    key = ("wscr", name, r0 // 128)
                    ks.append(key)
                    self.todo.append(lambda r0=r0, dst=dst, src=src, key=key: P.dma("pool", dst[r0:r0 + 128, :], src[r0:r0 + 128, :], writes=[key]))
                self.keys[name] = ks

    def issue(self, n):
        for _ in range(min(n, len(self.todo))):
            self.todo.pop(0)()


def build_fused_program(NT=S // 512, NTL=TOK // 512):
    nc = bass.Bass("TRN2", target_bir_lowering=False)
    n = NTL * 512
    dt = lambda name, shape, dtype, kind="ExternalInput": nc.dram_tensor(name, shape, dtype, kind=kind).ap()
    itn = lambda name, shape, dtype: nc.dram_tensor(name, shape, dtype).ap()
    io = {
        "x": dt("x", [NT * 512, D], F32), "pos": dt("pos", [32, NT * 512], I32), "ident": dt("ident", [128, 128], BF16),
        "masks": dt("masks", [128, 4, 512], BF16), "wq": dt("wq", [256, 192], F32), "wkv": dt("wkv", [128, 256], F32),
        "ev_normT": dt("ev_normT", [128, 8], F32), "q_normT": dt("q_normT", [128, 2], F32), "kv_normT": dt("kv_normT", [128, 1], F32),
        "invf": dt("invf", [128, 1], F32),
        "xT": dt("xT", [D, n], F32), "xhT": dt("xhT", [D, 16], F32), "invc": dt("invc", [128, 4, n], F32), "mem": dt("mem", [256, D], F32),
        "pool_scaleT": dt("pool_scaleT", [128, 4], F32), "xa_norm_x0": dt("xa_norm_x0", [128, 8], F32),
        "xa_norm_mem0": dt("xa_norm_mem0", [128, 8], F32), "ffn_norm0": dt("ffn_norm0", [128, 8], F32),
        "pool_w": dt("pool_w", [4, 128, 128], F32),
        "pos_rep": dt("pos_rep", [128, n], I32), "od_normT": dt("od_normT", [128, 8], F32),
        "conv_wT": dt("conv_wT", [128, 32], F32), "conv_bT": dt("conv_bT", [128, 8], F32),
        "b_rT": dt("b_rT", [128, 8], F32), "b_iT": dt("b_iT", [128, 8], F32), "lamT": dt("lamT", [128, 8], F32),
        "w_rgate": dt("w_rgate", [4, 256, 256], F32), "w_igate": dt("w_igate", [4, 256, 256], F32),
        "xa_norm_x1": dt("xa_norm_x1", [128, 8], F32), "xa_norm_mem1": dt("xa_norm_mem1", [128, 8], F32),
        "ffn_norm1": dt("ffn_norm1", [128, 8], F32), "final_normT": dt("final_normT", [128, 8], F32),
        "sel": dt("sel", [128, 4], F32), "selp": dt("selp", [128, 4], F32), "sel1": dt("sel1", [128, 4], F32),
        "yT": dt("yT", [D, n], F32, "ExternalOutput"),
        "yatt_pieces": [itn("yatt_loc%d" % i, [128, 2048], BF16) for i in range(NT // 4)],
        "yatt_all_pieces": [itn("yatt_all%d" % i, [512, 2048], BF16) for i in range(NT // 4)],
        "x1T": itn("x1T_int", [D, n], F32),
        "halo_loc": itn("halo_loc", [128, 32], F32), "halo_all": itn("halo_all", [512, 32], F32),
        "summ_loc": itn("summ_loc", [128, 16], F32), "summ_all": itn("summ_all", [512, 16], F32),
    }
    for name, K, N in WEIGHTS:
        io[name] = dt(name, [K, N], F32)
    io["w_in"] = io["ev_w_in"]
    with ExitStack() as es0:
        P = Prog(nc, es0)
        SC = Scratch(nc, P, io)
        with ExitStack() as es1:
            def gather_piece(t):
                if t % 4 != 3:
                    return
                pi = t // 4
                P.custom("pool", lambda e, pi=pi: e.collective_compute("AllGather", ALU.bypass, replica_groups=RG4,
                                                                         ins=[io["yatt_pieces"][pi]], outs=[io["yatt_all_pieces"][pi]]),
                         "cc1", reads=[("yatt", h, tt) for h in range(2) for tt in range(4 * pi, 4 * pi + 4)], writes=[("yatt_all", pi)])
            outs1 = build_p1(nc, P, es1, io, NT, tick=lambda t: SC.issue(6), after=gather_piece)
            SC.issue(10 ** 6)
            P.emit()
        with ExitStack() as es2:
            L = Layers(nc, P, es2, io, NTL, "B")
            L.fused = True
            L.chunk = NT * 512 // 4
            L.scr = SC.scr
            L.scr_keys = SC.keys
            L.poolw = L.sb("poolw", [128, 4, 128], BF16)
            P.dma("pool", L.poolw[:], io["pool_w"].rearrange("g c d -> c g d"), writes=["poolw"])
            rglru_setup(L, "F")
            L.invc = L.XB[:].rearrange("p a n -> p (a n)").bitcast(F32)[:, 0:2048].rearrange("p (g n) -> p g n", g=4)
            L.zeros = L.rec
            L.mem_kv(0, "xa_w_kv0")
            P.fence()
            L.set_prefetch("F_G")
            steps = []
            for t in range(NTL):
                a1_tile_steps(L, t, steps, first=(t == 0))
            L.run_steps(steps)
            P.dma("pool", io["halo_loc"].rearrange("p (j m) -> p j m", j=8), L.F_X[:, :, 508:512], reads=["F_X%d" % j for j in range(8)], writes=["halo_loc"])
            P.fence()
            P.custom("pool", lambda e: e.collective_compute("AllGather", ALU.bypass, replica_groups=RG4, ins=[io["halo_loc"]], outs=[io["halo_all"]]),
                     "cc2", reads=["halo_loc"], writes=["halo_all"])
            P.op("pool", lambda e: e.memset(L.zeros[:], 0.0), reads=["rec"], writes=["rec", "zeros"])
            L.pf_none = True
            steps = []
            for t in range(NTL):
                rglru_tile_steps(L, t, steps, "A2", first=(t == 0))
            L.run_steps(steps)
            L.pf_none = False
            P.dma("pool", io["summ_loc"][:, 0:8], L.STA[:], reads=["STA"], writes=["summ_locA"])
            P.dma("pool", io["summ_loc"][:, 8:16], L.ST[:], reads=["ST"], writes=["summ_locH"])
            P.fence()
            P.custom("pool", lambda e: e.collective_compute("AllGather", ALU.bypass, replica_groups=RG4, ins=[io["summ_loc"]], outs=[io["summ_all"]]),
                     "cc3", reads=["summ_locA", "summ_locH"], writes=["summ_all"])
            sAH = L.sg[1][:, 128:192].rearrange("p (c m) -> p c m", c=4)
            sel = L.param("sel", 4)
            P.dma("pool", sAH, io["summ_all"].rearrange("(c p) m -> p c m", p=128), reads=["summ_all"], writes=["sAH"])
            P.op("pool", lambda e: e.memset(L.ST[:], 0.0), reads=["ST"], writes=["ST"])
            t8 = L.small
            for cc in range(4):
                P.op("dve", lambda e, cc=cc: e.tensor_tensor(out=t8[:, 16:24], in0=sAH[:, cc, 0:8], in1=L.ST[:], op=ALU.mult), reads=["sAH", "ST"], writes=["t8"])
                P.op("dve", lambda e, cc=cc: e.tensor_tensor(out=t8[:, 16:24], in0=t8[:, 16:24], in1=sAH[:, cc, 8:16], op=ALU.add), reads=["sAH", "t8"], writes=["t8"])
                P.op("dve", lambda e: e.tensor_tensor(out=t8[:, 16:24], in0=t8[:, 16:24], in1=L.ST[:], op=ALU.subtract), reads=["t8", "ST"], writes=["t8"])
                P.op("dve", lambda e, cc=cc: e.scalar_tensor_tensor(out=L.ST[:], in0=t8[:, 16:24], scalar=sel[:, cc:cc + 1], in1=L.ST[:], op0=ALU.mult, op1=ALU.add),
                     reads=["t8", "ST", ("param", "sel")], writes=["ST"])
            L.mem_kv(1, "xa_w_kv1")
            P.fence()
            L.set_prefetch("F_G")
            steps = []
            for t in range(NTL):
                b_tile_steps(L, t, steps, first=(t == 0))
            L.run_steps(steps)
            P.wait_all("sp", [P.res[("yT", t)][0] for t in range(NTL)])
            P.emit()
    return nc


def fused_host_inputs(core, inp, NT=S // 512, NTL=TOK // 512):
    b, c = core // 4, core % 4
    m = {}
    m.update(p1_host_inputs(core, inp, NT))
    m.pop("w_in")
    a1 = a1_host_inputs(core, inp, np.zeros((512, NTL * 512), NPBF), NTL)
    a1.pop("yattT")
    m.update(a1)
    l1 = l1_host_inputs(core, inp, np.zeros((D, NTL * 512), np.float32), np.zeros((D, 4), np.float32), NTL, "B",
                        [(np.zeros((128, 8), np.float32),) * 2] * 8)
    for k in ("x1T", "x1hT", "sumA_all", "sumH_all"):
        l1.pop(k)
    m.update(l1)
    selp = np.zeros((128, 4), np.float32)
    if c > 0:
        selp[:, c - 1] = 1.0
    sel1 = np.zeros((128, 4), np.float32)
    sel1[:, c] = 1.0
    m["selp"] = selp
    m["sel1"] = sel1
    return m


def kernel_fused(inp):
    cores = list(range(NCORE))
    nc = _prog("fused", build_fused_program)
    r = run_bass_kernel_spmd(nc, [fused_host_inputs(c, inp) for c in cores], core_ids=cores).results
    out = np.empty((B, S, D), np.float32)
    for core in cores:
        b, c = core // 4, core % 4
        out[b, c * TOK:(c + 1) * TOK, :] = np.asarray(r[core]["yT"]).T
    return out
```
